# Optimizing a Trainium2 kernel written in Bass

```python
import jax
import jax.numpy as jnp
from jax import lax
import numpy as np

D_MODEL = 1024
BATCH = 2
SEQ = 8192
DEPTH = 2
DEC_BATCH = 32
DEC_SEQ = 8
PAST_LEN = 16384
PAGE_SIZE = 128

N_MIXERS = 2
A_GROUPS = ((128, 1), (512, 4), (2048, 16))
A_HEAD_DIM = 64
A_HEADS = D_MODEL // A_HEAD_DIM
A_WIDTH = A_HEADS * A_HEAD_DIM
A_IN_WIDTH = len(A_GROUPS) * 3 * A_WIDTH
ROPE_THETA = 10000.0
B_HEAD_K = 128
B_HEAD_V = 128
B_QK_HEADS = D_MODEL // 128
B_V_HEADS = 2 * B_QK_HEADS
B_QK_WIDTH = B_QK_HEADS * B_HEAD_K
B_V_WIDTH = B_V_HEADS * B_HEAD_V
B_CONV_DIM = 2 * B_QK_WIDTH + B_V_WIDTH
B_IN_WIDTH = B_CONV_DIM + B_V_WIDTH + 2 * B_V_HEADS
B_CONV_WIDTH = 4
B_CHUNK = 64
D_FF = ((8 * D_MODEL // 3 + 255) // 256) * 256
FFN_CONV_WIDTH = 3
NORM_EPS = 1e-6

kernel_name = 'dilated_swa_gated_deltanet_convffn_adaln_step'


def rmsnorm(x, g):
    xf = x.astype(jnp.float32)
    return xf * lax.rsqrt(jnp.mean(xf * xf, axis=-1, keepdims=True) + NORM_EPS) * g.astype(jnp.float32)


def l2norm(x):
    xf = x.astype(jnp.float32)
    return xf * lax.rsqrt(jnp.sum(xf * xf, axis=-1, keepdims=True) + NORM_EPS)


def rope(x, pos):
    half = x.shape[-1] // 2
    inv = ROPE_THETA ** (-jnp.arange(half, dtype=jnp.float32) / half)
    ang = pos.astype(jnp.float32)[:, None] * inv[None, :]
    cos = jnp.cos(ang)[None, :, None, :]
    sin = jnp.sin(ang)[None, :, None, :]
    x1, x2 = x[..., :half], x[..., half:]
    return jnp.concatenate([x1 * cos - x2 * sin, x1 * sin + x2 * cos], axis=-1).astype(x.dtype)


def causal_dwconv(x, left, w):
    width, t = w.shape[0], x.shape[1]
    xc = jnp.concatenate([left.astype(x.dtype), x], axis=1)
    y = xc[:, 0:t] * w[0]
    for j in range(1, width):
        y = y + xc[:, j:j + t] * w[j]
    return y, xc[:, xc.shape[1] - (width - 1):]


def dilated_attn_prompt(q, k, v, window, dil):
    bx, s, nh, dh = q.shape
    blk = window // dil
    span = blk * dil
    s_pad = -(-s // span) * span
    sub = s_pad // dil
    nb = sub // blk

    def split(a):
        a = jnp.pad(a, ((0, 0), (0, s_pad - s), (0, 0), (0, 0)))
        a = a.reshape(bx, sub, dil, nh, dh).transpose(0, 2, 1, 3, 4)
        return a.reshape(bx, dil, nb, blk, nh, dh)

    def with_prev(a):
        prev = jnp.pad(a, ((0, 0), (0, 0), (1, 0), (0, 0), (0, 0), (0, 0)))[:, :, :nb]
        return jnp.concatenate([prev, a], axis=3)

    qb = split(q)
    kb = with_prev(split(k))
    vb = with_prev(split(v))
    sc = jnp.einsum('brnqhd,brnkhd->brnhqk', qb, kb, preferred_element_type=jnp.float32)
    qi = jnp.arange(blk)[:, None]
    kj = jnp.arange(2 * blk)[None, :]
    dist = blk + qi - kj
    band = (dist >= 0) & (dist <= blk)
    has_prev = (jnp.arange(nb) > 0)[:, None, None] | (kj >= blk)[None]
    valid = band[None] & has_prev
    sc = jnp.where(valid[None, None, :, None], sc, -jnp.inf)
    lse = jax.nn.logsumexp(sc, axis=-1)
    o = jnp.einsum('brnhqk,brnkhd->brnqhd', jnp.exp(sc - lse[..., None]), vb)
    o = o.reshape(bx, dil, sub, nh, dh).transpose(0, 2, 1, 3, 4).reshape(bx, s_pad, nh, dh)[:, :s]
    lse = jnp.moveaxis(lse, 3, 4).reshape(bx, dil, sub, nh).transpose(0, 2, 1, 3).reshape(bx, s_pad, nh)[:, :s]
    return o, lse


def dilated_attn_sample(q, k, v, k_buf, v_buf, window, dil):
    t, lb = q.shape[1], k_buf.shape[1]
    kc = jnp.concatenate([k_buf.astype(k.dtype), k], axis=1)
    vc = jnp.concatenate([v_buf.astype(v.dtype), v], axis=1)
    idx = lb + jnp.arange(t)[:, None] - dil * jnp.arange(window // dil + 1)[None, :]
    valid = idx >= 0
    idx = jnp.maximum(idx, 0)
    kg = jnp.take(kc, idx, axis=1)
    vg = jnp.take(vc, idx, axis=1)
    sc = jnp.einsum('bthd,btjhd->bhtj', q, kg, preferred_element_type=jnp.float32)
    sc = jnp.where(valid[None, None], sc, -jnp.inf)
    lse = jax.nn.logsumexp(sc, axis=-1)
    o = jnp.einsum('bhtj,btjhd->bthd', jnp.exp(sc - lse[..., None]), vg)
    keep = min(window, lb + t)
    return o, jnp.swapaxes(lse, 1, 2), kc[:, lb + t - keep:], vc[:, lb + t - keep:]


def mixer_a(h, pos, w_in, w_out, bufs):
    bx, t = h.shape[0], h.shape[1]
    proj = (h @ w_in).reshape(bx, t, len(A_GROUPS), 3, A_HEADS, A_HEAD_DIM)
    outs, lses, new_bufs = [], [], []
    for gi, (win, dil) in enumerate(A_GROUPS):
        q = rope(proj[:, :, gi, 0], pos) * (A_HEAD_DIM ** -0.5)
        k = rope(proj[:, :, gi, 1], pos)
        v = proj[:, :, gi, 2]
        if bufs is None:
            o, lse = dilated_attn_prompt(q, k, v, win, dil)
            keep = min(win, t)
            k_keep, v_keep = k[:, t - keep:], v[:, t - keep:]
        else:
            o, lse, k_keep, v_keep = dilated_attn_sample(q, k, v, bufs[2 * gi], bufs[2 * gi + 1], win, dil)
        outs.append(o)
        lses.append(lse)
        new_bufs += [k_keep, v_keep]
    wts = jax.nn.softmax(jnp.stack(lses), axis=0)
    o = jnp.sum(wts[..., None] * jnp.stack(outs), axis=0)
    return o.reshape(bx, t, A_WIDTH) @ w_out, new_bufs


def gated_delta_chunked(q, k, v, g, beta, s0):
    f32 = jnp.float32
    bx, t, nh, dk = q.shape
    dv = v.shape[-1]
    c = B_CHUNK
    n = -(-t // c)
    pad = n * c - t

    def chunks(a):
        a = jnp.pad(a.astype(f32), [(0, 0), (0, pad)] + [(0, 0)] * (a.ndim - 2))
        a = a.reshape((bx, n, c) + a.shape[2:])
        return jnp.moveaxis(a, 3, 2)

    qc, kc, vc, gc, bc = chunks(q), chunks(k), chunks(v), chunks(g), chunks(beta)
    cg = jnp.cumsum(gc, axis=-1)
    incl = jnp.tril(jnp.ones((c, c), bool))
    strict = jnp.tril(jnp.ones((c, c), bool), -1)
    decay = jnp.where(incl, jnp.exp(jnp.where(incl, cg[..., :, None] - cg[..., None, :], 0.0)), 0.0)
    a_mat = jnp.where(strict, bc[..., :, None] * jnp.einsum('bnhid,bnhjd->bnhij', kc, kc) * decay, 0.0)
    rhs = jnp.concatenate([bc[..., None] * vc, (bc * jnp.exp(cg))[..., None] * kc], axis=-1)
    sol = lax.linalg.triangular_solve(a_mat + jnp.eye(c, dtype=f32), rhs,
                                      left_side=True, lower=True, unit_diagonal=True)
    u0, wk = sol[..., :dv], sol[..., dv:]
    p_mat = jnp.einsum('bnhid,bnhjd->bnhij', qc, kc) * decay
    qg = qc * jnp.exp(cg)[..., None]
    g_last = cg[..., -1]
    kdec = kc * jnp.exp(g_last[..., None] - cg)[..., None]

    def step(s, xs):
        u0_i, wk_i, qg_i, p_i, kdec_i, gl_i = xs
        u = u0_i - wk_i @ s
        o = qg_i @ s + p_i @ u
        s = jnp.exp(gl_i)[..., None, None] * s + jnp.swapaxes(kdec_i, -1, -2) @ u
        return s, o

    xs = tuple(jnp.moveaxis(a, 1, 0) for a in (u0, wk, qg, p_mat, kdec, g_last))
    s_fin, o = lax.scan(step, s0.astype(f32), xs)
    o = jnp.moveaxis(o, 0, 1)
    o = jnp.moveaxis(o, 3, 2).reshape(bx, n * c, nh, dv)[:, :t]
    return o, s_fin


def mixer_b(h, w_in, conv_w, a_log, dt_bias, norm_g, w_out, ssm0, conv0):
    bx, t = h.shape[0], h.shape[1]
    proj = h @ w_in
    qkv, conv_new = causal_dwconv(proj[..., :B_CONV_DIM], conv0, conv_w)
    qkv = jax.nn.silu(qkv)
    z = proj[..., B_CONV_DIM:B_CONV_DIM + B_V_WIDTH].reshape(bx, t, B_V_HEADS, B_HEAD_V)
    b_raw = proj[..., B_CONV_DIM + B_V_WIDTH:B_CONV_DIM + B_V_WIDTH + B_V_HEADS]
    a_raw = proj[..., B_CONV_DIM + B_V_WIDTH + B_V_HEADS:]
    rep = B_V_HEADS // B_QK_HEADS
    q = jnp.repeat(l2norm(qkv[..., :B_QK_WIDTH].reshape(bx, t, B_QK_HEADS, B_HEAD_K)), rep, axis=2) * (B_HEAD_K ** -0.5)
    k = jnp.repeat(l2norm(qkv[..., B_QK_WIDTH:2 * B_QK_WIDTH].reshape(bx, t, B_QK_HEADS, B_HEAD_K)), rep, axis=2)
    v = qkv[..., 2 * B_QK_WIDTH:].reshape(bx, t, B_V_HEADS, B_HEAD_V)
    beta = jax.nn.sigmoid(b_raw.astype(jnp.float32))
    g = -jnp.exp(a_log.astype(jnp.float32)) * jax.nn.softplus(a_raw.astype(jnp.float32) + dt_bias.astype(jnp.float32))
    o, ssm_new = gated_delta_chunked(q, k, v, g, beta, ssm0)
    o = rmsnorm(o, norm_g) * jax.nn.silu(z.astype(jnp.float32))
    return o.reshape(bx, t, B_V_WIDTH) @ w_out, ssm_new, conv_new


def conv_ffn(h, w_up, conv_w, conv_b, w_down, buf0):
    up = h @ w_up
    gate, buf_new = causal_dwconv(up[..., :D_FF], buf0, conv_w)
    return (jax.nn.silu(gate + conv_b) * up[..., D_FF:]) @ w_down, buf_new


def trunk(x, c, pos, a_caches, b_ssm, b_conv, f_conv, weights):
    (norm_mix_g, norm_ffn_g, norm_final_g, w_mod, b_mod, a_w_in, a_w_out,
     b_w_in, b_conv_w, b_a_log, b_dt_bias, b_norm_g, b_w_out,
     ffn_w_up, ffn_conv_w, ffn_conv_b, ffn_w_down) = weights
    bx = x.shape[0]
    new_a = [[] for _ in range(2 * len(A_GROUPS))]
    new_ssm, new_bconv, new_fconv = [], [], []
    cs = jax.nn.silu(c)
    for i in range(DEPTH):
        mod = cs @ w_mod[i] + b_mod[i]
        sh1, sc1, g1, sh2, sc2, g2 = jnp.split(mod[:, None, :], 6, axis=-1)
        h = rmsnorm(x, norm_mix_g[i]) * (1.0 + sc1) + sh1
        j = i // N_MIXERS
        if i % N_MIXERS == 0:
            bufs = None if a_caches is None else [a[j] for a in a_caches]
            y, bufs_new = mixer_a(h, pos, a_w_in[j], a_w_out[j], bufs)
            for m, buf in enumerate(bufs_new):
                new_a[m].append(buf)
        else:
            s0 = jnp.zeros((bx, B_V_HEADS, B_HEAD_K, B_HEAD_V), jnp.float32) if b_ssm is None else b_ssm[j]
            cv0 = jnp.zeros((bx, B_CONV_WIDTH - 1, B_CONV_DIM), jnp.float32) if b_conv is None else b_conv[j]
            y, s1, cv1 = mixer_b(h, b_w_in[j], b_conv_w[j], b_a_log[j], b_dt_bias[j], b_norm_g[j], b_w_out[j], s0, cv0)
            new_ssm.append(s1)
            new_bconv.append(cv1)
        x = x + g1 * y
        h = rmsnorm(x, norm_ffn_g[i]) * (1.0 + sc2) + sh2
        f0 = jnp.zeros((bx, FFN_CONV_WIDTH - 1, D_FF), jnp.float32) if f_conv is None else f_conv[i]
        y, f1 = conv_ffn(h, ffn_w_up[i], ffn_conv_w[i], ffn_conv_b[i], ffn_w_down[i], f0)
        new_fconv.append(f1)
        x = x + g2 * y
    x = rmsnorm(x, norm_final_g)
    return x, [jnp.stack(a) for a in new_a], jnp.stack(new_ssm), jnp.stack(new_bconv), jnp.stack(new_fconv)


def _layers_of(m):
    return len(range(m, DEPTH, N_MIXERS))


def setup_inputs(seed: int = 0) -> dict:
    key = jax.random.key(seed)
    ks = jax.random.split(key, 32)
    f32 = jnp.float32
    D = D_MODEL

    def nrm(i, shape, scale=1.0):
        return scale * jax.random.normal(ks[i], shape, f32)

    n_a, n_b = _layers_of(0), _layers_of(1)
    inp = {}
    inp['x_prompt'] = nrm(0, (BATCH, SEQ, D))
    inp['x_sample'] = nrm(1, (DEC_BATCH, DEC_SEQ, D))
    inp['c_prompt'] = nrm(2, (BATCH, D))
    inp['c_sample'] = nrm(3, (DEC_BATCH, D))
    for gi, (w, _) in enumerate(A_GROUPS):
        lb = min(w, PAST_LEN)
        inp[f'cache_a_k_w{w}'] = nrm(4 + 2 * gi, (n_a, DEC_BATCH, lb, A_HEADS, A_HEAD_DIM))
        inp[f'cache_a_v_w{w}'] = nrm(5 + 2 * gi, (n_a, DEC_BATCH, lb, A_HEADS, A_HEAD_DIM))
    inp['state_b_ssm'] = nrm(10, (n_b, DEC_BATCH, B_V_HEADS, B_HEAD_K, B_HEAD_V), 0.1)
    inp['state_b_conv'] = nrm(11, (n_b, DEC_BATCH, B_CONV_WIDTH - 1, B_CONV_DIM))
    inp['state_ffn_conv'] = nrm(12, (DEPTH, DEC_BATCH, FFN_CONV_WIDTH - 1, D_FF))
    inp['norm_mix_g'] = 1.0 + nrm(13, (DEPTH, D), 0.05)
    inp['norm_ffn_g'] = 1.0 + nrm(14, (DEPTH, D), 0.05)
    inp['norm_final_g'] = 1.0 + nrm(15, (D,), 0.05)
    inp['w_mod'] = nrm(16, (DEPTH, D, 6 * D), 0.5 * D ** -0.5)
    inp['b_mod'] = nrm(17, (DEPTH, 6 * D), 0.02)
    inp['a_w_in'] = nrm(18, (n_a, D, A_IN_WIDTH), D ** -0.5)
    inp['a_w_out'] = nrm(19, (n_a, A_WIDTH, D), A_WIDTH ** -0.5)
    inp['b_w_in'] = nrm(20, (n_b, D, B_IN_WIDTH), D ** -0.5)
    inp['b_conv_w'] = nrm(21, (n_b, B_CONV_WIDTH, B_CONV_DIM), B_CONV_WIDTH ** -0.5)
    inp['b_a_log'] = jnp.log(jax.random.uniform(ks[22], (n_b, B_V_HEADS), f32, 1.0, 16.0))
    dt = jnp.exp(jax.random.uniform(ks[23], (n_b, B_V_HEADS), f32, np.log(1e-3), np.log(1e-1)))
    inp['b_dt_bias'] = dt + jnp.log(-jnp.expm1(-dt))
    inp['b_norm_g'] = 1.0 + nrm(24, (n_b, B_HEAD_V), 0.05)
    inp['b_w_out'] = nrm(25, (n_b, B_V_WIDTH, D), B_V_WIDTH ** -0.5)
    inp['ffn_w_up'] = nrm(26, (DEPTH, D, 2 * D_FF), D ** -0.5)
    inp['ffn_conv_w'] = nrm(27, (DEPTH, FFN_CONV_WIDTH, D_FF), FFN_CONV_WIDTH ** -0.5)
    inp['ffn_conv_b'] = nrm(28, (DEPTH, D_FF), 0.01)
    inp['ffn_w_down'] = nrm(29, (DEPTH, D_FF, D), D_FF ** -0.5)
    return inp


def reference(x_prompt, x_sample, c_prompt, c_sample,
              cache_a_k_w128, cache_a_v_w128, cache_a_k_w512, cache_a_v_w512,
              cache_a_k_w2048, cache_a_v_w2048, state_b_ssm, state_b_conv, state_ffn_conv,
              norm_mix_g, norm_ffn_g, norm_final_g, w_mod, b_mod, a_w_in, a_w_out,
              b_w_in, b_conv_w, b_a_log, b_dt_bias, b_norm_g, b_w_out,
              ffn_w_up, ffn_conv_w, ffn_conv_b, ffn_w_down):
    weights = (norm_mix_g, norm_ffn_g, norm_final_g, w_mod, b_mod, a_w_in, a_w_out,
               b_w_in, b_conv_w, b_a_log, b_dt_bias, b_norm_g, b_w_out,
               ffn_w_up, ffn_conv_w, ffn_conv_b, ffn_w_down)
    pos_p = jnp.arange(x_prompt.shape[1], dtype=jnp.int32)
    pos_s = PAST_LEN + jnp.arange(x_sample.shape[1], dtype=jnp.int32)
    y_p, a_p, ssm_p, bconv_p, fconv_p = trunk(x_prompt, c_prompt, pos_p, None, None, None, None, weights)
    a_caches = [cache_a_k_w128, cache_a_v_w128, cache_a_k_w512, cache_a_v_w512, cache_a_k_w2048, cache_a_v_w2048]
    y_s, a_s, ssm_s, bconv_s, fconv_s = trunk(x_sample, c_sample, pos_s, a_caches, state_b_ssm, state_b_conv,
                                              state_ffn_conv, weights)
    return (y_p.astype(x_prompt.dtype), y_s.astype(x_sample.dtype),
            a_p[0], a_s[0], a_p[1], a_s[1], a_p[2], a_s[2], a_p[3], a_s[3], a_p[4], a_s[4], a_p[5], a_s[5],
            ssm_p, ssm_s, bconv_p, bconv_s, fconv_p, fconv_s)
```

```python
from contextlib import ExitStack

import numpy as np
import concourse.bass as bass
import concourse.mybir as mybir
from concourse.bass_utils import run_bass_kernel_spmd

F32 = mybir.dt.float32
BF16 = mybir.dt.bfloat16
ALU = mybir.AluOpType
AF = mybir.ActivationFunctionType
AX = mybir.AxisListType

NCORES = 8
D = 1024
SEQ = 8192
NT = SEQ // 128
DEC_B = 32
DEC_T = 8
SPC = DEC_B // NCORES
PAST = 16384
A_GROUPS = ((128, 1), (512, 4), (2048, 16))
NH = 16
DH = 64
D_FF = 2816
B_CONV_DIM = 4096
EPS = 1e-6
THETA = 10000.0

HM = 4
CC_GROUPS = [[0, 1, 2, 3], [4, 5, 6, 7]]
SEM_BLOCK = 8000
N_DMA_SEMS = 20
ARENA_F32 = 48 * 1024
DEBUG = False


class Buf:
    def __init__(self, ap, name=""):
        self.ap = ap
        self.name = name
        self.last_write = None
        self.reads = []


class K:
    ENGS = ("tensor", "vector", "scalar", "gpsimd", "sync")

    def __init__(self, nc, stack):
        self.nc = nc
        self.stack = stack
        self.ops = []
        self.arena = stack.enter_context(nc.sbuf_tensor("arena", [128, ARENA_F32], F32))
        self.psum_t = stack.enter_context(nc.psum_tensor("psum", [128, 8, 512], F32))
        self.banks = [Buf(self.psum_t[:, i, :], f"bank{i}") for i in range(8)]
        self.top = 0
        self.persist = 0

    def alloc(self, name, cols, dt=F32):
        words = cols if dt == F32 else (cols + 1) // 2
        a = self.top
        self.top += words
        assert self.top <= ARENA_F32, (name, self.top)
        ap = self.arena[:, a:a + words]
        if dt != F32:
            ap = ap.bitcast(dt)[:, 0:cols]
        return Buf(ap, name)

    def keep(self):
        self.persist = self.top

    def phase(self):
        self.ops.append(dict(barrier=True))
        self.top = self.persist

    def dram(self, name, shape, dt=F32, kind="Internal", **kw):
        return Buf(self.nc.dram_tensor(name, list(shape), dt, kind=kind, **kw).ap(), name)

    def op(self, eng, fn, reads=(), writes=(), dma=False):
        self.ops.append(dict(eng=eng, fn=fn, reads=list(reads), writes=list(writes), dma=dma))

    def dma(self, out_b, out_ap, in_b, in_ap, eng="sync"):
        self.op(eng, lambda e: e.dma_start(out=out_ap, in_=in_ap), reads=[in_b], writes=[out_b], dma=True)

    def mm(self, out_b, out_ap, lb, lap, rb, rap, start=True, stop=True):
        self.op("tensor", lambda e: e.matmul(out_ap, lhsT=lap, rhs=rap, start=start, stop=stop),
                reads=[lb, rb], writes=[out_b])

    def tr(self, out_b, out_ap, in_b, in_ap, ident_b, ident_ap):
        self.op("tensor", lambda e: e.transpose(out=out_ap, in_=in_ap, identity=ident_ap),
                reads=[in_b, ident_b], writes=[out_b])

    def tt(self, eng, out_b, out_ap, a_b, a_ap, b_b, b_ap, op):
        self.op(eng, lambda e: e.tensor_tensor(out=out_ap, in0=a_ap, in1=b_ap, op=op),
                reads=[a_b, b_b], writes=[out_b])

    def ts(self, eng, out_b, out_ap, a_b, a_ap, s1, op0, s2=None, op1=None, sbufs=()):
        kw = dict(out=out_ap, in0=a_ap, scalar1=s1, scalar2=s2, op0=op0)
        if op1 is not None:
            kw["op1"] = op1
        self.op(eng, lambda e: e.tensor_scalar(**kw), reads=[a_b, *sbufs], writes=[out_b])

    def stt(self, out_b, out_ap, a_b, a_ap, scalar, op0, b_b, b_ap, op1, sbufs=()):
        self.op("vector", lambda e: e.scalar_tensor_tensor(out=out_ap, in0=a_ap, scalar=scalar, op0=op0,
                                                           in1=b_ap, op1=op1),
                reads=[a_b, b_b, *sbufs], writes=[out_b])

    def act(self, out_b, out_ap, in_b, in_ap, func, scale=1.0, bias=None, sbufs=()):
        kw = dict(out=out_ap, in_=in_ap, func=func, scale=scale)
        if bias is not None:
            kw["bias"] = bias
        self.op("scalar", lambda e: e.activation(**kw), reads=[in_b, *sbufs], writes=[out_b])

    def copy(self, eng, out_b, out_ap, in_b, in_ap):
        if eng == "scalar":
            self.act(out_b, out_ap, in_b, in_ap, AF.Copy)
        else:
            self.op(eng, lambda e: e.tensor_copy(out=out_ap, in_=in_ap), reads=[in_b], writes=[out_b])

    def allgather(self, out_b, out_ap, in_b, in_ap, groups):
        self.op("gpsimd", lambda e: e.collective_compute("AllGather", ALU.bypass, replica_groups=groups,
                                                         ins=[in_ap], outs=[out_ap]),
                reads=[in_b], writes=[out_b], dma=True)
        self.ops[-1]["cc"] = True

    def memset(self, eng, out_b, out_ap, val):
        self.op(eng, lambda e: e.memset(out_ap, val), writes=[out_b])

    def emit(self):
        nc = self.nc
        ops = []
        pending = {}
        last_eng = {}
        last_dma = {}
        dma_rr = {e: 0 for e in self.ENGS}
        for o in self.ops:
            if o.get("barrier"):
                b = set(last_eng.values()) | set(last_dma.values())
                for e in self.ENGS:
                    pending[e] = set(pending.get(e, set())) | b
                continue
            i = len(ops)
            ops.append(o)
            deps = set()
            for bf in o["reads"]:
                if bf.last_write is not None:
                    deps.add(bf.last_write)
            for bf in o["writes"]:
                if bf.last_write is not None:
                    deps.add(bf.last_write)
                deps.update(bf.reads)
            deps |= pending.pop(o["eng"], set())
            deps.discard(i)
            o["deps"] = deps
            for bf in o["reads"]:
                bf.reads.append(i)
            for bf in o["writes"]:
                bf.last_write = i
                bf.reads = []
            e = o["eng"]
            if o.get("cc"):
                o["slot"] = f"cc{i}"
                last_dma[(e, o["slot"])] = i
            elif o["dma"]:
                slot = dma_rr[e] % N_DMA_SEMS
                dma_rr[e] += 1
                o["slot"] = slot
                last_dma[(e, slot)] = i
            else:
                last_eng[e] = i
        n_eng = {e: 0 for e in self.ENGS}
        sems = {}
        dma_count = {}
        prev_on_sem = {}
        for o in ops:
            e = o["eng"]
            if o.get("cc"):
                o["sig"] = (f"d_{e}_{o['slot']}", 1, None)
                o["prev_same_sem"] = None
            elif o["dma"]:
                s = f"d_{e}_{o['slot']}"
                dma_count[s] = dma_count.get(s, 0) + 1
                o["sig"] = (s, 16 * dma_count[s], 16)
                o["prev_same_sem"] = prev_on_sem.get(s)
                prev_on_sem[s] = o
            else:
                kk = n_eng[e]
                n_eng[e] += 1
                o["sig"] = (f"c_{e}_{kk // SEM_BLOCK}", kk % SEM_BLOCK + 1, 1)
                o["prev_same_sem"] = None
            if o["sig"][0] not in sems:
                sems[o["sig"][0]] = self.stack.enter_context(nc.semaphore(o["sig"][0]))
        by_eng = {e: [o for o in ops if o["eng"] == e] for e in self.ENGS}
        final_waits = {}
        for o in ops:
            if o["dma"]:
                s, v, _ = o["sig"]
                final_waits[s] = max(final_waits.get(s, 0), v)
        print(f"[kernel] ops: " + ", ".join(f"{e}={len(by_eng[e])}" for e in self.ENGS) + f" sems={len(sems)}")

        def emit_engine(ename, eh):
            w = {}
            for o in by_eng[ename]:
                need = {}
                for d in o["deps"]:
                    od = ops[d]
                    if od["eng"] == ename and not od["dma"] and ename == "tensor":
                        continue
                    s, v, _ = od["sig"]
                    need[s] = max(need.get(s, 0), v)
                p = o["prev_same_sem"]
                if p is not None:
                    s, v, _ = p["sig"]
                    need[s] = max(need.get(s, 0), v)
                for s, v in need.items():
                    if w.get(s, 0) < v:
                        eh.wait_ge(sems[s], v)
                        w[s] = v
                ins = o["fn"](eh)
                s, v, inc = o["sig"]
                if inc is None:
                    ins.then_inc(sems[s])
                else:
                    ins.then_inc(sems[s], inc)
            if ename == "sync":
                for s, v in final_waits.items():
                    if w.get(s, 0) < v:
                        eh.wait_ge(sems[s], v)
                for en in self.ENGS:
                    if en == "sync" or not by_eng[en]:
                        continue
                    last = [o for o in by_eng[en] if not o["dma"]]
                    if last:
                        s, v, _ = last[-1]["sig"]
                        eh.wait_ge(sems[s], v)

        with nc.Block() as block:
            @block.tensor
            def _(e):
                emit_engine("tensor", e)

            @block.vector
            def _(e):
                emit_engine("vector", e)

            @block.scalar
            def _(e):
                emit_engine("scalar", e)

            @block.gpsimd
            def _(e):
                emit_engine("gpsimd", e)

            @block.sync
            def _(e):
                emit_engine("sync", e)


def v3(ap, a):
    return ap.rearrange("p (a b) -> p a b", a=a)


def group_blocks(d):
    return 16 // d


def build_program(stage):
    nc = bass.Bass("TRN2", target_bir_lowering=False)
    stack = ExitStack()
    k = K(nc, stack)
    B = k.banks

    def din(name, shape, dt=F32):
        return Buf(nc.dram_tensor(name, list(shape), dt, kind="ExternalInput").ap(), name)

    def dout(name, shape, dt=F32):
        return Buf(nc.dram_tensor(name, list(shape), dt, kind="ExternalOutput").ap(), name)

    xp = din("xp", [SEQ, D])
    xs = din("xs", [SPC, DEC_T, D])
    cT = din("cT", [D, 1 + SPC])
    w_mod = din("w_mod", [2, D, 6 * D])
    b_mod = din("b_mod", [2, 6 * D])
    norm_mix_g = din("norm_mix_g", [2, D])
    norm_ffn_g = din("norm_ffn_g", [2, D])
    a_w_in = din("a_w_in", [D, 9 * D])
    a_w_in_m = din("a_w_in_m", [D, 9 * HM * DH])
    identf_in = din("identf", [128, 128])
    rope_p = din("rope_p", [3, NT, 128, 64])
    rope_s = din("rope_s", [DEC_T, 64])
    cache = {}
    new_s = {}
    new_p = {}
    for (w, _d) in A_GROUPS:
        for kv in ("k", "v"):
            cache[(kv, w)] = din(f"cache_{kv}_w{w}", [SPC, w, D])
            new_s[(kv, w)] = dout(f"new_{kv}_w{w}_s", [SPC, w, D])
            new_p[(kv, w)] = dout(f"new_{kv}_w{w}_p", [w, HM * DH])

    mask4_in = din("mask4", [128, 512])
    E_in = din("Emat", [65, 64])
    msk_s_in = din("msk_s", [3, 8, 128, 8])
    msk_n_in = din("msk_n", [3, 8, 8])
    a_w_out = din("a_w_out", [D, D])
    b_w_in = din("b_w_in", [D, 6176])
    b_w_in_m = din("b_w_in_m", [D, 1032])
    b_cw_m = din("b_cw_m", [128, 8, 4])
    b_dt_bias_m = din("b_dt_bias_m", [1, 4])
    b_a_log_m = din("b_a_log_m", [1, 4])
    b_w_out = din("b_w_out", [2048, D])
    b_cw = din("b_cw", [128, 32, 4])
    bconvT = din("bconvT", [128, 32, SPC, 3])
    b_dt_bias = din("b_dt_bias", [1, 16])
    b_a_log = din("b_a_log", [1, 16])
    b_norm_g = din("b_norm_g", [1, 128])
    norm_final_g = din("norm_final_g", [1, D])
    Umat_in = din("Umat", [128, 128])
    Lstrict_in = din("Lstrict", [128, 128])
    sel_in = din("selm", [16, 16 * 128])
    state_b_ssm = din("state_b_ssm", [SPC, 16, 128, 128])
    bconv_p_out = dout("bconv_p", [128, 8 * 3])
    bconv_s_out = dout("bconv_s", [128, 32 * SPC * 3])
    ssm_p_out = dout("ssm_p", [4, 128, 128])
    ssm_s_out = dout("ssm_s", [SPC, 16, 128, 128])
    y_p_out = dout("y_p", [SEQ, D])
    y_s_out = dout("y_s", [SPC, DEC_T, D])
    ffn_w_up = din("ffn_w_up", [2, D, 2 * D_FF])
    ffn_w_down = din("ffn_w_down", [2, D_FF, D])
    ffn_cw = din("ffn_cw", [2, 128, D_FF // 128, 4])
    fconvT = din("fconvT", [2, 128, D_FF // 128, SPC, 2])
    fconv_p_out = dout("fconv_p", [2, 128, (D_FF // 128) * 2])
    fconv_s_out = dout("fconv_s", [2, 128, (D_FF // 128) * SPC * 2])

    dbg_kind = "ExternalOutput" if DEBUG else "Internal"
    o_scr = k.dram("o_scr", [3, HM, 65, SEQ])
    oT_mine = [k.dram(f"oT_mine{i}", [HM * 64, 2048], BF16) for i in range(4)]
    oT_scr = [k.dram(f"oT_scr{i}", [NH * 64, 2048], BF16, addr_space="Local") for i in range(4)]
    qTs_scr = k.dram("qTs_scr", [3, 64, NH, SPC * DEC_T], BF16)
    kTs_scr = k.dram("kTs_scr", [3, 64, NH, SPC * DEC_T], BF16)
    Vs_scr = k.dram("Vs_scr", [3, SPC, DEC_T, NH * 65], BF16)
    oTs_scr = k.dram("oTs_scr", [SPC, 64, NH * DEC_T], BF16)
    x1p = k.dram("x1p", [SEQ, D], kind=dbg_kind)
    x1s = k.dram("x1s", [SPC, DEC_T, D], kind=dbg_kind)
    x2p = k.dram("x2p", [SEQ, D], kind=dbg_kind)
    x2s = k.dram("x2s", [SPC, DEC_T, D], kind=dbg_kind)
    x3p = k.dram("x3p", [SEQ, D], kind=dbg_kind)
    x3s = k.dram("x3s", [SPC, DEC_T, D], kind=dbg_kind)
    x4p = k.dram("x4p", [SEQ, D], kind=dbg_kind)
    x4s = k.dram("x4s", [SPC, DEC_T, D], kind=dbg_kind)
    qTb = k.dram("qTb", [2, 128, SEQ], BF16)
    kTb = k.dram("kTb", [2, 128, SEQ], BF16)
    qTbs = k.dram("qTbs", [8, 128, SPC * DEC_T], BF16)
    kTbs = k.dram("kTbs", [8, 128, SPC * DEC_T], BF16)
    ktm = k.dram("ktm", [2, SEQ, 128], BF16)
    vtm = k.dram("vtm", [4, SEQ, 128], BF16)
    ktms = k.dram("ktms", [8, SPC, DEC_T, 128], BF16)
    vtms = k.dram("vtms", [16, SPC, DEC_T, 128], BF16)
    bg_scr = k.dram("bg_scr", [NT, 128, 8])
    bgs_scr = k.dram("bgs_scr", [SPC, DEC_T, 32])
    otm_m = [k.dram(f"otm_m{i}", [512, 512]) for i in range(SEQ // 512)]
    otm_g = [k.dram(f"otm_g{i}", [4 * 512, 512], addr_space="Local") for i in range(SEQ // 512)]
    otms_scr = k.dram("otms_scr", [SPC, DEC_T, 2048])
    mod_scr = k.dram("mod_scr", [2, 1 + SPC, 6 * D])
    hTp = k.dram("hTp", [128, 8, SEQ], BF16)
    hTs = k.dram("hTs", [128, 8, SPC * DEC_T], BF16)
    qT_scr = k.dram("qT_scr", [3, 64, HM, SEQ], BF16)
    kT_scr = k.dram("kT_scr", [3, 64, HM, SEQ], BF16)
    V_scr = k.dram("V_scr", [3, NT, 128, HM * 65], BF16)

    for (w, _d) in A_GROUPS:
        for kv in ("k", "v"):
            for s in range(SPC):
                k.dma(new_s[(kv, w)], new_s[(kv, w)].ap[s, 0:w - DEC_T, :],
                      cache[(kv, w)], cache[(kv, w)].ap[s, DEC_T:w, :], eng="sync")

    identf = k.alloc("identf", 128)
    identb = k.alloc("identb", 128, BF16)
    epsb = k.alloc("epsb", 1)
    k.dma(identf, identf.ap, identf_in, identf_in.ap)
    k.copy("vector", identb, identb.ap, identf, identf.ap)
    k.memset("vector", epsb, epsb.ap, EPS)
    csb = k.alloc("csb", 8 * 8, BF16)
    k.keep()

    def phase_mod():
        k.phase()
        cs = k.alloc("cs", 8 * 5)
        k.dma(cs, v3(cs.ap, 8), cT, cT.ap.rearrange("(k p) n -> p k n", p=128))
        k.act(cs, cs.ap, cs, cs.ap, AF.Silu)
        k.memset("vector", csb, csb.ap, 0.0)
        k.copy("vector", csb, v3(csb.ap, 8)[:, :, 0:5], cs, v3(cs.ap, 8))
        modt = k.alloc("modt", 6 * D)
        gb = k.alloc("gb", D)
        wm = [k.alloc(f"wm{i}", 8 * 512, BF16) for i in range(2)]
        n = 1 + SPC
        it = 0
        for layer in range(2):
            k.dma(modt, modt.ap[0:n, :], b_mod, b_mod.ap[layer:layer + 1, :].partition_broadcast(n))
            for cb in range(12):
                wb = wm[it % 2]
                bank = B[it % 2]
                it += 1
                k.dma(wb, v3(wb.ap, 8),
                      w_mod, w_mod.ap[layer].rearrange("(k p) n -> p k n", p=128)[:, :, cb * 512:(cb + 1) * 512],
                      eng="gpsimd")
                for kc in range(8):
                    k.mm(bank, bank.ap[0:n, :], csb, v3(csb.ap, 8)[:, kc, 0:n], wb, v3(wb.ap, 8)[:, kc, :],
                         start=(kc == 0), stop=(kc == 7))
                sl = slice(cb * 512, (cb + 1) * 512)
                k.tt("vector", modt, modt.ap[0:n, sl], bank, bank.ap[0:n, :], modt, modt.ap[0:n, sl], ALU.add)
            for (gsrc, off) in ((norm_mix_g, 1 * D), (norm_ffn_g, 4 * D)):
                k.dma(gb, gb.ap[0:n, :], gsrc, gsrc.ap[layer:layer + 1, :].partition_broadcast(n))
                k.stt(modt, modt.ap[0:n, off:off + D], modt, modt.ap[0:n, off:off + D], 1.0, ALU.add,
                      gb, gb.ap[0:n, :], ALU.mult)
            k.dma(mod_scr, mod_scr.ap[layer], modt, modt.ap[0:n, :])

    def phase_norm(layer, x_p, x_s, a_off, b_off):
        k.phase()
        A_p = k.alloc("A_p", D)
        B_p = k.alloc("B_p", D)
        A_s = [k.alloc(f"A_s{i}", D) for i in range(4)]
        B_s = [k.alloc(f"B_s{i}", D) for i in range(4)]
        xt = [k.alloc(f"xt{i}", D) for i in range(4)]
        sq = k.alloc("sq", D)
        ssq = [k.alloc(f"ssq{i}", 1) for i in range(4)]
        std = [k.alloc(f"std{i}", 1) for i in range(4)]
        rstd = [k.alloc(f"rstd{i}", 1) for i in range(4)]
        tmp = [k.alloc(f"tmp{i}", D) for i in range(4)]
        hb = [k.alloc(f"hb{i}", D, BF16) for i in range(4)]
        hTt = [k.alloc(f"hTt{i}", D, BF16) for i in range(4)]
        mrow = mod_scr.ap[layer]
        k.dma(A_p, A_p.ap, mod_scr, mrow[0:1, a_off:a_off + D].partition_broadcast(128))
        k.dma(B_p, B_p.ap, mod_scr, mrow[0:1, b_off:b_off + D].partition_broadcast(128))
        tiles = [(128, i, None) for i in range(NT)] + [(DEC_T, None, s) for s in range(SPC)]
        for it, (rows, ti, si) in enumerate(tiles):
            j = it % 4
            if si is None:
                src_b, src_ap = x_p, x_p.ap[ti * 128:(ti + 1) * 128, :]
                Ab, Bb = A_p, B_p
            else:
                src_b, src_ap = x_s, x_s.ap[si]
                Ab, Bb = A_s[si % 2], B_s[si % 2]
                k.dma(Ab, Ab.ap[0:rows, :], mod_scr, mrow[1 + si:2 + si, a_off:a_off + D].partition_broadcast(rows))
                k.dma(Bb, Bb.ap[0:rows, :], mod_scr, mrow[1 + si:2 + si, b_off:b_off + D].partition_broadcast(rows))
            r = slice(0, rows)
            k.dma(xt[j], xt[j].ap[r, :], src_b, src_ap)
            k.tt("gpsimd", sq, sq.ap[r, :], xt[j], xt[j].ap[r, :], xt[j], xt[j].ap[r, :], ALU.mult)
            k.op("vector", (lambda o_, i_: (lambda e: e.tensor_reduce(out=o_, in_=i_, op=ALU.add, axis=AX.X)))(
                ssq[j].ap[r, :], sq.ap[r, :]), reads=[sq], writes=[ssq[j]])
            k.act(std[j], std[j].ap[r, :], ssq[j], ssq[j].ap[r, :], AF.Sqrt, scale=1.0 / D, bias=epsb.ap[r, :],
                  sbufs=[epsb])
            k.op("vector", (lambda o_, i_: (lambda e: e.reciprocal(out=o_, in_=i_)))(rstd[j].ap[r, :], std[j].ap[r, :]),
                 reads=[std[j]], writes=[rstd[j]])
            k.stt(tmp[j], tmp[j].ap[r, :], xt[j], xt[j].ap[r, :], rstd[j].ap[r, 0:1], ALU.mult,
                  Ab, Ab.ap[r, :], ALU.mult, sbufs=[rstd[j]])
            k.tt("gpsimd", hb[j], hb[j].ap[r, :], tmp[j], tmp[j].ap[r, :], Bb, Bb.ap[r, :], ALU.add)
            bank = B[j]
            bview = bank.ap.bitcast(BF16)
            for kc in range(8):
                k.tr(bank, bview[:, kc * 128:kc * 128 + rows], hb[j], hb[j].ap[r, kc * 128:(kc + 1) * 128],
                     identb, identb.ap[r, r])
            if si is None:
                k.copy("scalar", hTt[j], hTt[j].ap, bank, bview)
                k.dma(hTp, hTp.ap[:, :, ti * 128:(ti + 1) * 128], hTt[j], v3(hTt[j].ap, 8))
            else:
                k.copy("scalar", hTt[j], v3(hTt[j].ap, 8)[:, :, 0:rows], bank, v3(bview, 8)[:, :, 0:rows])
                k.dma(hTs, hTs.ap[:, :, si * DEC_T:(si + 1) * DEC_T], hTt[j], v3(hTt[j].ap, 8)[:, :, 0:rows])

    def phase_proj_a():
        k.phase()
        W = [k.alloc(f"Wa{j}", 8 * D, BF16) for j in range(3)]
        Wm = [k.alloc(f"Wm{j}", 8 * HM * DH, BF16) for j in range(3)]
        hsb = k.alloc("hsb", 8 * 2048, BF16)
        hss = k.alloc("hss", 8 * SPC * DEC_T, BF16)
        tab = [k.alloc(f"tab{i}", 64) for i in range(2)]
        tabs = k.alloc("tabs", 64)
        raw = [k.alloc(f"raw{i}", D) for i in range(4)]
        rp = [k.alloc(f"rp{i}", D) for i in range(4)]
        t1 = k.alloc("t1", 512)
        t2 = k.alloc("t2", 512)
        t3 = k.alloc("t3", 512)
        t4 = k.alloc("t4", 512)
        vf = [k.alloc(f"vf{i}", D) for i in range(4)]
        vaug = [k.alloc(f"vaug{i}", NH * 65, BF16) for i in range(4)]
        qkT = [k.alloc(f"qkT{i}", NH * 128, BF16) for i in range(4)]
        for i in range(4):
            k.memset("vector", vaug[i], vaug[i].ap, 1.0)
        k.dma(hss, v3(hss.ap, 8), hTs, hTs.ap)
        k.dma(tabs, tabs.ap[0:DEC_T, :], rope_s, rope_s.ap)
        cnt = {"raw": 0, "v": 0, "T": 0}

        def rope(dst, src, tb, rows, nh):
            r = slice(0, rows)
            s4 = src.ap[:, 0:nh * DH].rearrange("p (h t e) -> p h t e", h=nh, t=2)
            d4 = dst.ap[:, 0:nh * DH].rearrange("p (h t e) -> p h t e", h=nh, t=2)
            x1, x2 = s4[r, :, 0, :], s4[r, :, 1, :]
            cosb = tb.ap[r, 0:32].rearrange("p (o e) -> p o e", o=1).to_broadcast([rows, nh, 32])
            sinb = tb.ap[r, 32:64].rearrange("p (o e) -> p o e", o=1).to_broadcast([rows, nh, 32])
            tv = lambda t: v3(t.ap[:, 0:nh * 32], nh)[r]
            k.tt("vector", t1, tv(t1), src, x1, tb, cosb, ALU.mult)
            k.tt("gpsimd", t2, tv(t2), src, x2, tb, sinb, ALU.mult)
            k.tt("vector", dst, d4[r, :, 0, :], t1, tv(t1), t2, tv(t2), ALU.subtract)
            k.tt("gpsimd", t3, tv(t3), src, x1, tb, sinb, ALU.mult)
            k.tt("vector", t4, tv(t4), src, x2, tb, cosb, ALU.mult)
            k.tt("gpsimd", dst, d4[r, :, 1, :], t3, tv(t3), t4, tv(t4), ALU.add)

        def project(lhs_b, lhs_fn, rows, wt, ncol):
            i = cnt["raw"] % 3
            cnt["raw"] += 1
            pieces = []
            for pi, c0 in enumerate(range(0, ncol, 512)):
                wdt = min(512, ncol - c0)
                bank = B[2 * i + pi]
                for kc in range(8):
                    k.mm(bank, bank.ap[0:rows, 0:wdt], lhs_b, lhs_fn(kc),
                         wt, v3(wt.ap, 8)[:, kc, c0:c0 + wdt], start=(kc == 0), stop=(kc == 7))
                pieces.append((bank, c0, wdt))
            return pieces

        def qk_path(pieces, rows, tb, out_dram_fn, kout, nh):
            r = slice(0, rows)
            j = cnt["T"] % 4
            cnt["T"] += 1
            for (bank, c0, wdt) in pieces:
                k.copy("scalar", raw[j], raw[j].ap[r, c0:c0 + wdt], bank, bank.ap[r, 0:wdt])
            rope(rp[j], raw[j], tb, rows, nh)
            for (ob, oap) in kout:
                k.dma(ob, oap, rp[j], rp[j].ap[r, 0:nh * DH])
            tbs = (B[6], B[7])
            qv = v3(qkT[j].ap, NH)
            for h0 in range(0, nh, 8):
                hn = min(8, nh - h0)
                for hh in range(hn):
                    h = h0 + hh
                    bank = tbs[hh // 4]
                    k.tr(bank, bank.ap[0:64, (hh % 4) * 128:(hh % 4) * 128 + rows], rp[j], rp[j].ap[r, h * 64:(h + 1) * 64],
                         identf, identf.ap[r, r])
                for bi_ in range((hn + 3) // 4):
                    k.copy("scalar", qkT[j], qv[0:64, h0 + 4 * bi_:h0 + 4 * bi_ + 4, 0:rows], tbs[bi_],
                           v3(tbs[bi_].ap, 4)[0:64, :, 0:rows])
            ob, oap = out_dram_fn
            k.dma(ob, oap, qkT[j], qv[0:64, 0:nh, 0:rows])

        def v_path(pieces, rows, vout, vs_out, nh):
            r = slice(0, rows)
            j = cnt["v"] % 4
            cnt["v"] += 1
            for (bank, c0, wdt) in pieces:
                k.copy("scalar", vf[j], vf[j].ap[r, c0:c0 + wdt], bank, bank.ap[r, 0:wdt])
            for (ob, oap) in vout:
                k.dma(ob, oap, vf[j], vf[j].ap[r, 0:nh * DH])
            k.copy("vector", vaug[j], v3(vaug[j].ap, NH)[r, 0:nh, 0:64], vf[j], v3(vf[j].ap, NH)[r, 0:nh])
            ob, oap = vs_out
            k.dma(ob, oap, vaug[j], vaug[j].ap[r, 0:nh * 65])

        for g, (win, d) in enumerate(A_GROUPS):
            nb = group_blocks(d)
            for j in range(3):
                k.dma(W[j], v3(W[j].ap, 8),
                      a_w_in, a_w_in.ap.rearrange("(k p) n -> p k n", p=128)[:, :, (g * 3 + j) * D:(g * 3 + j + 1) * D],
                      eng="gpsimd")
                wc = HM * DH
                k.dma(Wm[j], v3(Wm[j].ap, 8), a_w_in_m,
                      a_w_in_m.ap.rearrange("(k p) n -> p k n", p=128)[:, :, (g * 3 + j) * wc:(g * 3 + j + 1) * wc], eng="gpsimd")
            for sb in range(4):
                k.dma(hsb, v3(hsb.ap, 8), hTp, hTp.ap[:, :, sb * 2048:(sb + 1) * 2048])
                for r_ in range(d):
                    for b_ in range(nb):
                        n = sb * 16 + r_ * nb + b_
                        c0 = r_ + d * b_ * 128
                        lhs = lambda kc, c0=c0, d=d: v3(hsb.ap, 8)[:, kc, c0:c0 + 127 * d + 1:d]
                        tb = tab[n % 2]
                        k.dma(tb, tb.ap, rope_p, rope_p.ap[g, n])
                        last = (sb == 3 and b_ == nb - 1)
                        bq = project(hsb, lhs, 128, Wm[0], HM * DH)
                        bk = project(hsb, lhs, 128, Wm[1], HM * DH)
                        bv = project(hsb, lhs, 128, Wm[2], HM * DH)
                        qk_path(bq, 128, tb, (qT_scr, qT_scr.ap[g][:, :, n * 128:(n + 1) * 128]), [], HM)
                        kout = [(new_p[("k", win)], new_p[("k", win)].ap[r_::d, :])] if last else []
                        qk_path(bk, 128, tb, (kT_scr, kT_scr.ap[g][:, :, n * 128:(n + 1) * 128]), kout, HM)
                        vout = [(new_p[("v", win)], new_p[("v", win)].ap[r_::d, :])] if last else []
                        v_path(bv, 128, vout, (V_scr, V_scr.ap[g, n]), HM)
            for s in range(SPC):
                lhs = lambda kc, s=s: v3(hss.ap, 8)[:, kc, s * DEC_T:(s + 1) * DEC_T]
                cs8 = slice(s * DEC_T, (s + 1) * DEC_T)
                bq = project(hss, lhs, DEC_T, W[0], D)
                qk_path(bq, DEC_T, tabs, (qTs_scr, qTs_scr.ap[g][:, :, cs8]), [], NH)
                bk = project(hss, lhs, DEC_T, W[1], D)
                qk_path(bk, DEC_T, tabs, (kTs_scr, kTs_scr.ap[g][:, :, cs8]),
                        [(new_s[("k", win)], new_s[("k", win)].ap[s, win - DEC_T:win, :])], NH)
                bv = project(hss, lhs, DEC_T, W[2], D)
                v_path(bv, DEC_T, [(new_s[("v", win)], new_s[("v", win)].ap[s, win - DEC_T:win, :])],
                       (Vs_scr, Vs_scr.ap[g, s]), NH)

    def prev_block(n, d):
        nb = group_blocks(d)
        sb, rem = divmod(n, 16)
        r_, b_ = divmod(rem, nb)
        if b_ > 0:
            return n - 1
        if sb > 0:
            return (sb - 1) * 16 + r_ * nb + nb - 1
        return None

    def recip(eng, out_b, out_ap, in_b, in_ap):
        k.op(eng, lambda e: e.reciprocal(out=out_ap, in_=in_ap), reads=[in_b], writes=[out_b])

    def phase_attn():
        k.phase()
        mask4 = k.alloc("mask4", 512, BF16)
        k.dma(mask4, mask4.ap, mask4_in, mask4_in.ap, eng="gpsimd")
        qh = [k.alloc(f"qh{i}", 2 * SEQ, BF16) for i in range(1)] * 2
        kh = [k.alloc(f"kh{i}", 2 * SEQ, BF16) for i in range(1)] * 2
        Vh = [k.alloc(f"Vh{i}", NT * 130, BF16) for i in range(2)]
        P = [k.alloc(f"P{i}", 512, BF16) for i in range(3)]
        osb = [k.alloc(f"osb{i}", 512) for i in range(4)]
        it = 0
        for g, (win, d) in enumerate(A_GROUPS):
            for hp in range(HM // 2):
                j = it % 2
                it += 1
                q3_, k3_ = v3(qh[j].ap, 2), v3(kh[j].ap, 2)
                k.dma(qh[j], q3_[0:64], qT_scr, qT_scr.ap[g][:, 2 * hp:2 * hp + 2, :])
                k.dma(kh[j], k3_[0:64], kT_scr, kT_scr.ap[g][:, 2 * hp:2 * hp + 2, :])
                for n8 in range(0, NT, 8):
                    k.dma(Vh[j], v3(Vh[j].ap, NT)[:, n8:n8 + 8], V_scr,
                          V_scr.ap[g].rearrange("n p c -> p n c")[:, n8:n8 + 8, hp * 130:(hp + 1) * 130])
                Vv = v3(Vh[j].ap, NT)
                def scores(n):
                    pn = prev_block(n, d)
                    sbank = B[n % 4]
                    cn = slice(n * 128, (n + 1) * 128)
                    for h in range(2):
                        k.mm(sbank, sbank.ap[:, (2 * h) * 128:(2 * h + 1) * 128], kh[j], k3_[0:64, h, cn],
                             qh[j], q3_[0:64, h, cn])
                        if pn is not None:
                            k.mm(sbank, sbank.ap[:, (2 * h + 1) * 128:(2 * h + 2) * 128], kh[j],
                                 k3_[0:64, h, pn * 128:(pn + 1) * 128], qh[j], q3_[0:64, h, cn])

                scores(0)
                for n in range(NT):
                    pn = prev_block(n, d)
                    sbank = B[n % 4]
                    if n + 1 < NT:
                        scores(n + 1)
                    Pb = P[n % 3]
                    if pn is not None:
                        k.act(Pb, Pb.ap, sbank, sbank.ap, AF.Exp, scale=DH ** -0.5)
                        k.tt("gpsimd" if n % 2 else "vector", Pb, Pb.ap, Pb, Pb.ap, mask4, mask4.ap, ALU.mult)
                    else:
                        for h in range(2):
                            c_ = slice((2 * h) * 128, (2 * h + 1) * 128)
                            k.act(Pb, Pb.ap[:, c_], sbank, sbank.ap[:, c_], AF.Exp, scale=DH ** -0.5)
                            k.tt("vector", Pb, Pb.ap[:, c_], Pb, Pb.ap[:, c_], mask4, mask4.ap[:, c_], ALU.mult)
                    q4 = n % 4
                    par = (n // 4) % 2
                    for h in range(2):
                        acc = B[4 + 2 * par + h]
                        oap = acc.ap[0:65, q4 * 128:(q4 + 1) * 128]
                        k.mm(acc, oap, Vh[j], Vv[:, n, h * 65:(h + 1) * 65], Pb, Pb.ap[:, (2 * h) * 128:(2 * h + 1) * 128],
                             start=True, stop=(pn is None))
                        if pn is not None:
                            k.mm(acc, oap, Vh[j], Vv[:, pn, h * 65:(h + 1) * 65], Pb,
                                 Pb.ap[:, (2 * h + 1) * 128:(2 * h + 2) * 128], start=False, stop=True)
                    if q4 == 3:
                        for h in range(2):
                            acc = B[4 + 2 * par + h]
                            ob = osb[2 * par + h]
                            k.copy("scalar" if h else "vector", ob, ob.ap[0:65, :], acc, acc.ap[0:65, :])
                            k.dma(o_scr, o_scr.ap[g, hp * 2 + h][:, (n - 3) * 128:(n + 1) * 128], ob, ob.ap[0:65, :])

    def phase_attn_sample():
        k.phase()
        E = k.alloc("E", 64)
        k.dma(E, E.ap[0:65, :], E_in, E_in.ap)
        mks = k.alloc("mks", 3 * 8 * 8, BF16)
        mkn = k.alloc("mkn", 3 * 8, BF16)
        for g_ in range(3):
            k.dma(mks, mks.ap.rearrange("p (g c t) -> p g c t", g=3, c=8)[:, g_], msk_s_in,
                  msk_s_in.ap[g_].rearrange("c p t -> p c t"), eng="gpsimd")
        k.dma(mkn, v3(mkn.ap, 3)[0:8], msk_n_in, msk_n_in.ap.rearrange("g p t -> p g t"), eng="gpsimd")
        mks4 = mks.ap.rearrange("p (g c t) -> p g c t", g=3, c=8)
        mkn3 = v3(mkn.ap, 3)
        qs = [k.alloc(f"qs{i}", 3 * NH * 8, BF16) for i in range(2)]
        kn = [k.alloc(f"kn{i}", 3 * NH * 8, BF16) for i in range(2)]
        Vn = [k.alloc(f"Vn{i}", 3 * NH * 65, BF16) for i in range(2)]
        Kc = [k.alloc(f"Kc{i}", D) for i in range(2)]
        Vc = [k.alloc(f"Vc{i}", D) for i in range(2)]
        kTc = [k.alloc(f"kTc{i}", NH * 128, BF16) for i in range(2)]
        Vac = [k.alloc(f"Vac{i}", NH * 65, BF16) for i in range(2)]
        Pc = [k.alloc(f"Pc{i}", 128, BF16) for i in range(2)]
        Pn = [k.alloc(f"Pn{i}", 128, BF16) for i in range(2)]
        tot = [k.alloc(f"tots{i}", 128) for i in range(2)]
        rd = k.alloc("rds", 128)
        oTs = [k.alloc(f"oTs{i}", 128, BF16) for i in range(2)]
        for i in range(2):
            k.memset("vector", Vac[i], Vac[i].ap, 1.0)
        it = 0
        for s in range(SPC):
            js = s % 2
            cs8 = slice(s * DEC_T, (s + 1) * DEC_T)
            q4 = qs[js].ap.rearrange("p (g c t) -> p g c t", g=3, c=NH)[0:64]
            k4 = kn[js].ap.rearrange("p (g c t) -> p g c t", g=3, c=NH)[0:64]
            V3 = v3(Vn[js].ap, 3)
            for g_ in range(3):
                k.dma(qs[js], q4[:, g_], qTs_scr, qTs_scr.ap[g_][:, :, cs8])
                k.dma(kn[js], k4[:, g_], kTs_scr, kTs_scr.ap[g_][:, :, cs8])
            k.dma(Vn[js], V3[0:DEC_T], Vs_scr, Vs_scr.ap[:, s].rearrange("g t c -> t g c"))
            nacc = [0]

            def accumulate(pvb):
                if nacc[0] == 0:
                    k.copy("vector", tot[js], tot[js].ap[0:65, :], pvb, pvb.ap[0:65, 0:128])
                else:
                    k.tt("vector", tot[js], tot[js].ap[0:65, :], tot[js], tot[js].ap[0:65, :], pvb, pvb.ap[0:65, 0:128], ALU.add)
                nacc[0] += 1
            for g, (win, d) in enumerate(A_GROUPS):
                for t0 in range(min(d, DEC_T)):
                    j = it % 2
                    it += 1
                    rows_sl = slice(t0, t0 + 127 * d + 1, d)
                    k.dma(Kc[j], Kc[j].ap, cache[("k", win)], cache[("k", win)].ap[s, rows_sl, :])
                    k.dma(Vc[j], Vc[j].ap, cache[("v", win)], cache[("v", win)].ap[s, rows_sl, :])
                    tb0, tb1 = B[2 * j], B[2 * j + 1]
                    for rnd in range(2):
                        for hh in range(8):
                            h = rnd * 8 + hh
                            bank = tb0 if hh < 4 else tb1
                            k.tr(bank, bank.ap[0:64, (hh % 4) * 128:(hh % 4 + 1) * 128], Kc[j], Kc[j].ap[:, h * 64:(h + 1) * 64],
                                 identf, identf.ap)
                        c0_ = rnd * 1024
                        k.copy("scalar", kTc[j], kTc[j].ap[0:64, c0_:c0_ + 512], tb0, tb0.ap[0:64, :])
                        k.copy("scalar", kTc[j], kTc[j].ap[0:64, c0_ + 512:c0_ + 1024], tb1, tb1.ap[0:64, :])
                    k.copy("vector", Vac[j], v3(Vac[j].ap, NH)[:, :, 0:64], Vc[j], v3(Vc[j].ap, NH))
                    sbank = B[6 + j]
                    kT3 = v3(kTc[j].ap, NH)
                    for h in range(NH):
                        k.mm(sbank, sbank.ap[:, h * 8:(h + 1) * 8], kTc[j], kT3[0:64, h, :],
                             qs[js], q4[:, g, h, :])
                    k.act(Pc[j], Pc[j].ap, sbank, sbank.ap[:, 0:128], AF.Exp, scale=DH ** -0.5)
                    mb = mks4[:, g, t0, :].rearrange("p (o t) -> p o t", o=1).to_broadcast([128, NH, DEC_T])
                    k.tt("vector", Pc[j], v3(Pc[j].ap, NH), Pc[j], v3(Pc[j].ap, NH), mks, mb, ALU.mult)
                    Va3 = v3(Vac[j].ap, NH)
                    acc = B[4 + j]
                    for h in range(NH):
                        k.mm(acc, acc.ap[0:65, h * 8:(h + 1) * 8], Vac[j], Va3[:, h, :], Pc[j], Pc[j].ap[:, h * 8:(h + 1) * 8])
                    accumulate(acc)
                j = it % 2
                sbank = B[6 + j]
                for h in range(NH):
                    k.mm(sbank, sbank.ap[0:DEC_T, 128 + h * 8:128 + (h + 1) * 8], kn[js], k4[:, g, h, :],
                         qs[js], q4[:, g, h, :])
                k.act(Pn[j], Pn[j].ap[0:DEC_T, :], sbank, sbank.ap[0:DEC_T, 128:256], AF.Exp, scale=DH ** -0.5)
                mb = mkn3[0:DEC_T, g, :].rearrange("p (o t) -> p o t", o=1).to_broadcast([DEC_T, NH, DEC_T])
                k.tt("vector", Pn[j], v3(Pn[j].ap, NH)[0:DEC_T], Pn[j], v3(Pn[j].ap, NH)[0:DEC_T], mkn, mb, ALU.mult)
                Vg = V3[0:DEC_T, g, :].rearrange("p (h c) -> p h c", h=NH)
                acc = B[4 + j]
                for h in range(NH):
                    k.mm(acc, acc.ap[0:65, h * 8:(h + 1) * 8], Vn[js], Vg[:, h, :], Pn[j], Pn[j].ap[0:DEC_T, h * 8:(h + 1) * 8])
                accumulate(acc)
            dbank = B[js]
            k.mm(dbank, dbank.ap[0:64, 0:128], E, E.ap[0:65, :], tot[js], tot[js].ap[0:65, :])
            recip("vector", rd, rd.ap[0:64, :], dbank, dbank.ap[0:64, 0:128])
            k.tt("vector", oTs[js], oTs[js].ap[0:64, :], tot[js], tot[js].ap[0:64, :], rd, rd.ap[0:64, :], ALU.mult)
            k.dma(oTs_scr, oTs_scr.ap[s], oTs[js], oTs[js].ap[0:64, :])

    def phase_combine():
        k.phase()
        E = k.alloc("E", 64)
        k.dma(E, E.ap[0:65, :], E_in, E_in.ap)
        og = [[k.alloc(f"og{i}_{g}", 2048) for g in range(3)] for i in range(2)]
        rden = [k.alloc(f"rden{i}", 512) for i in range(2)]
        oTb = [k.alloc(f"oTb{i}", 2048, BF16) for i in range(2)]
        it = 0
        for h in range(HM):
            for sb in range(4):
                j = it % 2
                it += 1
                cs_ = slice(sb * 2048, (sb + 1) * 2048)
                for g in range(3):
                    k.dma(og[j][g], og[j][g].ap[0:65, :], o_scr, o_scr.ap[g, h][:, cs_])
                tot = og[j][0]
                for g, r in ((1, 4), (2, 16)):
                    tv = tot.ap[0:65, :].rearrange("p (i r) -> p i r", r=r)
                    gv = og[j][g].ap[0:65, :].rearrange("p (r i) -> p i r", r=r)
                    k.tt("vector" if g == 1 else "gpsimd", tot, tv, tot, tv, og[j][g], gv, ALU.add)
                for q in range(4):
                    bank = B[(it * 4 + q) % 8]
                    cq = slice(q * 512, (q + 1) * 512)
                    k.mm(bank, bank.ap[0:64, :], E, E.ap[0:65, :], tot, tot.ap[0:65, cq])
                    rd = rden[q % 2]
                    recip("vector", rd, rd.ap[0:64, :], bank, bank.ap[0:64, :])
                    k.tt("gpsimd", oTb[j], oTb[j].ap[0:64, cq], tot, tot.ap[0:64, cq], rd, rd.ap[0:64, :], ALU.mult)
                k.dma(oT_mine[sb], oT_mine[sb].ap[h * 64:(h + 1) * 64, :], oTb[j], oTb[j].ap[0:64, :])
        for sb in range(4):
            k.allgather(oT_scr[sb], oT_scr[sb].ap, oT_mine[sb], oT_mine[sb].ap, CC_GROUPS)

    def phase_out_a():
        k.phase()
        wo = k.alloc("wo", NH * D, BF16)
        wo3 = v3(wo.ap, NH)
        k.dma(wo, wo3[0:64], a_w_out, a_w_out.ap.rearrange("(h e) n -> e h n", e=64), eng="gpsimd")
        G = k.alloc("G", D)
        Gs = [k.alloc(f"Gs{i}", D) for i in range(2)]
        k.dma(G, G.ap, mod_scr, mod_scr.ap[0][0:1, 2 * D:3 * D].partition_broadcast(128))
        o4 = [k.alloc(f"o4{i}", NH * 512, BF16) for i in range(2)]
        os8 = [k.alloc(f"os8{i}", NH * DEC_T, BF16) for i in range(2)]
        xt = [k.alloc(f"xt{i}", D) for i in range(2)]
        yt = [k.alloc(f"yt{i}", D) for i in range(2)]
        nb_ = 0

        def one(rows, lhs_b, lhs_fn, x_src_b, x_src_ap, Gb, dst_b, dst_ap, j):
            nonlocal nb_
            r = slice(0, rows)
            k.dma(xt[j], xt[j].ap[r, :], x_src_b, x_src_ap)
            for half in range(2):
                bank = B[nb_ % 8]
                nb_ += 1
                for h in range(NH):
                    k.mm(bank, bank.ap[r, :], lhs_b, lhs_fn(h), wo, wo3[0:64, h, half * 512:(half + 1) * 512],
                         start=(h == 0), stop=(h == NH - 1))
                hs = slice(half * 512, (half + 1) * 512)
                k.tt("vector", yt[j], yt[j].ap[r, hs], bank, bank.ap[r, :], Gb, Gb.ap[r, hs], ALU.mult)
                k.tt("gpsimd", yt[j], yt[j].ap[r, hs], yt[j], yt[j].ap[r, hs], xt[j], xt[j].ap[r, hs], ALU.add)
            k.dma(dst_b, dst_ap, yt[j], yt[j].ap[r, :])

        for t in range(NT):
            jj = (t // 4) % 2
            if t % 4 == 0:
                osb_ = oT_scr[t // 16]
                k.dma(o4[jj], v3(o4[jj].ap, NH)[0:64], osb_,
                      osb_.ap[:, (t % 16) * 128:(t % 16 + 4) * 128].rearrange("(h e) c -> e h c", e=64))
            ov = v3(o4[jj].ap, NH)
            c_ = slice((t % 4) * 128, (t % 4 + 1) * 128)
            one(128, o4[jj], (lambda h, ov=ov, c_=c_: ov[0:64, h, c_]), xp, xp.ap[t * 128:(t + 1) * 128, :], G,
                x1p, x1p.ap[t * 128:(t + 1) * 128, :], t % 2)
        for s in range(SPC):
            j = s % 2
            k.dma(os8[j], os8[j].ap[0:64, :], oTs_scr, oTs_scr.ap[s])
            k.dma(Gs[j], Gs[j].ap[0:DEC_T, :], mod_scr, mod_scr.ap[0][1 + s:2 + s, 2 * D:3 * D].partition_broadcast(DEC_T))
            ov = v3(os8[j].ap, NH)
            one(DEC_T, os8[j], (lambda h, ov=ov: ov[0:64, h, :]), xs, xs.ap[s], Gs[j], x1s, x1s.ap[s], j)

    phase_mod()
    phase_norm(0, xp, xs, 1 * D, 0)
    phase_proj_a()
    def phase_ffn(layer, xin_p, xin_s, xout_p, xout_s):
        k.phase()
        TB = 1024
        NC_ = D_FF // 128
        wd = k.alloc("wd", NC_ * D, BF16)
        wd3 = v3(wd.ap, NC_)
        k.dma(wd, wd3, ffn_w_down, ffn_w_down.ap[layer].rearrange("(c p) n -> p c n", p=128), eng="gpsimd")
        cw = k.alloc("cw", NC_ * 4)
        cw3 = v3(cw.ap, NC_)
        k.dma(cw, cw3, ffn_cw, ffn_cw.ap[layer])
        carry = k.alloc("carry", NC_ * 2)
        carry3 = v3(carry.ap, NC_)
        k.memset("vector", carry, carry.ap, 0.0)
        carry_s = k.alloc("carry_s", NC_ * SPC * 2)
        cs4 = carry_s.ap.rearrange("p (c s t) -> p c s t", c=NC_, s=SPC)
        for c in range(0, NC_, 11):
            k.dma(carry_s, cs4[:, c:c + 11].rearrange("p c s t -> p c (s t)"), fconvT,
                  fconvT.ap[layer][:, c:c + 11].rearrange("p c s t -> p c (s t)"))
        G = k.alloc("G", D)
        k.dma(G, G.ap, mod_scr, mod_scr.ap[layer][0:1, 5 * D:6 * D].partition_broadcast(128))
        Gs = k.alloc("Gs", D)
        for s in range(SPC):
            k.dma(Gs, Gs.ap[s * DEC_T:(s + 1) * DEC_T, :], mod_scr,
                  mod_scr.ap[layer][1 + s:2 + s, 5 * D:6 * D].partition_broadcast(DEC_T))
        hb = [k.alloc(f"hblk{i}", 8 * TB, BF16) for i in range(2)]
        act = k.alloc("act", NC_ * TB, BF16)
        act3 = v3(act.ap, NC_)
        wg = [k.alloc(f"wg{i}", 8 * 128, BF16) for i in range(2)]
        wv = [k.alloc(f"wv{i}", 8 * 128, BF16) for i in range(2)]
        gp = [k.alloc(f"gp{i}", TB + 2) for i in range(2)]
        tc_ = [k.alloc(f"tc{i}", TB) for i in range(2)]
        sg = [k.alloc(f"sg{i}", TB) for i in range(2)]
        xt = [k.alloc(f"xt{i}", D) for i in range(2)]
        yt = [k.alloc(f"yt{i}", D) for i in range(2)]
        wup = ffn_w_up.ap[layer].rearrange("(k p) n -> p k n", p=128)
        cnt = 0
        blocks = [("p", b) for b in range(SEQ // TB)] + [("s", 0)]
        for bi, (kind, b) in enumerate(blocks):
            hbb = hb[bi % 2]
            if kind == "p":
                ntok = TB
                k.dma(hbb, v3(hbb.ap, 8), hTp, hTp.ap[:, :, b * TB:(b + 1) * TB])
                h3 = v3(hbb.ap, 8)
            else:
                ntok = SPC * DEC_T
                h3 = v3(hbb.ap, 8)[:, :, 0:ntok]
                k.dma(hbb, h3, hTs, hTs.ap)
            nq = max(1, ntok // 512)
            qw = min(512, ntok)
            for c in range(NC_):
                j = cnt % 2
                cnt += 1
                k.dma(wg[j], v3(wg[j].ap, 8), ffn_w_up, wup[:, :, c * 128:(c + 1) * 128], eng="gpsimd")
                k.dma(wv[j], v3(wv[j].ap, 8), ffn_w_up, wup[:, :, D_FF + c * 128:D_FF + (c + 1) * 128], eng="gpsimd")
                gpj = gp[j]
                if kind == "p":
                    k.copy("gpsimd", gpj, gpj.ap[:, 0:2], carry, carry3[:, c, :])
                    gnew = gpj.ap[:, 2:2 + ntok]
                else:
                    g3 = gpj.ap[:, 0:SPC * 10].rearrange("p (s t) -> p s t", s=SPC)
                    k.copy("gpsimd", gpj, g3[:, :, 0:2], carry_s, cs4[:, c])
                vbanks = []
                for q in range(nq):
                    gb_ = B[(4 * cnt + 2 * q) % 8]
                    vb_ = B[(4 * cnt + 2 * q + 1) % 8]
                    cq = slice(q * qw, (q + 1) * qw)
                    for kc in range(8):
                        k.mm(gb_, gb_.ap[:, 0:qw], wg[j], v3(wg[j].ap, 8)[:, kc, :], hbb, h3[:, kc, cq],
                             start=(kc == 0), stop=(kc == 7))
                    for kc in range(8):
                        k.mm(vb_, vb_.ap[:, 0:qw], wv[j], v3(wv[j].ap, 8)[:, kc, :], hbb, h3[:, kc, cq],
                             start=(kc == 0), stop=(kc == 7))
                    if kind == "p":
                        k.copy("scalar", gpj, gpj.ap[:, 2 + q * qw:2 + (q + 1) * qw], gb_, gb_.ap[:, 0:qw])
                    else:
                        k.copy("scalar", gpj, g3[:, :, 2:10], gb_, gb_.ap[:, 0:qw].rearrange("p (s t) -> p s t", s=SPC))
                    vbanks.append(vb_)
                w0, w1, w2, bb = (cw3[:, c, i_:i_ + 1] for i_ in range(4))
                if kind == "p":
                    x0, x1_, x2_ = gpj.ap[:, 0:ntok], gpj.ap[:, 1:1 + ntok], gpj.ap[:, 2:2 + ntok]
                    tcv, sgv = tc_[j].ap[:, 0:ntok], sg[j].ap[:, 0:ntok]
                else:
                    x0, x1_, x2_ = g3[:, :, 0:8], g3[:, :, 1:9], g3[:, :, 2:10]
                    tcv = tc_[j].ap[:, 0:ntok].rearrange("p (s t) -> p s t", s=SPC)
                    sgv = sg[j].ap[:, 0:ntok]
                k.ts("vector", tc_[j], tcv, gpj, x0, w0, ALU.mult, sbufs=[cw])
                k.stt(tc_[j], tcv, gpj, x1_, w1, ALU.mult, tc_[j], tcv, ALU.add, sbufs=[cw])
                k.stt(tc_[j], tcv, gpj, x2_, w2, ALU.mult, tc_[j], tcv, ALU.add, sbufs=[cw])
                k.act(sg[j], sgv, tc_[j], tc_[j].ap[:, 0:ntok], AF.Silu, bias=bb, sbufs=[cw])
                for q in range(nq):
                    cq = slice(q * qw, (q + 1) * qw)
                    k.tt("vector", act, act3[:, c, cq], vbanks[q], vbanks[q].ap[:, 0:qw], sg[j], sg[j].ap[:, cq], ALU.mult)
                if kind == "p":
                    k.copy("gpsimd", carry, carry3[:, c, :], gpj, gpj.ap[:, ntok:ntok + 2])
                else:
                    k.copy("gpsimd", carry_s, cs4[:, c], gpj, g3[:, :, 8:10])
            ntile = max(1, ntok // 128)
            rows = min(128, ntok)
            r = slice(0, rows)
            for t in range(ntile):
                j = t % 2
                if kind == "p":
                    tok0 = b * TB + t * 128
                    k.dma(xt[j], xt[j].ap, xin_p, xin_p.ap[tok0:tok0 + 128, :])
                    Gb = G
                else:
                    k.dma(xt[j], xt[j].ap[r, :], xin_s, xin_s.ap.rearrange("s t d -> (s t) d"))
                    Gb = Gs
                for half in range(2):
                    bank = B[(2 * t + half) % 8]
                    for c in range(NC_):
                        k.mm(bank, bank.ap[r, :], act, act3[:, c, t * 128:t * 128 + rows], wd, wd3[:, c, half * 512:(half + 1) * 512],
                             start=(c == 0), stop=(c == NC_ - 1))
                    hs = slice(half * 512, (half + 1) * 512)
                    k.tt("vector", yt[j], yt[j].ap[r, hs], bank, bank.ap[r, :], Gb, Gb.ap[r, hs], ALU.mult)
                    k.tt("gpsimd", yt[j], yt[j].ap[r, hs], yt[j], yt[j].ap[r, hs], xt[j], xt[j].ap[r, hs], ALU.add)
                if kind == "p":
                    k.dma(xout_p, xout_p.ap[tok0:tok0 + 128, :], yt[j], yt[j].ap)
                else:
                    k.dma(xout_s, xout_s.ap.rearrange("s t d -> (s t) d"), yt[j], yt[j].ap[r, :])
        k.dma(fconv_p_out, fconv_p_out.ap[layer], carry, carry3)
        k.dma(fconv_s_out, fconv_s_out.ap[layer], carry_s, carry_s.ap)

    NQ = 8
    NV = 16
    bankrr = [0]

    def nbank():
        b_ = B[bankrr[0] % 8]
        bankrr[0] += 1
        return b_

    def phase_b1():
        k.phase()
        Wq = k.alloc("Wqkv", 8 * B_CONV_DIM, BF16)
        Wq3 = v3(Wq.ap, 8)
        bw = b_w_in.ap.rearrange("(k p) n -> p k n", p=128)
        for c0 in range(0, B_CONV_DIM, 1024):
            k.dma(Wq, Wq3[:, :, c0:c0 + 1024], b_w_in, bw[:, :, c0:c0 + 1024], eng="gpsimd")
        Wba = k.alloc("Wba", 8 * 32, BF16)
        k.dma(Wba, v3(Wba.ap, 8), b_w_in, bw[:, :, 6144:6176], eng="gpsimd")
        bwm = b_w_in_m.ap.rearrange("(k p) n -> p k n", p=128)
        Wqm = k.alloc("Wqm", 8 * 1024, BF16)
        Wqm3 = v3(Wqm.ap, 8)
        k.dma(Wqm, Wqm3, b_w_in_m, bwm[:, :, 0:1024], eng="gpsimd")
        Wbam = k.alloc("Wbam", 8 * 8, BF16)
        k.dma(Wbam, v3(Wbam.ap, 8), b_w_in_m, bwm[:, :, 1024:1032], eng="gpsimd")
        cwb = k.alloc("cwb", 32 * 4)
        cwb3 = v3(cwb.ap, 32)
        k.dma(cwb, cwb3, b_cw, b_cw.ap)
        cwm = k.alloc("cwm", 8 * 4)
        cwm3 = v3(cwm.ap, 8)
        k.dma(cwm, cwm3, b_cw_m, b_cw_m.ap)
        carry = k.alloc("carryB", 8 * 3)
        carry3 = v3(carry.ap, 8)
        k.memset("vector", carry, carry.ap, 0.0)
        carry_s = k.alloc("carryBs", 32 * SPC * 3)
        cs4 = carry_s.ap.rearrange("p (c s t) -> p c s t", c=32, s=SPC)
        for c in range(0, 32, 8):
            k.dma(carry_s, cs4[:, c:c + 8].rearrange("p c s t -> p c (s t)"), bconvT,
                  bconvT.ap[:, c:c + 8].rearrange("p c s t -> p c (s t)"))
        onesf = k.alloc("onesf", 128)
        k.memset("vector", onesf, onesf.ap, 1.0)
        one1 = k.alloc("one1", 1)
        k.memset("vector", one1, one1.ap, 1.0)

        def head_consts(nm, dt_in, al_in, n):
            dtb_ = k.alloc("dtb" + nm, n)
            negA_ = k.alloc("negA" + nm, n)
            k.dma(dtb_, dtb_.ap, dt_in, dt_in.ap.partition_broadcast(128))
            k.dma(negA_, negA_.ap, al_in, al_in.ap.partition_broadcast(128))
            k.act(negA_, negA_.ap, negA_, negA_.ap, AF.Exp)
            k.ts("vector", negA_, negA_.ap, negA_, negA_.ap, -1.0, ALU.mult)
            return dtb_, negA_

        dtb, negA = head_consts("", b_dt_bias, b_a_log, 16)
        dtbm, negAm = head_consts("m", b_dt_bias_m, b_a_log_m, 4)
        TBk = 512
        hb = [k.alloc(f"hblk{i}", 8 * TBk, BF16) for i in range(2)]
        NB1 = 4
        pre = [k.alloc(f"pre{i}", TBk + 3) for i in range(NB1)]
        tcv_ = [k.alloc(f"tcv{i}", TBk) for i in range(NB1)]
        av = [k.alloc(f"av{i}", TBk) for i in range(NB1)]
        sq = [k.alloc(f"sq{i}", TBk) for i in range(NB1)]
        sd = [k.alloc(f"sd{i}", TBk) for i in range(NB1)]
        xn = [k.alloc(f"xn{i}", TBk, BF16) for i in range(NB1)]
        tmb = [k.alloc(f"tmb{i}", TBk, BF16) for i in range(NB1)]
        braw = k.alloc("braw", 128)
        xa = k.alloc("xa", 64)
        ab = k.alloc("ab", 64)
        sp = k.alloc("sp", 64)
        bgt = [k.alloc(f"bgt{i}", 128) for i in range(2)]
        cnt = 0
        blocks = [("p", b) for b in range(SEQ // TBk)] + [("s", 0)]
        for bi, (kind, b) in enumerate(blocks):
            hbb = hb[bi % 2]
            if kind == "p":
                ntok = TBk
                h3 = v3(hbb.ap, 8)
                k.dma(hbb, h3, hTp, hTp.ap[:, :, b * TBk:(b + 1) * TBk])
                chunks = [(Wqm, Wqm3, cm * 128, [cwm3[:, cm, i_:i_ + 1] for i_ in range(4)], cwm,
                           "q" if cm < 2 else ("k" if cm < 4 else "v"), cm if cm < 2 else (cm - 2 if cm < 4 else cm - 4), cm)
                          for cm in range(8)]
                nvh = 4
            else:
                ntok = SPC * DEC_T
                h3 = v3(hbb.ap, 8)[:, :, 0:ntok]
                k.dma(hbb, h3, hTs, hTs.ap)
                chunks = [(Wq, Wq3, cc * 128, [cwb3[:, cc, i_:i_ + 1] for i_ in range(4)], cwb,
                           "q" if cc < 8 else ("k" if cc < 16 else "v"), cc if cc < 8 else (cc - 8 if cc < 16 else cc - 16), cc)
                          for cc in range(32)]
                nvh = 16
            for (wt, wt3, wcol, w, cwbuf, role, hidx, ci_) in chunks:
                j = cnt % NB1
                cnt += 1
                bank = nbank()
                for kc in range(8):
                    k.mm(bank, bank.ap[:, 0:ntok], wt, wt3[:, kc, wcol:wcol + 128], hbb, h3[:, kc, :],
                         start=(kc == 0), stop=(kc == 7))
                pj = pre[j]
                if kind == "p":
                    k.copy("gpsimd", pj, pj.ap[:, 0:3], carry, carry3[:, ci_, :])
                    k.copy("scalar", pj, pj.ap[:, 3:3 + ntok], bank, bank.ap[:, 0:ntok])
                    k.copy("gpsimd", carry, carry3[:, ci_, :], pj, pj.ap[:, ntok:ntok + 3])
                    xs_ = [pj.ap[:, i_:i_ + ntok] for i_ in range(4)]
                    tv = tcv_[j].ap[:, 0:ntok]
                else:
                    p3 = pj.ap[:, 0:SPC * 11].rearrange("p (s t) -> p s t", s=SPC)
                    k.copy("gpsimd", pj, p3[:, :, 0:3], carry_s, cs4[:, ci_])
                    k.copy("scalar", pj, p3[:, :, 3:11], bank, bank.ap[:, 0:ntok].rearrange("p (s t) -> p s t", s=SPC))
                    k.copy("gpsimd", carry_s, cs4[:, ci_], pj, p3[:, :, 8:11])
                    xs_ = [p3[:, :, i_:i_ + 8] for i_ in range(4)]
                    tv = tcv_[j].ap[:, 0:ntok].rearrange("p (s t) -> p s t", s=SPC)
                k.ts("vector", tcv_[j], tv, pj, xs_[0], w[0], ALU.mult, sbufs=[cwbuf])
                for i_ in range(1, 4):
                    k.stt(tcv_[j], tv, pj, xs_[i_], w[i_], ALU.mult, tcv_[j], tv, ALU.add, sbufs=[cwbuf])
                a_ = av[j]
                k.act(a_, a_.ap[:, 0:ntok], tcv_[j], tcv_[j].ap[:, 0:ntok], AF.Silu)
                nt_ = max(1, ntok // 128)
                if role in ("q", "k"):
                    k.tt("gpsimd", sq[j], sq[j].ap[:, 0:ntok], a_, a_.ap[:, 0:ntok], a_, a_.ap[:, 0:ntok], ALU.mult)
                    b2 = nbank()
                    k.mm(b2, b2.ap[:, 0:ntok], onesf, onesf.ap, sq[j], sq[j].ap[:, 0:ntok])
                    k.act(sd[j], sd[j].ap[:, 0:ntok], b2, b2.ap[:, 0:ntok], AF.Sqrt, bias=epsb.ap[:, 0:1], sbufs=[epsb])
                    recip("vector", sd[j], sd[j].ap[:, 0:ntok], sd[j], sd[j].ap[:, 0:ntok])
                    scl = (128.0 ** -0.5) if role == "q" else 1.0
                    k.stt(xn[j], xn[j].ap[:, 0:ntok], a_, a_.ap[:, 0:ntok], scl, ALU.mult, sd[j], sd[j].ap[:, 0:ntok], ALU.mult)
                    dstT = (qTb, qTbs) if role == "q" else (kTb, kTbs)
                    if kind == "p":
                        k.dma(dstT[0], dstT[0].ap[hidx][:, b * TBk:(b + 1) * TBk], xn[j], xn[j].ap[:, 0:ntok])
                    else:
                        k.dma(dstT[1], dstT[1].ap[hidx], xn[j], xn[j].ap[:, 0:ntok])
                    src_tm = xn[j] if role == "k" else None
                    dst_tm = (ktm, ktms, hidx)
                else:
                    k.copy("vector", xn[j], xn[j].ap[:, 0:ntok], a_, a_.ap[:, 0:ntok])
                    src_tm = xn[j]
                    dst_tm = (vtm, vtms, hidx)
                if src_tm is not None:
                    b3 = nbank()
                    b3v = b3.ap.bitcast(BF16)
                    if kind == "p":
                        for t in range(nt_):
                            k.tr(b3, b3v[:, t * 128:(t + 1) * 128], src_tm, src_tm.ap[:, t * 128:(t + 1) * 128], identb, identb.ap)
                        k.copy("scalar", tmb[j], tmb[j].ap[:, 0:ntok], b3, b3v[:, 0:ntok])
                        k.dma(dst_tm[0], dst_tm[0].ap[dst_tm[2]][b * TBk:(b + 1) * TBk, :].rearrange("(t p) d -> p t d", p=128),
                              tmb[j], v3(tmb[j].ap, nt_))
                    else:
                        for s in range(SPC):
                            k.tr(b3, b3v[0:DEC_T, s * 128:(s + 1) * 128], src_tm, src_tm.ap[:, s * DEC_T:(s + 1) * DEC_T],
                                 identb, identb.ap)
                        k.copy("scalar", tmb[j], tmb[j].ap[0:DEC_T, 0:SPC * 128], b3, b3v[0:DEC_T, 0:SPC * 128])
                        k.dma(dst_tm[1], dst_tm[1].ap[dst_tm[2]].rearrange("s t d -> t s d"),
                              tmb[j], v3(tmb[j].ap, SPC)[0:DEC_T])
            if kind == "p":
                tl = [(128, lambda kc, t=t: h3[:, kc, t * 128:(t + 1) * 128]) for t in range(4)]
                wba_, dtb_, negA_ = Wbam, dtbm, negAm
            else:
                tl = [(DEC_T, lambda kc, s=s: h3[:, kc, s * DEC_T:(s + 1) * DEC_T]) for s in range(SPC)]
                wba_, dtb_, negA_ = Wba, dtb, negA
            w2 = 2 * nvh
            bank = nbank()
            rows = tl[0][0]
            r = slice(0, rows)
            for ti, (_r, lf) in enumerate(tl):
                for kc in range(8):
                    k.mm(bank, bank.ap[r, ti * w2:(ti + 1) * w2], hbb, lf(kc), wba_, v3(wba_.ap, 8)[:, kc, :],
                         start=(kc == 0), stop=(kc == 7))
            k.copy("scalar", braw, braw.ap[r, 0:4 * w2], bank, bank.ap[r, 0:4 * w2])
            b3_ = v3(braw.ap[:, 0:4 * w2], 4)
            bgb = bgt[bi % 2]
            bg3 = v3(bgb.ap[:, 0:4 * w2], 4)
            k.act(bgb, bg3[r, :, 0:nvh], braw, b3_[r, :, 0:nvh], AF.Sigmoid)
            bcn = lambda t_: t_.ap[r, :].rearrange("p (o e) -> p o e", o=1).to_broadcast([rows, 4, nvh])
            xa3, ab3, sp3 = (v3(t_.ap[:, 0:4 * nvh], 4)[r] for t_ in (xa, ab, sp))
            k.tt("vector", xa, xa3, braw, b3_[r, :, nvh:w2], dtb_, bcn(dtb_), ALU.add)
            k.act(ab, ab3, xa, xa3, AF.Abs)
            k.act(ab, ab3, ab, ab3, AF.Exp, scale=-1.0)
            k.act(ab, ab3, ab, ab3, AF.Ln, bias=one1.ap[r, 0:1], sbufs=[one1])
            k.ts("vector", sp, sp3, xa, xa3, 0.0, ALU.max)
            k.tt("vector", sp, sp3, sp, sp3, ab, ab3, ALU.add)
            k.tt("vector", bgb, bg3[r, :, nvh:w2], sp, sp3, negA_, bcn(negA_), ALU.mult)
            if kind == "p":
                k.dma(bg_scr, bg_scr.ap[b * 4:(b + 1) * 4].rearrange("t p c -> p t c"), bgb, bg3)
            else:
                k.dma(bgs_scr, bgs_scr.ap.rearrange("s p c -> p s c"), bgb, bg3[r])
        k.dma(bconv_p_out, bconv_p_out.ap, carry, carry.ap)
        k.dma(bconv_s_out, bconv_s_out.ap, carry_s, carry_s.ap)

    def b2_run(prompt):
        k.phase()
        nqm, nvm, DS = (2, 4, 4) if prompt else (NQ, NV, 2)
        Um = k.alloc("Um", 128)
        Umb = k.alloc("Umb", 128, BF16)
        Ls = k.alloc("Ls", 128)
        sel = k.alloc("sel", NV * 128)
        k.dma(Um, Um.ap, Umat_in, Umat_in.ap)
        k.copy("vector", Umb, Umb.ap, Um, Um.ap)
        k.dma(Ls, Ls.ap, Lstrict_in, Lstrict_in.ap)
        k.dma(sel, sel.ap[0:NV, :], sel_in, sel_in.ap)
        sel3 = v3(sel.ap, NV)
        S = k.alloc("S", nvm * 128)
        Sb = k.alloc("Sb", nvm * 128, BF16)
        S3, Sb3 = v3(S.ap, nvm), v3(Sb.ap, nvm)
        qT = [k.alloc(f"qTc{i}", nqm * 128, BF16) for i in range(DS)]
        kT = [k.alloc(f"kTc{i}", nqm * 128, BF16) for i in range(DS)]
        kt = [k.alloc(f"ktc{i}", nqm * 128, BF16) for i in range(DS)]
        vt = [k.alloc(f"vtc{i}", nvm * 128, BF16) for i in range(DS)]
        bg = [k.alloc(f"bgc{i}", 32) for i in range(DS)]
        cg_ = [k.alloc(f"cg{i}", 16) for i in range(DS)]
        cgT_ = [k.alloc(f"cgT{i}", 128) for i in range(DS)]
        nbeta_ = [k.alloc(f"nbeta{i}", 16) for i in range(DS)]
        bec_ = [k.alloc(f"bec{i}", 16) for i in range(DS)]
        Gs_ = [k.alloc(f"Gs{i}", nqm * 128) for i in range(DS)]
        QKs_ = [k.alloc(f"QKs{i}", nqm * 128) for i in range(DS)]
        G4 = range(4)
        dec_ = [k.alloc(f"dec{g}", 512) for g in G4]
        decT_ = [k.alloc(f"decT{g}", 512) for g in G4]
        eR_ = [k.alloc(f"eR{g}", 512) for g in G4]
        tG_ = [k.alloc(f"tG{g}", 512) for g in G4]
        X_ = [[k.alloc(f"X{g}_{i}", 512) for i in range(2)] for g in G4]
        Xt_ = [[k.alloc(f"Xt{g}_{i}", 512) for i in range(2)] for g in G4]
        Tt_ = [[k.alloc(f"Tt{g}_{i}", 512) for i in range(2)] for g in G4]
        TtB_ = [k.alloc(f"TtB{g}", 512, BF16) for g in G4]
        PT_ = [k.alloc(f"PT{g}", 512, BF16) for g in G4]
        bv_ = [k.alloc(f"bv{g}", 512, BF16) for g in G4]
        bk_ = [k.alloc(f"bk{g}", 512, BF16) for g in G4]
        kdec_ = [k.alloc(f"kdec{g}", 512, BF16) for g in G4]
        qgT_ = [k.alloc(f"qgT{g}", 512, BF16) for g in G4]
        u0s_ = [k.alloc(f"u0s{g}", 512) for g in G4]
        wkT_ = [k.alloc(f"wkT{g}", 512, BF16) for g in G4]
        ub_ = [k.alloc(f"ub{g}", 512, BF16) for g in G4]
        otm_ = [k.alloc(f"otm{g}", 512) for g in G4]
        v4 = lambda t_, C: v3(t_.ap, 4)[0:C]

        done = [0]

        def chunk(C, levels, ci, loads, o_dst, nq, nv, ws_fn, order=None):
            r = slice(0, C)
            j = ci % DS
            cg, cgT, nbeta, bec, Gs, QKs = cg_[j], cgT_[j], nbeta_[j], bec_[j], Gs_[j], QKs_[j]
            q3, k3, kt3 = (v3(t_.ap[:, 0:nq * 128], nq) for t_ in (qT[j], kT[j], kt[j]))
            vt3 = v3(vt[j].ap[:, 0:nv * 128], nv)
            loads(qT[j], q3, kT[j], k3, kt[j], kt3, vt[j], vt3, bg[j])
            beta, g_ = bg[j].ap[r, 0:nv], bg[j].ap[r, nv:2 * nv]
            b0 = nbank()
            k.mm(b0, b0.ap[r, 0:nv], Um, Um.ap[r, r], bg[j], g_)
            k.copy("vector", cg, cg.ap[r, 0:nv], b0, b0.ap[r, 0:nv])
            b1 = nbank()
            k.tr(b1, b1.ap[0:nv, 0:C], cg, cg.ap[r, 0:nv], identf, identf.ap[r, r])
            k.copy("vector", cgT, cgT.ap[0:nv, 0:C], b1, b1.ap[0:nv, 0:C])
            k.ts("vector", nbeta, nbeta.ap[r, 0:nv], bg[j], beta, -1.0, ALU.mult)
            k.act(bec, bec.ap[r, 0:nv], cg, cg.ap[r, 0:nv], AF.Exp)
            k.tt("vector", bec, bec.ap[r, 0:nv], bec, bec.ap[r, 0:nv], bg[j], beta, ALU.mult)
            Gs3, QK3 = v3(Gs.ap, nqm), v3(QKs.ap, nqm)
            for h0 in range(0, nq, 4):
                hn = min(4, nq - h0)
                bG, bQ = nbank(), nbank()
                for hh in range(hn):
                    hq = h0 + hh
                    k.mm(bG, v3(bG.ap, 4)[r, hh, 0:C], kT[j], k3[:, hq, 0:C], kT[j], k3[:, hq, 0:C])
                    k.mm(bQ, v3(bQ.ap, 4)[r, hh, 0:C], kT[j], k3[:, hq, 0:C], qT[j], q3[:, hq, 0:C])
                k.copy("scalar", Gs, Gs3[r, h0:h0 + hn, 0:C], bG, v3(bG.ap, 4)[r, 0:hn, 0:C])
                k.copy("scalar", QKs, QK3[r, h0:h0 + hn, 0:C], bQ, v3(bQ.ap, 4)[r, 0:hn, 0:C])
            def group(gq, ws, order):
                dec, decT, eR, tG = dec_[ws], decT_[ws], eR_[ws], tG_[ws]
                X, Xt, Tt, TtB = X_[ws], Xt_[ws], Tt_[ws], TtB_[ws]
                PT, bv, bk, kdec, qgT = PT_[ws], bv_[ws], bk_[ws], kdec_[ws], qgT_[ws]
                u0s, wkT, ub = u0s_[ws], wkT_[ws], ub_[ws]
                hvs = [4 * gq + i_ for i_ in range(4)]
                bR = nbank()
                R4 = v3(bR.ap, 4)
                for i_, hv in enumerate(hvs):
                    k.mm(bR, R4[:, i_, 0:C], sel, sel3[0:nv, hv, :], cgT, cgT.ap[0:nv, 0:C])
                dec4, decT4, eR4, tG4 = v4(dec, C), v4(decT, C), v3(eR.ap, 4), v4(tG, C)
                for i_, hv in enumerate(hvs):
                    cgc = cg.ap[r, hv:hv + 1]
                    k.ts("vector", dec, dec4[:, i_, 0:C], bR, R4[r, i_, 0:C], cgc, ALU.subtract, 0.0, ALU.max, sbufs=[cg])
                    k.ts("gpsimd" if False else "vector", decT, decT4[:, i_, 0:C], bR, R4[r, i_, 0:C], cgc, ALU.subtract, 0.0,
                         ALU.min, sbufs=[cg])
                k.act(dec, dec4[:, :, 0:C], dec, dec4[:, :, 0:C], AF.Exp, scale=-1.0)
                k.act(decT, decT4[:, :, 0:C], decT, decT4[:, :, 0:C], AF.Exp)
                k.act(eR, eR4[:, :, 0:C], bR, R4[:, :, 0:C], AF.Exp)
                yield
                X4 = [v4(X[0], C), v4(X[1], C)]
                Xt4 = [v4(Xt[0], C), v4(Xt[1], C)]
                Tt4 = [v4(Tt[0], C), v4(Tt[1], C)]
                for i_, hv in enumerate(hvs):
                    hq = hv // 2
                    k.tt("gpsimd", tG, tG4[:, i_, 0:C], Gs, Gs3[r, hq, 0:C], dec, dec4[:, i_, 0:C], ALU.mult)
                    k.stt(X[0], X4[0][:, i_, 0:C], tG, tG4[:, i_, 0:C], nbeta.ap[r, hv:hv + 1], ALU.mult,
                          Ls, Ls.ap[r, r], ALU.mult, sbufs=[nbeta])
                bT = nbank()
                bTv = v3(bT.ap, 4)
                for i_ in range(4):
                    k.tr(bT, bTv[r, i_, 0:C], X[0], X4[0][:, i_, 0:C], identf, identf.ap[r, r])
                k.copy("scalar", Xt[0], Xt4[0][:, :, 0:C], bT, bTv[r, :, 0:C])
                yield
                idb = identf.ap[r, r].rearrange("p (o e) -> p o e", o=1).to_broadcast([C, 4, C])
                k.tt("vector", Tt[0], Tt4[0][:, :, 0:C], Xt[0], Xt4[0][:, :, 0:C], identf, idb, ALU.add)
                cur = 0
                for lv in range(1, levels + 1):
                    nxt = 1 - cur
                    bX = nbank()
                    for i_ in range(4):
                        k.mm(bX, v3(bX.ap, 4)[r, i_, 0:C], Xt[cur], Xt4[cur][:, i_, 0:C], X[cur], X4[cur][:, i_, 0:C])
                    k.copy("scalar", X[nxt], X4[nxt][:, :, 0:C], bX, v3(bX.ap, 4)[r, :, 0:C])
                    yield
                    if lv < levels:
                        bXt = nbank()
                        for i_ in range(4):
                            k.mm(bXt, v3(bXt.ap, 4)[r, i_, 0:C], X[cur], X4[cur][:, i_, 0:C], Xt[cur], Xt4[cur][:, i_, 0:C])
                        k.copy("gpsimd" if False else "vector", Xt[nxt], Xt4[nxt][:, :, 0:C], bXt, v3(bXt.ap, 4)[r, :, 0:C])
                    bD = nbank()
                    for i_ in range(4):
                        k.mm(bD, v3(bD.ap, 4)[r, i_, 0:C], X[nxt], X4[nxt][:, i_, 0:C], Tt[cur], Tt4[cur][:, i_, 0:C])
                    k.tt("vector", Tt[nxt], Tt4[nxt][:, :, 0:C], bD, v3(bD.ap, 4)[r, :, 0:C], Tt[cur], Tt4[cur][:, :, 0:C], ALU.add)
                    cur = nxt
                    yield
                TtF, TtF4 = TtB, v4(TtB, C)
                k.copy("vector", TtB, TtF4[:, :, 0:C], Tt[cur], Tt4[cur][:, :, 0:C])
                PT4, bv4, bk4, kd4, qg4 = v4(PT, C), v4(bv, C), v4(bk, C), v4(kdec, C), v3(qgT.ap, 4)
                for i_, hv in enumerate(hvs):
                    hq = hv // 2
                    k.tt("gpsimd", tG, tG4[:, i_, 0:C], QKs, QK3[r, hq, 0:C], decT, decT4[:, i_, 0:C], ALU.mult)
                    k.ts("vector", bv, bv4[:, i_, :], vt[j], vt3[r, hv, :], bg[j].ap[r, hv:hv + 1], ALU.mult, sbufs=[bg[j]])
                    k.ts("vector", bk, bk4[:, i_, :], kt[j], kt3[r, hq, :], bec.ap[r, hv:hv + 1], ALU.mult, sbufs=[bec])
                    k.ts("gpsimd", kdec, kd4[:, i_, :], kt[j], kt3[r, hq, :], decT4[:, i_, C - 1:C], ALU.mult, sbufs=[decT])
                    k.tt("gpsimd", qgT, qg4[:, i_, 0:C], qT[j], q3[:, hq, 0:C], eR, eR4[:, i_, 0:C], ALU.mult)
                umb = Umb.ap[r, r].rearrange("p (o e) -> p o e", o=1).to_broadcast([C, 4, C])
                k.tt("vector", PT, PT4[:, :, 0:C], tG, tG4[:, :, 0:C], Umb, umb, ALU.mult)
                yield
                bU = nbank()
                bW = nbank()
                for i_ in range(4):
                    k.mm(bU, v3(bU.ap, 4)[r, i_, :], TtF, TtF4[:, i_, 0:C], bv, bv4[:, i_, :])
                    k.mm(bW, v3(bW.ap, 4)[:, i_, 0:C], bk, bk4[:, i_, :], TtF, TtF4[:, i_, 0:C])
                k.copy("scalar", u0s, v4(u0s, C), bU, v3(bU.ap, 4)[r])
                wk4 = v3(wkT.ap, 4)
                k.copy("scalar", wkT, wk4[:, :, 0:C], bW, v3(bW.ap, 4)[:, :, 0:C])
                yield
                if order is not None:
                    while done[0] < order:
                        yield
                bS = nbank()
                for i_, hv in enumerate(hvs):
                    k.mm(bS, v3(bS.ap, 4)[r, i_, :], wkT, wk4[:, i_, 0:C], Sb, Sb3[:, hv, :])
                ub4 = v4(ub, C)
                k.tt("vector", ub, ub4, u0s, v4(u0s, C), bS, v3(bS.ap, 4)[r], ALU.subtract)
                yield
                bO = nbank()
                for i_, hv in enumerate(hvs):
                    k.mm(bO, v3(bO.ap, 4)[r, i_, :], qgT, qg4[:, i_, 0:C], Sb, Sb3[:, hv, :], start=True, stop=False)
                    k.mm(bO, v3(bO.ap, 4)[r, i_, :], PT, PT4[:, i_, 0:C], ub, ub4[:, i_, :], start=False, stop=True)
                oj = otm_[ws]
                k.copy("scalar", oj, v4(oj, C), bO, v3(bO.ap, 4)[r])
                ob, oap = o_dst(gq)
                k.dma(ob, oap, oj, v4(oj, C))
                yield
                bN = nbank()
                for i_, hv in enumerate(hvs):
                    k.mm(bN, v3(bN.ap, 4)[:, i_, :], kdec, kd4[:, i_, :], ub, ub4[:, i_, :])
                for i_, hv in enumerate(hvs):
                    k.stt(S, S3[:, hv, :], S, S3[:, hv, :], eR4[:, i_, C - 1:C], ALU.mult, bN, v3(bN.ap, 4)[:, i_, :], ALU.add,
                          sbufs=[eR])
                k.copy("scalar", Sb, Sb3[:, 4 * gq:4 * gq + 4, :], S, S3[:, 4 * gq:4 * gq + 4, :])
                if order is not None:
                    done[0] += 1

            return [group(gq, ws_fn(gq), order) for gq in range(nv // 4)]

        def step_all(active):
            for g_ in list(active):
                try:
                    next(g_)
                except StopIteration:
                    active.remove(g_)

        CH = 64
        if prompt:
            k.memset("vector", S, S.ap, 0.0)
            k.memset("vector", Sb, Sb.ap, 0.0)
            active = []
            for n in range(SEQ // CH):
                tok = slice(n * CH, (n + 1) * CH)

                def loads(qb, q3, kb, k3, ktb, kt3, vtb, vt3, bgb, n=n, tok=tok):
                    k.dma(qb, q3[:, :, 0:CH], qTb, qTb.ap[:, :, tok].rearrange("h p t -> p h t"))
                    k.dma(kb, k3[:, :, 0:CH], kTb, kTb.ap[:, :, tok].rearrange("h p t -> p h t"))
                    k.dma(ktb, kt3[0:CH], ktm, ktm.ap[:, tok, :].rearrange("h t d -> t h d"))
                    k.dma(vtb, vt3[0:CH], vtm, vtm.ap[:, tok, :].rearrange("h t d -> t h d"))
                    r0 = (n % 2) * CH
                    k.dma(bgb, bgb.ap[0:CH, 0:8], bg_scr, bg_scr.ap[n // 2][r0:r0 + CH, :])

                pc_, t0_ = (n * CH) // 512, (n * CH) % 512
                while len(active) >= 4:
                    step_all(active)
                active += chunk(CH, 5, n, loads,
                                lambda gq, pc_=pc_, t0_=t0_: (otm_m[pc_], otm_m[pc_].ap[t0_:t0_ + CH, :].rearrange("t (h d) -> t h d", h=4)),
                                2, 4, lambda gq, n=n: n % 4, order=n)
                step_all(active)
                if t0_ + CH == 512:
                    while active:
                        step_all(active)
                    k.allgather(otm_g[pc_], otm_g[pc_].ap, otm_m[pc_], otm_m[pc_].ap, CC_GROUPS)
            k.dma(ssm_p_out, ssm_p_out.ap.rearrange("h k v -> k h v"), S, S3[:, 0:4, :])
        else:
            for s in range(SPC):
                k.dma(S, S3, state_b_ssm, state_b_ssm.ap[s].rearrange("h k v -> k h v"))
                k.copy("scalar", Sb, Sb.ap, S, S.ap)

                def loads(qb, q3, kb, k3, ktb, kt3, vtb, vt3, bgb, s=s):
                    c8 = slice(s * DEC_T, (s + 1) * DEC_T)
                    k.dma(qb, q3[:, :, 0:DEC_T], qTbs, qTbs.ap[:, :, c8].rearrange("h p t -> p h t"))
                    k.dma(kb, k3[:, :, 0:DEC_T], kTbs, kTbs.ap[:, :, c8].rearrange("h p t -> p h t"))
                    k.dma(ktb, kt3[0:DEC_T], ktms, ktms.ap[:, s].rearrange("h t d -> t h d"))
                    k.dma(vtb, vt3[0:DEC_T], vtms, vtms.ap[:, s].rearrange("h t d -> t h d"))
                    k.dma(bgb, bgb.ap[0:DEC_T, :], bgs_scr, bgs_scr.ap[s])

                gens = chunk(DEC_T, 2, s, loads,
                             lambda gq, s=s: (otms_scr, otms_scr.ap[s][:, gq * 512:(gq + 1) * 512].rearrange("t (h d) -> t h d", h=4)),
                             NQ, NV, lambda gq: gq)
                while gens:
                    step_all(gens)
                k.dma(ssm_s_out, ssm_s_out.ap[s].rearrange("h k v -> k h v"), S, S3)

    def phase_b3(xin_p, xin_s, xout_p, xout_s):
        k.phase()
        Wz = k.alloc("Wz", 8 * 2048, BF16)
        Wz3 = v3(Wz.ap, 8)
        bw = b_w_in.ap.rearrange("(k p) n -> p k n", p=128)
        for c0 in range(0, 2048, 1024):
            k.dma(Wz, Wz3[:, :, c0:c0 + 1024], b_w_in, bw[:, :, B_CONV_DIM + c0:B_CONV_DIM + c0 + 1024], eng="gpsimd")
        Wo = k.alloc("Wo", NV * D, BF16)
        Wo3 = v3(Wo.ap, NV)
        k.dma(Wo, Wo3, b_w_out, b_w_out.ap.rearrange("(c p) n -> p c n", p=128), eng="gpsimd")
        gn = k.alloc("gn", 128)
        k.dma(gn, gn.ap, b_norm_g, b_norm_g.ap.partition_broadcast(128))
        G = k.alloc("G", D)
        k.dma(G, G.ap, mod_scr, mod_scr.ap[1][0:1, 2 * D:3 * D].partition_broadcast(128))
        Gs = [k.alloc(f"Gs{i}", D) for i in range(2)]
        ht = [k.alloc(f"ht{i}", 8 * 128, BF16) for i in range(2)]
        ot = [k.alloc(f"ot{i}", 2048) for i in range(2)]
        sqo = k.alloc("sqo", 2048)
        ssq = k.alloc("ssq", 16)
        sdv = k.alloc("sdv", 16)
        sz = k.alloc("sz", 2048)
        of = [k.alloc(f"of{i}", 2048, BF16) for i in range(2)]
        oT = [k.alloc(f"oT{i}", NV * 128, BF16) for i in range(2)]
        xt = [k.alloc(f"xt{i}", D) for i in range(2)]
        yt = [k.alloc(f"yt{i}", D) for i in range(2)]
        tiles = [(128, i, None) for i in range(NT)] + [(DEC_T, None, s) for s in range(SPC)]
        for it, (rows, ti, si) in enumerate(tiles):
            j = it % 2
            r = slice(0, rows)
            h3 = v3(ht[j].ap, 8)
            if si is None:
                tok = slice(ti * 128, (ti + 1) * 128)
                k.dma(ht[j], h3, hTp, hTp.ap[:, :, tok])
                og_ = otm_g[ti // 4]
                r0_ = (ti % 4) * 128
                k.dma(ot[j], v3(ot[j].ap, 4), og_, og_.ap.rearrange("(r t) c -> t r c", r=4)[r0_:r0_ + 128])
                k.dma(xt[j], xt[j].ap, xin_p, xin_p.ap[tok, :])
                Gb = G
            else:
                k.dma(ht[j], h3[:, :, 0:rows], hTs, hTs.ap[:, :, si * DEC_T:(si + 1) * DEC_T])
                k.dma(ot[j], ot[j].ap[r, :], otms_scr, otms_scr.ap[si])
                k.dma(xt[j], xt[j].ap[r, :], xin_s, xin_s.ap[si])
                Gb = Gs[si % 2]
                k.dma(Gb, Gb.ap[r, :], mod_scr, mod_scr.ap[1][1 + si:2 + si, 2 * D:3 * D].partition_broadcast(rows))
            for q in range(4):
                bank = nbank()
                for kc in range(8):
                    k.mm(bank, bank.ap[r, :], ht[j], h3[:, kc, 0:rows], Wz, Wz3[:, kc, q * 512:(q + 1) * 512],
                         start=(kc == 0), stop=(kc == 7))
                k.act(sz, sz.ap[r, q * 512:(q + 1) * 512], bank, bank.ap[r, :], AF.Silu)
            o3 = v3(ot[j].ap, NV)[r]
            k.tt("gpsimd", sqo, sqo.ap[r, :], ot[j], ot[j].ap[r, :], ot[j], ot[j].ap[r, :], ALU.mult)
            k.op("vector", (lambda o_, i_: (lambda e: e.tensor_reduce(out=o_, in_=i_, op=ALU.add, axis=AX.X)))(
                ssq.ap[r, :], v3(sqo.ap, NV)[r]), reads=[sqo], writes=[ssq])
            k.act(sdv, sdv.ap[r, :], ssq, ssq.ap[r, :], AF.Sqrt, scale=1.0 / 128, bias=epsb.ap[r, :], sbufs=[epsb])
            recip("vector", sdv, sdv.ap[r, :], sdv, sdv.ap[r, :])
            rb = sdv.ap[r, :].rearrange("p (h o) -> p h o", o=1).to_broadcast([rows, NV, 128])
            gb_ = gn.ap[r, :].rearrange("p (o e) -> p o e", o=1).to_broadcast([rows, NV, 128])
            k.tt("vector", sqo, v3(sqo.ap, NV)[r], ot[j], o3, sdv, rb, ALU.mult)
            k.tt("gpsimd", sqo, v3(sqo.ap, NV)[r], sqo, v3(sqo.ap, NV)[r], gn, gb_, ALU.mult)
            k.tt("vector", of[j], of[j].ap[r, :], sqo, sqo.ap[r, :], sz, sz.ap[r, :], ALU.mult)
            oT3 = v3(oT[j].ap, NV)
            for half in range(2):
                bank = nbank()
                bv_ = v3(bank.ap.bitcast(BF16), 8)
                for c in range(8):
                    cc = half * 8 + c
                    k.tr(bank, bv_[:, c, 0:rows], of[j], of[j].ap[r, cc * 128:(cc + 1) * 128], identb, identb.ap[r, r])
                k.copy("scalar", oT[j], oT3[:, half * 8:half * 8 + 8, 0:rows], bank, bv_[:, :, 0:rows])
            for half in range(2):
                bank = nbank()
                for c in range(NV):
                    k.mm(bank, bank.ap[r, :], oT[j], oT3[:, c, 0:rows], Wo, Wo3[:, c, half * 512:(half + 1) * 512],
                         start=(c == 0), stop=(c == NV - 1))
                hs = slice(half * 512, (half + 1) * 512)
                k.tt("vector", yt[j], yt[j].ap[r, hs], bank, bank.ap[r, :], Gb, Gb.ap[r, hs], ALU.mult)
                k.tt("gpsimd", yt[j], yt[j].ap[r, hs], yt[j], yt[j].ap[r, hs], xt[j], xt[j].ap[r, hs], ALU.add)
            if si is None:
                k.dma(xout_p, xout_p.ap[tok, :], yt[j], yt[j].ap)
            else:
                k.dma(xout_s, xout_s.ap[si], yt[j], yt[j].ap[r, :])

    def phase_final(xin_p, xin_s):
        k.phase()
        gf = k.alloc("gf", D)
        k.dma(gf, gf.ap, norm_final_g, norm_final_g.ap.partition_broadcast(128))
        xt = [k.alloc(f"xt{i}", D) for i in range(4)]
        sq = k.alloc("sq", D)
        ssq = [k.alloc(f"ssq{i}", 1) for i in range(4)]
        yt = [k.alloc(f"yt{i}", D) for i in range(4)]
        tiles = [(128, i, None) for i in range(NT)] + [(DEC_T, None, s) for s in range(SPC)]
        for it, (rows, ti, si) in enumerate(tiles):
            j = it % 4
            r = slice(0, rows)
            if si is None:
                k.dma(xt[j], xt[j].ap, xin_p, xin_p.ap[ti * 128:(ti + 1) * 128, :])
            else:
                k.dma(xt[j], xt[j].ap[r, :], xin_s, xin_s.ap[si])
            k.tt("gpsimd", sq, sq.ap[r, :], xt[j], xt[j].ap[r, :], xt[j], xt[j].ap[r, :], ALU.mult)
            k.op("vector", (lambda o_, i_: (lambda e: e.tensor_reduce(out=o_, in_=i_, op=ALU.add, axis=AX.X)))(
                ssq[j].ap[r, :], sq.ap[r, :]), reads=[sq], writes=[ssq[j]])
            k.act(ssq[j], ssq[j].ap[r, :], ssq[j], ssq[j].ap[r, :], AF.Sqrt, scale=1.0 / D, bias=epsb.ap[r, :], sbufs=[epsb])
            recip("vector", ssq[j], ssq[j].ap[r, :], ssq[j], ssq[j].ap[r, :])
            k.stt(yt[j], yt[j].ap[r, :], xt[j], xt[j].ap[r, :], ssq[j].ap[r, 0:1], ALU.mult, gf, gf.ap[r, :], ALU.mult,
                  sbufs=[ssq[j]])
            if si is None:
                k.dma(y_p_out, y_p_out.ap[ti * 128:(ti + 1) * 128, :], yt[j], yt[j].ap)
            else:
                k.dma(y_s_out, y_s_out.ap[si], yt[j], yt[j].ap[r, :])

    import os
    nph = int(os.environ.get("KSTAGE", "99"))
    for i_, ph_ in enumerate([phase_attn, phase_attn_sample, phase_combine, phase_out_a,
                                lambda: phase_norm(0, x1p, x1s, 4 * D, 3 * D),
                                lambda: phase_ffn(0, x1p, x1s, x2p, x2s),
                                lambda: phase_norm(1, x2p, x2s, 1 * D, 0),
                                phase_b1, lambda: b2_run(True), lambda: b2_run(False),
                                lambda: phase_b3(x2p, x2s, x3p, x3s),
                                lambda: phase_norm(1, x3p, x3s, 4 * D, 3 * D),
                                lambda: phase_ffn(1, x3p, x3s, x4p, x4s),
                                lambda: phase_final(x4p, x4s)]):
        if i_ < nph:
            ph_()

    with stack:
        k.emit()
    return nc


_NC = {}


def _rope_tables():
    half = DH // 2
    inv = THETA ** (-np.arange(half, dtype=np.float32) / half)
    tabs = np.zeros((3, NT, 128, 64), np.float32)
    for g, (_w, d) in enumerate(A_GROUPS):
        nb = group_blocks(d)
        for sb in range(4):
            for r_ in range(d):
                for b_ in range(nb):
                    n = sb * 16 + r_ * nb + b_
                    pos = (sb * 2048 + d * (b_ * 128 + np.arange(128)) + r_).astype(np.float32)
                    ang = pos[:, None] * inv[None, :]
                    tabs[g, n, :, :32] = np.cos(ang)
                    tabs[g, n, :, 32:] = np.sin(ang)
    pos = (PAST + np.arange(DEC_T)).astype(np.float32)
    ang = pos[:, None] * inv[None, :]
    ts = np.concatenate([np.cos(ang), np.sin(ang)], axis=1).astype(np.float32)
    return tabs, ts


def _masks():
    u = np.arange(128)[:, None]
    v = np.arange(128)[None, :]
    own = (u <= v).astype(np.float32)
    prev = (u >= v).astype(np.float32)
    mask4 = np.concatenate([own, prev, own, prev], axis=1)
    E = np.zeros((65, 64), np.float32)
    E[64, :] = 1.0
    ms = np.zeros((3, 8, 128, 8), np.float32)
    mn = np.zeros((3, 8, 8), np.float32)
    for g, (_w, d) in enumerate(A_GROUPS):
        for t0 in range(min(d, DEC_T)):
            for t in range(t0, DEC_T, d):
                a = (t - t0) // d
                ms[g, t0, a:, t] = 1.0
        for tk in range(DEC_T):
            for t in range(tk, DEC_T):
                if (t - tk) % d == 0:
                    mn[g, tk, t] = 1.0
    return dict(mask4=mask4, Emat=E, msk_s=ms, msk_n=mn)


def kernel(**inp):
    f32 = np.float32
    stage = 1
    if stage not in _NC:
        _NC[stage] = build_program(stage)
    nc = _NC[stage]
    g = lambda n: np.asarray(inp[n], dtype=f32)
    rope_p, rope_s = _rope_tables()
    shared = dict(
        w_mod=g("w_mod"), b_mod=g("b_mod"), norm_mix_g=g("norm_mix_g"), norm_ffn_g=g("norm_ffn_g"),
        a_w_in=g("a_w_in")[0], identf=np.eye(128, dtype=f32), rope_p=rope_p, rope_s=rope_s,
        a_w_out=g("a_w_out")[0], **_masks(),
        ffn_w_up=g("ffn_w_up"), ffn_w_down=g("ffn_w_down"),
    )
    NC_ = D_FF // 128
    cwb = np.concatenate([g("ffn_conv_w"), g("ffn_conv_b")[:, None, :]], axis=1)
    shared["ffn_cw"] = np.ascontiguousarray(cwb.reshape(2, 4, NC_, 128).transpose(0, 3, 2, 1))
    sfc = g("state_ffn_conv")
    u_ = np.arange(128)
    shared.update(
        b_w_in=g("b_w_in")[0], b_w_out=g("b_w_out")[0],
        b_cw=np.ascontiguousarray(g("b_conv_w")[0].reshape(4, 32, 128).transpose(2, 1, 0)),
        b_dt_bias=g("b_dt_bias").reshape(1, 16), b_a_log=g("b_a_log").reshape(1, 16),
        b_norm_g=g("b_norm_g").reshape(1, 128), norm_final_g=g("norm_final_g").reshape(1, D),
        Umat=(u_[:, None] <= u_[None, :]).astype(f32), Lstrict=(u_[None, :] < u_[:, None]).astype(f32),
        selm=np.ascontiguousarray(np.repeat(np.eye(16, dtype=f32)[:, :, None], 128, axis=2).reshape(16, 2048)),
    )
    sbc = g("state_b_conv")[0]
    ssm0 = g("state_b_ssm")[0]
    xprompt, xsample, cp, cs_ = g("x_prompt"), g("x_sample"), g("c_prompt"), g("c_sample")
    in_maps = []
    for c in range(NCORES):
        m = dict(shared)
        sl = slice(c * SPC, (c + 1) * SPC)
        seq_, rk = c // 4, c % 4
        m["xp"] = xprompt[seq_]
        m["xs"] = np.ascontiguousarray(xsample[sl])
        cpc = cp[seq_]
        awi = shared["a_w_in"].reshape(D, 3, 3, NH, DH)[:, :, :, rk * HM:(rk + 1) * HM, :]
        m["a_w_in_m"] = np.ascontiguousarray(awi.reshape(D, 9 * HM * DH))
        bwi = shared["b_w_in"]
        qh_, vh_ = slice(2 * rk * 128, (2 * rk + 2) * 128), slice(4 * rk * 128, (4 * rk + 4) * 128)
        m["b_w_in_m"] = np.ascontiguousarray(np.concatenate(
            [bwi[:, 0:1024][:, qh_], bwi[:, 1024:2048][:, qh_], bwi[:, 2048:4096][:, vh_],
             bwi[:, 6144 + 4 * rk:6144 + 4 * rk + 4], bwi[:, 6160 + 4 * rk:6160 + 4 * rk + 4]], axis=1))
        mych = [2 * rk, 2 * rk + 1, 8 + 2 * rk, 8 + 2 * rk + 1] + [16 + 4 * rk + i_ for i_ in range(4)]
        m["b_cw_m"] = np.ascontiguousarray(shared["b_cw"][:, mych, :])
        m["b_dt_bias_m"] = np.ascontiguousarray(shared["b_dt_bias"][:, 4 * rk:4 * rk + 4])
        m["b_a_log_m"] = np.ascontiguousarray(shared["b_a_log"][:, 4 * rk:4 * rk + 4])
        m["cT"] = np.ascontiguousarray(np.concatenate([cpc[None, :], cs_[sl]], axis=0).T)
        m["fconvT"] = np.ascontiguousarray(sfc[:, sl].reshape(2, SPC, 2, NC_, 128).transpose(0, 4, 3, 1, 2))
        m["bconvT"] = np.ascontiguousarray(sbc[sl].reshape(SPC, 3, 32, 128).transpose(3, 2, 0, 1))
        m["state_b_ssm"] = np.ascontiguousarray(ssm0[sl])
        for (w, _d) in A_GROUPS:
            for kv in ("k", "v"):
                a = g(f"cache_a_{kv}_w{w}")[0, sl]
                m[f"cache_{kv}_w{w}"] = np.ascontiguousarray(a.reshape(SPC, w, D))
        in_maps.append(m)
    res = run_bass_kernel_spmd(nc, in_maps, core_ids=list(range(NCORES))).results
    if DEBUG:
        global _DBG
        _DBG = res

    def cat_s(name, tail):
        return np.concatenate([res[c][name] for c in range(NCORES)], axis=0).reshape((1, DEC_B) + tail)

    PC = (0, 4)

    def cat_p(name, tail):
        per_seq = [np.concatenate([res[4 * b + r_][name].reshape(tail[0], HM, DH) for r_ in range(4)], axis=1) for b in range(2)]
        return np.stack(per_seq, axis=0).reshape((1, 2) + tail)

    o = {}
    for (w, _d) in A_GROUPS:
        for kv in ("k", "v"):
            o[f"{kv}{w}s"] = cat_s(f"new_{kv}_w{w}_s", (w, NH, DH))
            o[f"{kv}{w}p"] = cat_p(f"new_{kv}_w{w}_p", (w, NH, DH))
    z = lambda *s: np.zeros(s, f32)
    NC_ = D_FF // 128
    fcp = np.stack([res[c]["fconv_p"].reshape(2, 128, NC_, 2) for c in PC], axis=1)
    fcp = np.ascontiguousarray(fcp.transpose(0, 1, 4, 3, 2)).reshape(2, 2, 2, D_FF)
    fcs = np.stack([res[c]["fconv_s"].reshape(2, 128, NC_, SPC, 2) for c in range(NCORES)], axis=1)
    fcs = np.ascontiguousarray(fcs.transpose(0, 1, 4, 5, 3, 2)).reshape(2, DEC_B, 2, D_FF)
    y_p = np.stack([res[c]["y_p"] for c in PC], axis=0)
    y_s = np.concatenate([res[c]["y_s"] for c in range(NCORES)], axis=0)
    ssm_p = np.stack([np.concatenate([res[4 * b + r_]["ssm_p"] for r_ in range(4)], axis=0) for b in range(2)], axis=0)[None]
    ssm_s = np.concatenate([res[c]["ssm_s"] for c in range(NCORES)], axis=0)[None]
    bcp = np.zeros((2, 128, 32, 3), f32)
    for b in range(2):
        for r_ in range(4):
            mych = [2 * r_, 2 * r_ + 1, 8 + 2 * r_, 8 + 2 * r_ + 1] + [16 + 4 * r_ + i_ for i_ in range(4)]
            bcp[b][:, mych, :] = res[4 * b + r_]["bconv_p"].reshape(128, 8, 3)
    bcp = np.ascontiguousarray(bcp.transpose(0, 3, 2, 1)).reshape(1, 2, 3, B_CONV_DIM)
    bcs = np.stack([res[c]["bconv_s"].reshape(128, 32, SPC, 3) for c in range(NCORES)], axis=0)
    bcs = np.ascontiguousarray(bcs.transpose(0, 3, 4, 2, 1)).reshape(1, DEC_B, 3, B_CONV_DIM)
    return (
        y_p, y_s,
        o["k128p"], o["k128s"], o["v128p"], o["v128s"],
        o["k512p"], o["k512s"], o["v512p"], o["v512s"],
        o["k2048p"], o["k2048s"], o["v2048p"], o["v2048s"],
        ssm_p, ssm_s,
        bcp, bcs,
        fcp, fcs,
    )
```

```python
from contextlib import ExitStack

import numpy as np
import concourse.bass as bass
import concourse.mybir as mybir
from concourse.bass_utils import run_bass_kernel_spmd

F32 = mybir.dt.float32
BF16 = mybir.dt.bfloat16
ALU = mybir.AluOpType
AF = mybir.ActivationFunctionType
AX = mybir.AxisListType

NCORES = 8
D = 1024
SEQ = 8192
NT = SEQ // 128
DEC_B = 32
DEC_T = 8
SPC = DEC_B // NCORES
PAST = 16384
A_GROUPS = ((128, 1), (512, 4), (2048, 16))
NH = 16
DH = 64
D_FF = 2816
B_CONV_DIM = 4096
EPS = 1e-6
THETA = 10000.0

HM = 4
CC_GROUPS = [[0, 1, 2, 3], [4, 5, 6, 7]]
SEM_BLOCK = 8000
N_DMA_SEMS = 20
ARENA_F32 = 48 * 1024
DEBUG = False


class Buf:
    def __init__(self, ap, name=""):
        self.ap = ap
        self.name = name
        self.last_write = None
        self.reads = []
        self.sb = False


class K:
    ENGS = ("tensor", "vector", "scalar", "gpsimd", "sync")

    def __init__(self, nc, stack):
        self.nc = nc
        self.stack = stack
        self.ops = []
        self.arena = stack.enter_context(nc.sbuf_tensor("arena", [128, ARENA_F32], F32))
        self.psum_t = stack.enter_context(nc.psum_tensor("psum", [128, 8, 512], F32))
        self.banks = [Buf(self.psum_t[:, i, :], f"bank{i}") for i in range(8)]
        self.top = 0
        self.persist = 0

    def alloc(self, name, cols, dt=F32):
        words = cols if dt == F32 else (cols + 1) // 2
        a = self.top
        self.top += words
        assert self.top <= ARENA_F32, (name, self.top)
        ap = self.arena[:, a:a + words]
        if dt != F32:
            ap = ap.bitcast(dt)[:, 0:cols]
        b = Buf(ap, name)
        b.sb = True
        return b

    def keep(self):
        self.persist = self.top

    def phase(self):
        self.ops.append(dict(barrier=True))
        self.top = self.persist

    def dram(self, name, shape, dt=F32, kind="Internal", **kw):
        return Buf(self.nc.dram_tensor(name, list(shape), dt, kind=kind, **kw).ap(), name)

    def op(self, eng, fn, reads=(), writes=(), dma=False):
        self.ops.append(dict(eng=eng, fn=fn, reads=list(reads), writes=list(writes), dma=dma))

    def dma(self, out_b, out_ap, in_b, in_ap, eng="sync"):
        if eng == "sync" and out_b.sb and not in_b.sb:
            eng = "scalar"
        self.op(eng, lambda e: e.dma_start(out=out_ap, in_=in_ap), reads=[in_b], writes=[out_b], dma=True)

    def mm(self, out_b, out_ap, lb, lap, rb, rap, start=True, stop=True):
        self.op("tensor", lambda e: e.matmul(out_ap, lhsT=lap, rhs=rap, start=start, stop=stop),
                reads=[lb, rb], writes=[out_b])

    def tr(self, out_b, out_ap, in_b, in_ap, ident_b, ident_ap):
        self.op("tensor", lambda e: e.transpose(out=out_ap, in_=in_ap, identity=ident_ap),
                reads=[in_b, ident_b], writes=[out_b])

    def tt(self, eng, out_b, out_ap, a_b, a_ap, b_b, b_ap, op):
        self.op(eng, lambda e: e.tensor_tensor(out=out_ap, in0=a_ap, in1=b_ap, op=op),
                reads=[a_b, b_b], writes=[out_b])

    def ts(self, eng, out_b, out_ap, a_b, a_ap, s1, op0, s2=None, op1=None, sbufs=()):
        kw = dict(out=out_ap, in0=a_ap, scalar1=s1, scalar2=s2, op0=op0)
        if op1 is not None:
            kw["op1"] = op1
        self.op(eng, lambda e: e.tensor_scalar(**kw), reads=[a_b, *sbufs], writes=[out_b])

    def stt(self, out_b, out_ap, a_b, a_ap, scalar, op0, b_b, b_ap, op1, sbufs=()):
        self.op("vector", lambda e: e.scalar_tensor_tensor(out=out_ap, in0=a_ap, scalar=scalar, op0=op0,
                                                           in1=b_ap, op1=op1),
                reads=[a_b, b_b, *sbufs], writes=[out_b])

    def act(self, out_b, out_ap, in_b, in_ap, func, scale=1.0, bias=None, sbufs=()):
        kw = dict(out=out_ap, in_=in_ap, func=func, scale=scale)
        if bias is not None:
            kw["bias"] = bias
        self.op("scalar", lambda e: e.activation(**kw), reads=[in_b, *sbufs], writes=[out_b])

    def copy(self, eng, out_b, out_ap, in_b, in_ap):
        if eng == "scalar":
            self.act(out_b, out_ap, in_b, in_ap, AF.Copy)
        else:
            self.op(eng, lambda e: e.tensor_copy(out=out_ap, in_=in_ap), reads=[in_b], writes=[out_b])

    def allgather(self, out_b, out_ap, in_b, in_ap, groups):
        self.op("gpsimd", lambda e: e.collective_compute("AllGather", ALU.bypass, replica_groups=groups,
                                                         ins=[in_ap], outs=[out_ap]),
                reads=[in_b], writes=[out_b], dma=True)
        self.ops[-1]["cc"] = True

    def memset(self, eng, out_b, out_ap, val):
        self.op(eng, lambda e: e.memset(out_ap, val), writes=[out_b])

    def emit(self):
        nc = self.nc
        ops = []
        pending = {}
        last_eng = {}
        last_dma = {}
        dma_rr = {e: 0 for e in self.ENGS}
        for o in self.ops:
            if o.get("barrier"):
                b = set(last_eng.values()) | set(last_dma.values())
                for e in self.ENGS:
                    pending[e] = set(pending.get(e, set())) | b
                continue
            i = len(ops)
            ops.append(o)
            deps = set()
            for bf in o["reads"]:
                if bf.last_write is not None:
                    deps.add(bf.last_write)
            for bf in o["writes"]:
                if bf.last_write is not None:
                    deps.add(bf.last_write)
                deps.update(bf.reads)
            deps |= pending.pop(o["eng"], set())
            deps.discard(i)
            o["deps"] = deps
            for bf in o["reads"]:
                bf.reads.append(i)
            for bf in o["writes"]:
                bf.last_write = i
                bf.reads = []
            e = o["eng"]
            if o.get("cc"):
                o["slot"] = f"cc{i}"
                last_dma[(e, o["slot"])] = i
            elif o["dma"]:
                slot = dma_rr[e] % N_DMA_SEMS
                dma_rr[e] += 1
                o["slot"] = slot
                last_dma[(e, slot)] = i
            else:
                last_eng[e] = i
        n_eng = {e: 0 for e in self.ENGS}
        sems = {}
        dma_count = {}
        prev_on_sem = {}
        for o in ops:
            e = o["eng"]
            if o.get("cc"):
                o["sig"] = (f"d_{e}_{o['slot']}", 1, None)
                o["prev_same_sem"] = None
            elif o["dma"]:
                s = f"d_{e}_{o['slot']}"
                dma_count[s] = dma_count.get(s, 0) + 1
                o["sig"] = (s, 16 * dma_count[s], 16)
                o["prev_same_sem"] = prev_on_sem.get(s)
                prev_on_sem[s] = o
            else:
                kk = n_eng[e]
                n_eng[e] += 1
                o["sig"] = (f"c_{e}_{kk // SEM_BLOCK}", kk % SEM_BLOCK + 1, 1)
                o["prev_same_sem"] = None
            if o["sig"][0] not in sems:
                sems[o["sig"][0]] = self.stack.enter_context(nc.semaphore(o["sig"][0]))
        by_eng = {e: [o for o in ops if o["eng"] == e] for e in self.ENGS}
        final_waits = {}
        for o in ops:
            if o["dma"]:
                s, v, _ = o["sig"]
                final_waits[s] = max(final_waits.get(s, 0), v)
        print(f"[kernel] ops: " + ", ".join(f"{e}={len(by_eng[e])}" for e in self.ENGS) + f" sems={len(sems)}")

        def emit_engine(ename, eh):
            w = {}
            for o in by_eng[ename]:
                need = {}
                for d in o["deps"]:
                    od = ops[d]
                    if od["eng"] == ename and not od["dma"] and ename == "tensor":
                        continue
                    s, v, _ = od["sig"]
                    need[s] = max(need.get(s, 0), v)
                p = o["prev_same_sem"]
                if p is not None:
                    s, v, _ = p["sig"]
                    need[s] = max(need.get(s, 0), v)
                for s, v in need.items():
                    if w.get(s, 0) < v:
                        eh.wait_ge(sems[s], v)
                        w[s] = v
                ins = o["fn"](eh)
                s, v, inc = o["sig"]
                if inc is None:
                    ins.then_inc(sems[s])
                else:
                    ins.then_inc(sems[s], inc)
            if ename == "sync":
                for s, v in final_waits.items():
                    if w.get(s, 0) < v:
                        eh.wait_ge(sems[s], v)
                for en in self.ENGS:
                    if en == "sync" or not by_eng[en]:
                        continue
                    last = [o for o in by_eng[en] if not o["dma"]]
                    if last:
                        s, v, _ = last[-1]["sig"]
                        eh.wait_ge(sems[s], v)

        with nc.Block() as block:
            @block.tensor
            def _(e):
                emit_engine("tensor", e)

            @block.vector
            def _(e):
                emit_engine("vector", e)

            @block.scalar
            def _(e):
                emit_engine("scalar", e)

            @block.gpsimd
            def _(e):
                emit_engine("gpsimd", e)

            @block.sync
            def _(e):
                emit_engine("sync", e)


def v3(ap, a):
    return ap.rearrange("p (a b) -> p a b", a=a)


def group_blocks(d):
    return 16 // d


def build_program(stage):
    nc = bass.Bass("TRN2", target_bir_lowering=False)
    stack = ExitStack()
    k = K(nc, stack)
    B = k.banks

    def din(name, shape, dt=F32):
        return Buf(nc.dram_tensor(name, list(shape), dt, kind="ExternalInput").ap(), name)

    def dout(name, shape, dt=F32):
        return Buf(nc.dram_tensor(name, list(shape), dt, kind="ExternalOutput").ap(), name)

    xp = din("xp", [SEQ, D])
    xs = din("xs", [SPC, DEC_T, D])
    cT = din("cT", [D, 1 + SPC])
    w_mod = din("w_mod", [2, D, 6 * D])
    b_mod = din("b_mod", [2, 6 * D])
    norm_mix_g = din("norm_mix_g", [2, D])
    norm_ffn_g = din("norm_ffn_g", [2, D])
    a_w_in = din("a_w_in", [D, 9 * D])
    a_w_in_m = din("a_w_in_m", [D, 9 * HM * DH])
    identf_in = din("identf", [128, 128])
    rope_p = din("rope_p", [3, NT, 128, 64])
    rope_s = din("rope_s", [DEC_T, 64])
    cache = {}
    new_s = {}
    new_p = {}
    for (w, _d) in A_GROUPS:
        for kv in ("k", "v"):
            cache[(kv, w)] = din(f"cache_{kv}_w{w}", [SPC, w, D])
            new_s[(kv, w)] = dout(f"new_{kv}_w{w}_s", [SPC, w, D])
            new_p[(kv, w)] = dout(f"new_{kv}_w{w}_p", [w, HM * DH])

    mask4_in = din("mask4", [128, 512])
    E_in = din("Emat", [65, 64])
    msk_s_in = din("msk_s", [3, 8, 128, 8])
    msk_n_in = din("msk_n", [3, 8, 8])
    a_w_out = din("a_w_out", [D, D])
    b_w_in = din("b_w_in", [D, 6176])
    b_w_in_m = din("b_w_in_m", [D, 1032])
    b_cw_m = din("b_cw_m", [128, 8, 4])
    b_dt_bias_m = din("b_dt_bias_m", [1, 4])
    b_a_log_m = din("b_a_log_m", [1, 4])
    b_w_out = din("b_w_out", [2048, D])
    b_cw = din("b_cw", [128, 32, 4])
    bconvT = din("bconvT", [128, 32, SPC, 3])
    b_dt_bias = din("b_dt_bias", [1, 16])
    b_a_log = din("b_a_log", [1, 16])
    b_norm_g = din("b_norm_g", [1, 128])
    norm_final_g = din("norm_final_g", [1, D])
    Umat_in = din("Umat", [128, 128])
    Lstrict_in = din("Lstrict", [128, 128])
    sel_in = din("selm", [16, 16 * 128])
    state_b_ssm = din("state_b_ssm", [SPC, 16, 128, 128])
    bconv_p_out = dout("bconv_p", [128, 8 * 3])
    bconv_s_out = dout("bconv_s", [128, 32 * SPC * 3])
    ssm_p_out = dout("ssm_p", [4, 128, 128])
    ssm_s_out = dout("ssm_s", [SPC, 16, 128, 128])
    y_p_out = dout("y_p", [SEQ, D])
    y_s_out = dout("y_s", [SPC, DEC_T, D])
    ffn_w_up = din("ffn_w_up", [2, D, 2 * D_FF])
    ffn_w_down = din("ffn_w_down", [2, D_FF, D])
    ffn_cw = din("ffn_cw", [2, 128, D_FF // 128, 4])
    fconvT = din("fconvT", [2, 128, D_FF // 128, SPC, 2])
    fconv_p_out = dout("fconv_p", [2, 128, (D_FF // 128) * 2])
    fconv_s_out = dout("fconv_s", [2, 128, (D_FF // 128) * SPC * 2])

    dbg_kind = "ExternalOutput" if DEBUG else "Internal"
    o_scr = k.dram("o_scr", [3, HM, 65, SEQ])
    oT_mine = [k.dram(f"oT_mine{i}", [HM * 64, 2048], BF16) for i in range(4)]
    oT_scr = [k.dram(f"oT_scr{i}", [NH * 64, 2048], BF16, addr_space="Local") for i in range(4)]
    qTs_scr = k.dram("qTs_scr", [3, 64, NH, SPC * DEC_T], BF16)
    kTs_scr = k.dram("kTs_scr", [3, 64, NH, SPC * DEC_T], BF16)
    Vs_scr = k.dram("Vs_scr", [3, SPC, DEC_T, NH * 65], BF16)
    oTs_scr = k.dram("oTs_scr", [SPC, 64, NH * DEC_T], BF16)
    x1p = k.dram("x1p", [SEQ, D], kind=dbg_kind)
    x1s = k.dram("x1s", [SPC, DEC_T, D], kind=dbg_kind)
    x2p = k.dram("x2p", [SEQ, D], kind=dbg_kind)
    x2s = k.dram("x2s", [SPC, DEC_T, D], kind=dbg_kind)
    x3p = k.dram("x3p", [SEQ, D], kind=dbg_kind)
    x3s = k.dram("x3s", [SPC, DEC_T, D], kind=dbg_kind)
    x4p = k.dram("x4p", [SEQ, D], kind=dbg_kind)
    x4s = k.dram("x4s", [SPC, DEC_T, D], kind=dbg_kind)
    qTb = k.dram("qTb", [2, 128, SEQ], BF16)
    kTb = k.dram("kTb", [2, 128, SEQ], BF16)
    qTbs = k.dram("qTbs", [8, 128, SPC * DEC_T], BF16)
    kTbs = k.dram("kTbs", [8, 128, SPC * DEC_T], BF16)
    ktm = k.dram("ktm", [2, SEQ, 128], BF16)
    vtm = k.dram("vtm", [4, SEQ, 128], BF16)
    ktms = k.dram("ktms", [8, SPC, DEC_T, 128], BF16)
    vtms = k.dram("vtms", [16, SPC, DEC_T, 128], BF16)
    bg_scr = k.dram("bg_scr", [NT, 128, 8])
    bgs_scr = k.dram("bgs_scr", [SPC, DEC_T, 32])
    otm_m = [k.dram(f"otm_m{i}", [512, 512]) for i in range(SEQ // 512)]
    otm_g = [k.dram(f"otm_g{i}", [4 * 512, 512], addr_space="Local") for i in range(SEQ // 512)]
    otms_scr = k.dram("otms_scr", [SPC, DEC_T, 2048])
    mod_scr = k.dram("mod_scr", [2, 1 + SPC, 6 * D])
    hTp = k.dram("hTp", [128, 8, SEQ], BF16)
    hTs = k.dram("hTs", [128, 8, SPC * DEC_T], BF16)
    qT_scr = k.dram("qT_scr", [3, 64, HM, SEQ], BF16)
    kT_scr = k.dram("kT_scr", [3, 64, HM, SEQ], BF16)
    V_scr = k.dram("V_scr", [3, NT, 128, HM * 65], BF16)

    for (w, _d) in A_GROUPS:
        for kv in ("k", "v"):
            for s in range(SPC):
                k.dma(new_s[(kv, w)], new_s[(kv, w)].ap[s, 0:w - DEC_T, :],
                      cache[(kv, w)], cache[(kv, w)].ap[s, DEC_T:w, :], eng="sync")

    identf = k.alloc("identf", 128)
    identb = k.alloc("identb", 128, BF16)
    epsb = k.alloc("epsb", 1)
    k.dma(identf, identf.ap, identf_in, identf_in.ap)
    k.copy("vector", identb, identb.ap, identf, identf.ap)
    k.memset("vector", epsb, epsb.ap, EPS)
    csb = k.alloc("csb", 8 * 8, BF16)
    k.keep()

    def phase_mod():
        k.phase()
        cs = k.alloc("cs", 8 * 5)
        k.dma(cs, v3(cs.ap, 8), cT, cT.ap.rearrange("(k p) n -> p k n", p=128))
        k.act(cs, cs.ap, cs, cs.ap, AF.Silu)
        k.memset("vector", csb, csb.ap, 0.0)
        k.copy("vector", csb, v3(csb.ap, 8)[:, :, 0:5], cs, v3(cs.ap, 8))
        modt = k.alloc("modt", 6 * D)
        gb = k.alloc("gb", D)
        wm = [k.alloc(f"wm{i}", 8 * 512, BF16) for i in range(2)]
        n = 1 + SPC
        it = 0
        for layer in range(2):
            k.dma(modt, modt.ap[0:n, :], b_mod, b_mod.ap[layer:layer + 1, :].partition_broadcast(n))
            for cb in range(12):
                wb = wm[it % 2]
                bank = B[it % 2]
                it += 1
                k.dma(wb, v3(wb.ap, 8),
                      w_mod, w_mod.ap[layer].rearrange("(k p) n -> p k n", p=128)[:, :, cb * 512:(cb + 1) * 512],
                      eng="gpsimd")
                for kc in range(8):
                    k.mm(bank, bank.ap[0:n, :], csb, v3(csb.ap, 8)[:, kc, 0:n], wb, v3(wb.ap, 8)[:, kc, :],
                         start=(kc == 0), stop=(kc == 7))
                sl = slice(cb * 512, (cb + 1) * 512)
                k.tt("vector", modt, modt.ap[0:n, sl], bank, bank.ap[0:n, :], modt, modt.ap[0:n, sl], ALU.add)
            for (gsrc, off) in ((norm_mix_g, 1 * D), (norm_ffn_g, 4 * D)):
                k.dma(gb, gb.ap[0:n, :], gsrc, gsrc.ap[layer:layer + 1, :].partition_broadcast(n))
                k.stt(modt, modt.ap[0:n, off:off + D], modt, modt.ap[0:n, off:off + D], 1.0, ALU.add,
                      gb, gb.ap[0:n, :], ALU.mult)
            k.dma(mod_scr, mod_scr.ap[layer], modt, modt.ap[0:n, :])

    def phase_norm(layer, x_p, x_s, a_off, b_off):
        k.phase()
        A_p = k.alloc("A_p", D)
        B_p = k.alloc("B_p", D)
        A_s = [k.alloc(f"A_s{i}", D) for i in range(2)]
        B_s = [k.alloc(f"B_s{i}", D) for i in range(2)]
        xt = [k.alloc(f"xt{i}", D) for i in range(2)]
        sq = k.alloc("sq", D)
        ssq = [k.alloc(f"ssq{i}", 1) for i in range(2)]
        std = [k.alloc(f"std{i}", 1) for i in range(2)]
        rstd = [k.alloc(f"rstd{i}", 1) for i in range(2)]
        tmp = [k.alloc(f"tmp{i}", D) for i in range(2)]
        hb = [k.alloc(f"hb{i}", D, BF16) for i in range(2)]
        hTt = [k.alloc(f"hTt{i}", D, BF16) for i in range(2)]
        mrow = mod_scr.ap[layer]
        k.dma(A_p, A_p.ap, mod_scr, mrow[0:1, a_off:a_off + D].partition_broadcast(128))
        k.dma(B_p, B_p.ap, mod_scr, mrow[0:1, b_off:b_off + D].partition_broadcast(128))
        tiles = [(128, i, None) for i in range(NT)] + [(DEC_T, None, s) for s in range(SPC)]
        for it, (rows, ti, si) in enumerate(tiles):
            j = it % 2
            if si is None:
                src_b, src_ap = x_p, x_p.ap[ti * 128:(ti + 1) * 128, :]
                Ab, Bb = A_p, B_p
            else:
                src_b, src_ap = x_s, x_s.ap[si]
                Ab, Bb = A_s[si % 2], B_s[si % 2]
                k.dma(Ab, Ab.ap[0:rows, :], mod_scr, mrow[1 + si:2 + si, a_off:a_off + D].partition_broadcast(rows))
                k.dma(Bb, Bb.ap[0:rows, :], mod_scr, mrow[1 + si:2 + si, b_off:b_off + D].partition_broadcast(rows))
            r = slice(0, rows)
            k.dma(xt[j], xt[j].ap[r, :], src_b, src_ap)
            k.tt("gpsimd", sq, sq.ap[r, :], xt[j], xt[j].ap[r, :], xt[j], xt[j].ap[r, :], ALU.mult)
            k.op("vector", (lambda o_, i_: (lambda e: e.tensor_reduce(out=o_, in_=i_, op=ALU.add, axis=AX.X)))(
                ssq[j].ap[r, :], sq.ap[r, :]), reads=[sq], writes=[ssq[j]])
            k.act(std[j], std[j].ap[r, :], ssq[j], ssq[j].ap[r, :], AF.Sqrt, scale=1.0 / D, bias=epsb.ap[r, :],
                  sbufs=[epsb])
            k.op("vector", (lambda o_, i_: (lambda e: e.reciprocal(out=o_, in_=i_)))(rstd[j].ap[r, :], std[j].ap[r, :]),
                 reads=[std[j]], writes=[rstd[j]])
            k.stt(tmp[j], tmp[j].ap[r, :], xt[j], xt[j].ap[r, :], rstd[j].ap[r, 0:1], ALU.mult,
                  Ab, Ab.ap[r, :], ALU.mult, sbufs=[rstd[j]])
            k.tt("gpsimd", hb[j], hb[j].ap[r, :], tmp[j], tmp[j].ap[r, :], Bb, Bb.ap[r, :], ALU.add)
            bank = B[j]
            bview = bank.ap.bitcast(BF16)
            for kc in range(8):
                k.tr(bank, bview[:, kc * 128:kc * 128 + rows], hb[j], hb[j].ap[r, kc * 128:(kc + 1) * 128],
                     identb, identb.ap[r, r])
            if si is None:
                k.copy("scalar", hTt[j], hTt[j].ap, bank, bview)
                k.dma(hTp, hTp.ap[:, :, ti * 128:(ti + 1) * 128], hTt[j], v3(hTt[j].ap, 8))
            else:
                k.copy("scalar", hTt[j], v3(hTt[j].ap, 8)[:, :, 0:rows], bank, v3(bview, 8)[:, :, 0:rows])
                k.dma(hTs, hTs.ap[:, :, si * DEC_T:(si + 1) * DEC_T], hTt[j], v3(hTt[j].ap, 8)[:, :, 0:rows])

    def phase_proj_a():
        k.phase()
        W = [k.alloc(f"Wa{j}", 8 * D, BF16) for j in range(3)]
        Wm = [k.alloc(f"Wm{j}", 8 * HM * DH, BF16) for j in range(3)]
        hsb = k.alloc("hsb", 8 * 2048, BF16)
        hss = k.alloc("hss", 8 * SPC * DEC_T, BF16)
        tab = [k.alloc(f"tab{i}", 64) for i in range(2)]
        tabs = k.alloc("tabs", 64)
        raw = [k.alloc(f"raw{i}", D) for i in range(2)]
        rp = [k.alloc(f"rp{i}", D) for i in range(2)]
        t1 = k.alloc("t1", 512)
        t2 = k.alloc("t2", 512)
        t3 = k.alloc("t3", 512)
        t4 = k.alloc("t4", 512)
        vf = [k.alloc(f"vf{i}", D) for i in range(2)]
        vaug = [k.alloc(f"vaug{i}", NH * 65, BF16) for i in range(2)]
        qkT = [k.alloc(f"qkT{i}", NH * 128, BF16) for i in range(2)]
        for i in range(2):
            k.memset("vector", vaug[i], vaug[i].ap, 1.0)
        k.dma(hss, v3(hss.ap, 8), hTs, hTs.ap)
        k.dma(tabs, tabs.ap[0:DEC_T, :], rope_s, rope_s.ap)
        cnt = {"raw": 0, "v": 0, "T": 0}

        def rope(dst, src, tb, rows, nh):
            r = slice(0, rows)
            s4 = src.ap[:, 0:nh * DH].rearrange("p (h t e) -> p h t e", h=nh, t=2)
            d4 = dst.ap[:, 0:nh * DH].rearrange("p (h t e) -> p h t e", h=nh, t=2)
            x1, x2 = s4[r, :, 0, :], s4[r, :, 1, :]
            cosb = tb.ap[r, 0:32].rearrange("p (o e) -> p o e", o=1).to_broadcast([rows, nh, 32])
            sinb = tb.ap[r, 32:64].rearrange("p (o e) -> p o e", o=1).to_broadcast([rows, nh, 32])
            tv = lambda t: v3(t.ap[:, 0:nh * 32], nh)[r]
            k.tt("vector", t1, tv(t1), src, x1, tb, cosb, ALU.mult)
            k.tt("gpsimd", t2, tv(t2), src, x2, tb, sinb, ALU.mult)
            k.tt("vector", dst, d4[r, :, 0, :], t1, tv(t1), t2, tv(t2), ALU.subtract)
            k.tt("gpsimd", t3, tv(t3), src, x1, tb, sinb, ALU.mult)
            k.tt("vector", t4, tv(t4), src, x2, tb, cosb, ALU.mult)
            k.tt("gpsimd", dst, d4[r, :, 1, :], t3, tv(t3), t4, tv(t4), ALU.add)

        def project(lhs_b, lhs_fn, rows, wt, ncol):
            i = cnt["raw"] % 3
            cnt["raw"] += 1
            pieces = []
            for pi, c0 in enumerate(range(0, ncol, 512)):
                wdt = min(512, ncol - c0)
                bank = B[2 * i + pi]
                for kc in range(8):
                    k.mm(bank, bank.ap[0:rows, 0:wdt], lhs_b, lhs_fn(kc),
                         wt, v3(wt.ap, 8)[:, kc, c0:c0 + wdt], start=(kc == 0), stop=(kc == 7))
                pieces.append((bank, c0, wdt))
            return pieces

        def qk_path(pieces, rows, tb, out_dram_fn, kout, nh):
            r = slice(0, rows)
            j = cnt["T"] % 2
            cnt["T"] += 1
            for (bank, c0, wdt) in pieces:
                k.copy("scalar", raw[j], raw[j].ap[r, c0:c0 + wdt], bank, bank.ap[r, 0:wdt])
            rope(rp[j], raw[j], tb, rows, nh)
            for (ob, oap) in kout:
                k.dma(ob, oap, rp[j], rp[j].ap[r, 0:nh * DH])
            tbs = (B[6], B[7])
            qv = v3(qkT[j].ap, NH)
            for h0 in range(0, nh, 8):
                hn = min(8, nh - h0)
                for hh in range(hn):
                    h = h0 + hh
                    bank = tbs[hh // 4]
                    k.tr(bank, bank.ap[0:64, (hh % 4) * 128:(hh % 4) * 128 + rows], rp[j], rp[j].ap[r, h * 64:(h + 1) * 64],
                         identf, identf.ap[r, r])
                for bi_ in range((hn + 3) // 4):
                    k.copy("scalar", qkT[j], qv[0:64, h0 + 4 * bi_:h0 + 4 * bi_ + 4, 0:rows], tbs[bi_],
                           v3(tbs[bi_].ap, 4)[0:64, :, 0:rows])
            ob, oap = out_dram_fn
            k.dma(ob, oap, qkT[j], qv[0:64, 0:nh, 0:rows])

        def v_path(pieces, rows, vout, vs_out, nh):
            r = slice(0, rows)
            j = cnt["v"] % 2
            cnt["v"] += 1
            for (bank, c0, wdt) in pieces:
                k.copy("scalar", vf[j], vf[j].ap[r, c0:c0 + wdt], bank, bank.ap[r, 0:wdt])
            for (ob, oap) in vout:
                k.dma(ob, oap, vf[j], vf[j].ap[r, 0:nh * DH])
            k.copy("vector", vaug[j], v3(vaug[j].ap, NH)[r, 0:nh, 0:64], vf[j], v3(vf[j].ap, NH)[r, 0:nh])
            ob, oap = vs_out
            k.dma(ob, oap, vaug[j], vaug[j].ap[r, 0:nh * 65])

        for g, (win, d) in enumerate(A_GROUPS):
            nb = group_blocks(d)
            for j in range(3):
                k.dma(W[j], v3(W[j].ap, 8),
                      a_w_in, a_w_in.ap.rearrange("(k p) n -> p k n", p=128)[:, :, (g * 3 + j) * D:(g * 3 + j + 1) * D],
                      eng="gpsimd")
                wc = HM * DH
                k.dma(Wm[j], v3(Wm[j].ap, 8), a_w_in_m,
                      a_w_in_m.ap.rearrange("(k p) n -> p k n", p=128)[:, :, (g * 3 + j) * wc:(g * 3 + j + 1) * wc], eng="gpsimd")
            for sb in range(4):
                k.dma(hsb, v3(hsb.ap, 8), hTp, hTp.ap[:, :, sb * 2048:(sb + 1) * 2048])
                for r_ in range(d):
                    for b_ in range(nb):
                        n = sb * 16 + r_ * nb + b_
                        c0 = r_ + d * b_ * 128
                        lhs = lambda kc, c0=c0, d=d: v3(hsb.ap, 8)[:, kc, c0:c0 + 127 * d + 1:d]
                        tb = tab[n % 2]
                        k.dma(tb, tb.ap, rope_p, rope_p.ap[g, n])
                        last = (sb == 3 and b_ == nb - 1)
                        bq = project(hsb, lhs, 128, Wm[0], HM * DH)
                        bk = project(hsb, lhs, 128, Wm[1], HM * DH)
                        bv = project(hsb, lhs, 128, Wm[2], HM * DH)
                        qk_path(bq, 128, tb, (qT_scr, qT_scr.ap[g][:, :, n * 128:(n + 1) * 128]), [], HM)
                        kout = [(new_p[("k", win)], new_p[("k", win)].ap[r_::d, :])] if last else []
                        qk_path(bk, 128, tb, (kT_scr, kT_scr.ap[g][:, :, n * 128:(n + 1) * 128]), kout, HM)
                        vout = [(new_p[("v", win)], new_p[("v", win)].ap[r_::d, :])] if last else []
                        v_path(bv, 128, vout, (V_scr, V_scr.ap[g, n]), HM)
            for s in range(SPC):
                lhs = lambda kc, s=s: v3(hss.ap, 8)[:, kc, s * DEC_T:(s + 1) * DEC_T]
                cs8 = slice(s * DEC_T, (s + 1) * DEC_T)
                bq = project(hss, lhs, DEC_T, W[0], D)
                qk_path(bq, DEC_T, tabs, (qTs_scr, qTs_scr.ap[g][:, :, cs8]), [], NH)
                bk = project(hss, lhs, DEC_T, W[1], D)
                qk_path(bk, DEC_T, tabs, (kTs_scr, kTs_scr.ap[g][:, :, cs8]),
                        [(new_s[("k", win)], new_s[("k", win)].ap[s, win - DEC_T:win, :])], NH)
                bv = project(hss, lhs, DEC_T, W[2], D)
                v_path(bv, DEC_T, [(new_s[("v", win)], new_s[("v", win)].ap[s, win - DEC_T:win, :])],
                       (Vs_scr, Vs_scr.ap[g, s]), NH)

    def prev_block(n, d):
        nb = group_blocks(d)
        sb, rem = divmod(n, 16)
        r_, b_ = divmod(rem, nb)
        if b_ > 0:
            return n - 1
        if sb > 0:
            return (sb - 1) * 16 + r_ * nb + nb - 1
        return None

    def recip(eng, out_b, out_ap, in_b, in_ap):
        k.op(eng, lambda e: e.reciprocal(out=out_ap, in_=in_ap), reads=[in_b], writes=[out_b])

    def phase_attn():
        k.phase()
        mask4 = k.alloc("mask4", 512, BF16)
        k.dma(mask4, mask4.ap, mask4_in, mask4_in.ap, eng="gpsimd")
        qh = [k.alloc(f"qh{i}", 2 * SEQ, BF16) for i in range(1)] * 2
        kh = [k.alloc(f"kh{i}", 2 * SEQ, BF16) for i in range(1)] * 2
        Vh = [k.alloc(f"Vh{i}", NT * 130, BF16) for i in range(2)]
        P = [k.alloc(f"P{i}", 512, BF16) for i in range(3)]
        osb = [k.alloc(f"osb{i}", 512) for i in range(4)]
        it = 0
        for g, (win, d) in enumerate(A_GROUPS):
            for hp in range(HM // 2):
                j = it % 2
                it += 1
                q3_, k3_ = v3(qh[j].ap, 2), v3(kh[j].ap, 2)
                k.dma(qh[j], q3_[0:64], qT_scr, qT_scr.ap[g][:, 2 * hp:2 * hp + 2, :])
                k.dma(kh[j], k3_[0:64], kT_scr, kT_scr.ap[g][:, 2 * hp:2 * hp + 2, :])
                for n8 in range(0, NT, 8):
                    k.dma(Vh[j], v3(Vh[j].ap, NT)[:, n8:n8 + 8], V_scr,
                          V_scr.ap[g].rearrange("n p c -> p n c")[:, n8:n8 + 8, hp * 130:(hp + 1) * 130])
                Vv = v3(Vh[j].ap, NT)
                def scores(n):
                    pn = prev_block(n, d)
                    sbank = B[n % 4]
                    cn = slice(n * 128, (n + 1) * 128)
                    for h in range(2):
                        k.mm(sbank, sbank.ap[:, (2 * h) * 128:(2 * h + 1) * 128], kh[j], k3_[0:64, h, cn],
                             qh[j], q3_[0:64, h, cn])
                        if pn is not None:
                            k.mm(sbank, sbank.ap[:, (2 * h + 1) * 128:(2 * h + 2) * 128], kh[j],
                                 k3_[0:64, h, pn * 128:(pn + 1) * 128], qh[j], q3_[0:64, h, cn])

                scores(0)
                for n in range(NT):
                    pn = prev_block(n, d)
                    sbank = B[n % 4]
                    if n + 1 < NT:
                        scores(n + 1)
                    Pb = P[n % 3]
                    if pn is not None:
                        k.act(Pb, Pb.ap, sbank, sbank.ap, AF.Exp, scale=DH ** -0.5)
                        k.tt("gpsimd" if n % 2 else "vector", Pb, Pb.ap, Pb, Pb.ap, mask4, mask4.ap, ALU.mult)
                    else:
                        for h in range(2):
                            c_ = slice((2 * h) * 128, (2 * h + 1) * 128)
                            k.act(Pb, Pb.ap[:, c_], sbank, sbank.ap[:, c_], AF.Exp, scale=DH ** -0.5)
                            k.tt("vector", Pb, Pb.ap[:, c_], Pb, Pb.ap[:, c_], mask4, mask4.ap[:, c_], ALU.mult)
                    q4 = n % 4
                    par = (n // 4) % 2
                    for h in range(2):
                        acc = B[4 + 2 * par + h]
                        oap = acc.ap[0:65, q4 * 128:(q4 + 1) * 128]
                        k.mm(acc, oap, Vh[j], Vv[:, n, h * 65:(h + 1) * 65], Pb, Pb.ap[:, (2 * h) * 128:(2 * h + 1) * 128],
                             start=True, stop=(pn is None))
                        if pn is not None:
                            k.mm(acc, oap, Vh[j], Vv[:, pn, h * 65:(h + 1) * 65], Pb,
                                 Pb.ap[:, (2 * h + 1) * 128:(2 * h + 2) * 128], start=False, stop=True)
                    if q4 == 3:
                        for h in range(2):
                            acc = B[4 + 2 * par + h]
                            ob = osb[2 * par + h]
                            k.copy("scalar" if h else "vector", ob, ob.ap[0:65, :], acc, acc.ap[0:65, :])
                            k.dma(o_scr, o_scr.ap[g, hp * 2 + h][:, (n - 3) * 128:(n + 1) * 128], ob, ob.ap[0:65, :])

    def phase_attn_sample():
        k.phase()
        E = k.alloc("E", 64)
        k.dma(E, E.ap[0:65, :], E_in, E_in.ap)
        mks = k.alloc("mks", 3 * 8 * 8, BF16)
        mkn = k.alloc("mkn", 3 * 8, BF16)
        for g_ in range(3):
            k.dma(mks, mks.ap.rearrange("p (g c t) -> p g c t", g=3, c=8)[:, g_], msk_s_in,
                  msk_s_in.ap[g_].rearrange("c p t -> p c t"), eng="gpsimd")
        k.dma(mkn, v3(mkn.ap, 3)[0:8], msk_n_in, msk_n_in.ap.rearrange("g p t -> p g t"), eng="gpsimd")
        mks4 = mks.ap.rearrange("p (g c t) -> p g c t", g=3, c=8)
        mkn3 = v3(mkn.ap, 3)
        qs = [k.alloc(f"qs{i}", 3 * NH * 8, BF16) for i in range(2)]
        kn = [k.alloc(f"kn{i}", 3 * NH * 8, BF16) for i in range(2)]
        Vn = [k.alloc(f"Vn{i}", 3 * NH * 65, BF16) for i in range(2)]
        Kc = [k.alloc(f"Kc{i}", D) for i in range(2)]
        Vc = [k.alloc(f"Vc{i}", D) for i in range(2)]
        kTc = [k.alloc(f"kTc{i}", NH * 128, BF16) for i in range(2)]
        Vac = [k.alloc(f"Vac{i}", NH * 65, BF16) for i in range(2)]
        Pc = [k.alloc(f"Pc{i}", 128, BF16) for i in range(2)]
        Pn = [k.alloc(f"Pn{i}", 128, BF16) for i in range(2)]
        tot = [k.alloc(f"tots{i}", 128) for i in range(2)]
        rd = k.alloc("rds", 128)
        oTs = [k.alloc(f"oTs{i}", 128, BF16) for i in range(2)]
        for i in range(2):
            k.memset("vector", Vac[i], Vac[i].ap, 1.0)
        it = 0
        for s in range(SPC):
            js = s % 2
            cs8 = slice(s * DEC_T, (s + 1) * DEC_T)
            q4 = qs[js].ap.rearrange("p (g c t) -> p g c t", g=3, c=NH)[0:64]
            k4 = kn[js].ap.rearrange("p (g c t) -> p g c t", g=3, c=NH)[0:64]
            V3 = v3(Vn[js].ap, 3)
            for g_ in range(3):
                k.dma(qs[js], q4[:, g_], qTs_scr, qTs_scr.ap[g_][:, :, cs8])
                k.dma(kn[js], k4[:, g_], kTs_scr, kTs_scr.ap[g_][:, :, cs8])
            k.dma(Vn[js], V3[0:DEC_T], Vs_scr, Vs_scr.ap[:, s].rearrange("g t c -> t g c"))
            nacc = [0]

            def accumulate(pvb):
                if nacc[0] == 0:
                    k.copy("vector", tot[js], tot[js].ap[0:65, :], pvb, pvb.ap[0:65, 0:128])
                else:
                    k.tt("vector", tot[js], tot[js].ap[0:65, :], tot[js], tot[js].ap[0:65, :], pvb, pvb.ap[0:65, 0:128], ALU.add)
                nacc[0] += 1
            for g, (win, d) in enumerate(A_GROUPS):
                for t0 in range(min(d, DEC_T)):
                    j = it % 2
                    it += 1
                    rows_sl = slice(t0, t0 + 127 * d + 1, d)
                    k.dma(Kc[j], Kc[j].ap, cache[("k", win)], cache[("k", win)].ap[s, rows_sl, :])
                    k.dma(Vc[j], Vc[j].ap, cache[("v", win)], cache[("v", win)].ap[s, rows_sl, :])
                    tb0, tb1 = B[2 * j], B[2 * j + 1]
                    for rnd in range(2):
                        for hh in range(8):
                            h = rnd * 8 + hh
                            bank = tb0 if hh < 4 else tb1
                            k.tr(bank, bank.ap[0:64, (hh % 4) * 128:(hh % 4 + 1) * 128], Kc[j], Kc[j].ap[:, h * 64:(h + 1) * 64],
                                 identf, identf.ap)
                        c0_ = rnd * 1024
                        k.copy("scalar", kTc[j], kTc[j].ap[0:64, c0_:c0_ + 512], tb0, tb0.ap[0:64, :])
                        k.copy("scalar", kTc[j], kTc[j].ap[0:64, c0_ + 512:c0_ + 1024], tb1, tb1.ap[0:64, :])
                    k.copy("vector", Vac[j], v3(Vac[j].ap, NH)[:, :, 0:64], Vc[j], v3(Vc[j].ap, NH))
                    sbank = B[6 + j]
                    kT3 = v3(kTc[j].ap, NH)
                    for h in range(NH):
                        k.mm(sbank, sbank.ap[:, h * 8:(h + 1) * 8], kTc[j], kT3[0:64, h, :],
                             qs[js], q4[:, g, h, :])
                    k.act(Pc[j], Pc[j].ap, sbank, sbank.ap[:, 0:128], AF.Exp, scale=DH ** -0.5)
                    mb = mks4[:, g, t0, :].rearrange("p (o t) -> p o t", o=1).to_broadcast([128, NH, DEC_T])
                    k.tt("vector", Pc[j], v3(Pc[j].ap, NH), Pc[j], v3(Pc[j].ap, NH), mks, mb, ALU.mult)
                    Va3 = v3(Vac[j].ap, NH)
                    acc = B[4 + j]
                    for h in range(NH):
                        k.mm(acc, acc.ap[0:65, h * 8:(h + 1) * 8], Vac[j], Va3[:, h, :], Pc[j], Pc[j].ap[:, h * 8:(h + 1) * 8])
                    accumulate(acc)
                j = it % 2
                sbank = B[6 + j]
                for h in range(NH):
                    k.mm(sbank, sbank.ap[0:DEC_T, 128 + h * 8:128 + (h + 1) * 8], kn[js], k4[:, g, h, :],
                         qs[js], q4[:, g, h, :])
                k.act(Pn[j], Pn[j].ap[0:DEC_T, :], sbank, sbank.ap[0:DEC_T, 128:256], AF.Exp, scale=DH ** -0.5)
                mb = mkn3[0:DEC_T, g, :].rearrange("p (o t) -> p o t", o=1).to_broadcast([DEC_T, NH, DEC_T])
                k.tt("vector", Pn[j], v3(Pn[j].ap, NH)[0:DEC_T], Pn[j], v3(Pn[j].ap, NH)[0:DEC_T], mkn, mb, ALU.mult)
                Vg = V3[0:DEC_T, g, :].rearrange("p (h c) -> p h c", h=NH)
                acc = B[4 + j]
                for h in range(NH):
                    k.mm(acc, acc.ap[0:65, h * 8:(h + 1) * 8], Vn[js], Vg[:, h, :], Pn[j], Pn[j].ap[0:DEC_T, h * 8:(h + 1) * 8])
                accumulate(acc)
            dbank = B[js]
            k.mm(dbank, dbank.ap[0:64, 0:128], E, E.ap[0:65, :], tot[js], tot[js].ap[0:65, :])
            recip("vector", rd, rd.ap[0:64, :], dbank, dbank.ap[0:64, 0:128])
            k.tt("vector", oTs[js], oTs[js].ap[0:64, :], tot[js], tot[js].ap[0:64, :], rd, rd.ap[0:64, :], ALU.mult)
            k.dma(oTs_scr, oTs_scr.ap[s], oTs[js], oTs[js].ap[0:64, :])

    def phase_combine():
        k.phase()
        E = k.alloc("E", 64)
        k.dma(E, E.ap[0:65, :], E_in, E_in.ap)
        og = [[k.alloc(f"og{i}_{g}", 2048) for g in range(3)] for i in range(2)]
        rden = [k.alloc(f"rden{i}", 512) for i in range(2)]
        oTb = [k.alloc(f"oTb{i}", 2048, BF16) for i in range(2)]
        it = 0
        for h in range(HM):
            for sb in range(4):
                j = it % 2
                it += 1
                cs_ = slice(sb * 2048, (sb + 1) * 2048)
                for g in range(3):
                    k.dma(og[j][g], og[j][g].ap[0:65, :], o_scr, o_scr.ap[g, h][:, cs_])
                tot = og[j][0]
                for g, r in ((1, 4), (2, 16)):
                    tv = tot.ap[0:65, :].rearrange("p (i r) -> p i r", r=r)
                    gv = og[j][g].ap[0:65, :].rearrange("p (r i) -> p i r", r=r)
                    k.tt("vector" if g == 1 else "gpsimd", tot, tv, tot, tv, og[j][g], gv, ALU.add)
                for q in range(4):
                    bank = B[(it * 4 + q) % 8]
                    cq = slice(q * 512, (q + 1) * 512)
                    k.mm(bank, bank.ap[0:64, :], E, E.ap[0:65, :], tot, tot.ap[0:65, cq])
                    rd = rden[q % 2]
                    recip("vector", rd, rd.ap[0:64, :], bank, bank.ap[0:64, :])
                    k.tt("gpsimd", oTb[j], oTb[j].ap[0:64, cq], tot, tot.ap[0:64, cq], rd, rd.ap[0:64, :], ALU.mult)
                k.dma(oT_mine[sb], oT_mine[sb].ap[h * 64:(h + 1) * 64, :], oTb[j], oTb[j].ap[0:64, :])
        for sb in range(4):
            k.allgather(oT_scr[sb], oT_scr[sb].ap, oT_mine[sb], oT_mine[sb].ap, CC_GROUPS)

    def phase_out_a():
        k.phase()
        wo = k.alloc("wo", NH * D, BF16)
        wo3 = v3(wo.ap, NH)
        k.dma(wo, wo3[0:64], a_w_out, a_w_out.ap.rearrange("(h e) n -> e h n", e=64), eng="gpsimd")
        G = k.alloc("G", D)
        Gs = [k.alloc(f"Gs{i}", D) for i in range(2)]
        k.dma(G, G.ap, mod_scr, mod_scr.ap[0][0:1, 2 * D:3 * D].partition_broadcast(128))
        o4 = [k.alloc(f"o4{i}", NH * 512, BF16) for i in range(2)]
        os8 = [k.alloc(f"os8{i}", NH * DEC_T, BF16) for i in range(2)]
        xt = [k.alloc(f"xt{i}", D) for i in range(2)]
        yt = [k.alloc(f"yt{i}", D) for i in range(2)]
        nb_ = 0

        def one(rows, lhs_b, lhs_fn, x_src_b, x_src_ap, Gb, dst_b, dst_ap, j):
            nonlocal nb_
            r = slice(0, rows)
            k.dma(xt[j], xt[j].ap[r, :], x_src_b, x_src_ap)
            for half in range(2):
                bank = B[nb_ % 8]
                nb_ += 1
                for h in range(NH):
                    k.mm(bank, bank.ap[r, :], lhs_b, lhs_fn(h), wo, wo3[0:64, h, half * 512:(half + 1) * 512],
                         start=(h == 0), stop=(h == NH - 1))
                hs = slice(half * 512, (half + 1) * 512)
                k.tt("vector", yt[j], yt[j].ap[r, hs], bank, bank.ap[r, :], Gb, Gb.ap[r, hs], ALU.mult)
                k.tt("gpsimd", yt[j], yt[j].ap[r, hs], yt[j], yt[j].ap[r, hs], xt[j], xt[j].ap[r, hs], ALU.add)
            k.dma(dst_b, dst_ap, yt[j], yt[j].ap[r, :])

        for t in range(NT):
            jj = (t // 4) % 2
            if t % 4 == 0:
                osb_ = oT_scr[t // 16]
                k.dma(o4[jj], v3(o4[jj].ap, NH)[0:64], osb_,
                      osb_.ap[:, (t % 16) * 128:(t % 16 + 4) * 128].rearrange("(h e) c -> e h c", e=64))
            ov = v3(o4[jj].ap, NH)
            c_ = slice((t % 4) * 128, (t % 4 + 1) * 128)
            one(128, o4[jj], (lambda h, ov=ov, c_=c_: ov[0:64, h, c_]), xp, xp.ap[t * 128:(t + 1) * 128, :], G,
                x1p, x1p.ap[t * 128:(t + 1) * 128, :], t % 2)
        for s in range(SPC):
            j = s % 2
            k.dma(os8[j], os8[j].ap[0:64, :], oTs_scr, oTs_scr.ap[s])
            k.dma(Gs[j], Gs[j].ap[0:DEC_T, :], mod_scr, mod_scr.ap[0][1 + s:2 + s, 2 * D:3 * D].partition_broadcast(DEC_T))
            ov = v3(os8[j].ap, NH)
            one(DEC_T, os8[j], (lambda h, ov=ov: ov[0:64, h, :]), xs, xs.ap[s], Gs[j], x1s, x1s.ap[s], j)

    phase_mod()
    phase_norm(0, xp, xs, 1 * D, 0)
    phase_proj_a()
    def phase_ffn(layer, xin_p, xin_s, xout_p, xout_s):
        k.phase()
        TB = 1024
        NC_ = D_FF // 128
        wd = k.alloc("wd", NC_ * D, BF16)
        wd3 = v3(wd.ap, NC_)
        k.dma(wd, wd3, ffn_w_down, ffn_w_down.ap[layer].rearrange("(c p) n -> p c n", p=128), eng="gpsimd")
        cw = k.alloc("cw", NC_ * 4)
        cw3 = v3(cw.ap, NC_)
        k.dma(cw, cw3, ffn_cw, ffn_cw.ap[layer])
        carry = k.alloc("carry", NC_ * 2)
        carry3 = v3(carry.ap, NC_)
        k.memset("vector", carry, carry.ap, 0.0)
        carry_s = k.alloc("carry_s", NC_ * SPC * 2)
        cs4 = carry_s.ap.rearrange("p (c s t) -> p c s t", c=NC_, s=SPC)
        for c in range(0, NC_, 11):
            k.dma(carry_s, cs4[:, c:c + 11].rearrange("p c s t -> p c (s t)"), fconvT,
                  fconvT.ap[layer][:, c:c + 11].rearrange("p c s t -> p c (s t)"))
        G = k.alloc("G", D)
        k.dma(G, G.ap, mod_scr, mod_scr.ap[layer][0:1, 5 * D:6 * D].partition_broadcast(128))
        Gs = k.alloc("Gs", D)
        for s in range(SPC):
            k.dma(Gs, Gs.ap[s * DEC_T:(s + 1) * DEC_T, :], mod_scr,
                  mod_scr.ap[layer][1 + s:2 + s, 5 * D:6 * D].partition_broadcast(DEC_T))
        hb = [k.alloc(f"hblk{i}", 8 * TB, BF16) for i in range(2)]
        act = k.alloc("act", NC_ * TB, BF16)
        act3 = v3(act.ap, NC_)
        wg = [k.alloc(f"wg{i}", 8 * 128, BF16) for i in range(2)]
        wv = [k.alloc(f"wv{i}", 8 * 128, BF16) for i in range(2)]
        gp = [k.alloc(f"gp{i}", TB + 2) for i in range(2)]
        tc_ = [k.alloc(f"tc{i}", TB) for i in range(2)]
        sg = [k.alloc(f"sg{i}", TB) for i in range(2)]
        xt = [k.alloc(f"xt{i}", D) for i in range(2)]
        yt = [k.alloc(f"yt{i}", D) for i in range(2)]
        wup = ffn_w_up.ap[layer].rearrange("(k p) n -> p k n", p=128)
        cnt = 0
        blocks = [("p", b) for b in range(SEQ // TB)] + [("s", 0)]
        for bi, (kind, b) in enumerate(blocks):
            hbb = hb[bi % 2]
            if kind == "p":
                ntok = TB
                k.dma(hbb, v3(hbb.ap, 8), hTp, hTp.ap[:, :, b * TB:(b + 1) * TB])
                h3 = v3(hbb.ap, 8)
            else:
                ntok = SPC * DEC_T
                h3 = v3(hbb.ap, 8)[:, :, 0:ntok]
                k.dma(hbb, h3, hTs, hTs.ap)
            nq = max(1, ntok // 512)
            qw = min(512, ntok)
            for c in range(NC_):
                j = cnt % 2
                cnt += 1
                k.dma(wg[j], v3(wg[j].ap, 8), ffn_w_up, wup[:, :, c * 128:(c + 1) * 128], eng="gpsimd")
                k.dma(wv[j], v3(wv[j].ap, 8), ffn_w_up, wup[:, :, D_FF + c * 128:D_FF + (c + 1) * 128], eng="gpsimd")
                gpj = gp[j]
                if kind == "p":
                    k.copy("gpsimd", gpj, gpj.ap[:, 0:2], carry, carry3[:, c, :])
                    gnew = gpj.ap[:, 2:2 + ntok]
                else:
                    g3 = gpj.ap[:, 0:SPC * 10].rearrange("p (s t) -> p s t", s=SPC)
                    k.copy("gpsimd", gpj, g3[:, :, 0:2], carry_s, cs4[:, c])
                vbanks = []
                for q in range(nq):
                    gb_ = B[(4 * cnt + 2 * q) % 8]
                    vb_ = B[(4 * cnt + 2 * q + 1) % 8]
                    cq = slice(q * qw, (q + 1) * qw)
                    for kc in range(8):
                        k.mm(gb_, gb_.ap[:, 0:qw], wg[j], v3(wg[j].ap, 8)[:, kc, :], hbb, h3[:, kc, cq],
                             start=(kc == 0), stop=(kc == 7))
                    for kc in range(8):
                        k.mm(vb_, vb_.ap[:, 0:qw], wv[j], v3(wv[j].ap, 8)[:, kc, :], hbb, h3[:, kc, cq],
                             start=(kc == 0), stop=(kc == 7))
                    if kind == "p":
                        k.copy("scalar", gpj, gpj.ap[:, 2 + q * qw:2 + (q + 1) * qw], gb_, gb_.ap[:, 0:qw])
                    else:
                        k.copy("scalar", gpj, g3[:, :, 2:10], gb_, gb_.ap[:, 0:qw].rearrange("p (s t) -> p s t", s=SPC))
                    vbanks.append(vb_)
                w0, w1, w2, bb = (cw3[:, c, i_:i_ + 1] for i_ in range(4))
                if kind == "p":
                    x0, x1_, x2_ = gpj.ap[:, 0:ntok], gpj.ap[:, 1:1 + ntok], gpj.ap[:, 2:2 + ntok]
                    tcv, sgv = tc_[j].ap[:, 0:ntok], sg[j].ap[:, 0:ntok]
                else:
                    x0, x1_, x2_ = g3[:, :, 0:8], g3[:, :, 1:9], g3[:, :, 2:10]
                    tcv = tc_[j].ap[:, 0:ntok].rearrange("p (s t) -> p s t", s=SPC)
                    sgv = sg[j].ap[:, 0:ntok]
                k.ts("vector", tc_[j], tcv, gpj, x0, w0, ALU.mult, sbufs=[cw])
                k.stt(tc_[j], tcv, gpj, x1_, w1, ALU.mult, tc_[j], tcv, ALU.add, sbufs=[cw])
                k.stt(tc_[j], tcv, gpj, x2_, w2, ALU.mult, tc_[j], tcv, ALU.add, sbufs=[cw])
                k.act(sg[j], sgv, tc_[j], tc_[j].ap[:, 0:ntok], AF.Silu, bias=bb, sbufs=[cw])
                for q in range(nq):
                    cq = slice(q * qw, (q + 1) * qw)
                    k.tt("vector", act, act3[:, c, cq], vbanks[q], vbanks[q].ap[:, 0:qw], sg[j], sg[j].ap[:, cq], ALU.mult)
                if kind == "p":
                    k.copy("gpsimd", carry, carry3[:, c, :], gpj, gpj.ap[:, ntok:ntok + 2])
                else:
                    k.copy("gpsimd", carry_s, cs4[:, c], gpj, g3[:, :, 8:10])
            ntile = max(1, ntok // 128)
            rows = min(128, ntok)
            r = slice(0, rows)
            for t in range(ntile):
                j = t % 2
                if kind == "p":
                    tok0 = b * TB + t * 128
                    k.dma(xt[j], xt[j].ap, xin_p, xin_p.ap[tok0:tok0 + 128, :])
                    Gb = G
                else:
                    k.dma(xt[j], xt[j].ap[r, :], xin_s, xin_s.ap.rearrange("s t d -> (s t) d"))
                    Gb = Gs
                for half in range(2):
                    bank = B[(2 * t + half) % 8]
                    for c in range(NC_):
                        k.mm(bank, bank.ap[r, :], act, act3[:, c, t * 128:t * 128 + rows], wd, wd3[:, c, half * 512:(half + 1) * 512],
                             start=(c == 0), stop=(c == NC_ - 1))
                    hs = slice(half * 512, (half + 1) * 512)
                    k.tt("vector", yt[j], yt[j].ap[r, hs], bank, bank.ap[r, :], Gb, Gb.ap[r, hs], ALU.mult)
                    k.tt("gpsimd", yt[j], yt[j].ap[r, hs], yt[j], yt[j].ap[r, hs], xt[j], xt[j].ap[r, hs], ALU.add)
                if kind == "p":
                    k.dma(xout_p, xout_p.ap[tok0:tok0 + 128, :], yt[j], yt[j].ap)
                else:
                    k.dma(xout_s, xout_s.ap.rearrange("s t d -> (s t) d"), yt[j], yt[j].ap[r, :])
        k.dma(fconv_p_out, fconv_p_out.ap[layer], carry, carry3)
        k.dma(fconv_s_out, fconv_s_out.ap[layer], carry_s, carry_s.ap)

    NQ = 8
    NV = 16
    bankrr = [0]

    def nbank():
        b_ = B[bankrr[0] % 8]
        bankrr[0] += 1
        return b_

    def phase_b1():
        k.phase()
        Wq = k.alloc("Wqkv", 8 * B_CONV_DIM, BF16)
        Wq3 = v3(Wq.ap, 8)
        bw = b_w_in.ap.rearrange("(k p) n -> p k n", p=128)
        for c0 in range(0, B_CONV_DIM, 1024):
            k.dma(Wq, Wq3[:, :, c0:c0 + 1024], b_w_in, bw[:, :, c0:c0 + 1024], eng="gpsimd")
        Wba = k.alloc("Wba", 8 * 32, BF16)
        k.dma(Wba, v3(Wba.ap, 8), b_w_in, bw[:, :, 6144:6176], eng="gpsimd")
        bwm = b_w_in_m.ap.rearrange("(k p) n -> p k n", p=128)
        Wqm = k.alloc("Wqm", 8 * 1024, BF16)
        Wqm3 = v3(Wqm.ap, 8)
        k.dma(Wqm, Wqm3, b_w_in_m, bwm[:, :, 0:1024], eng="gpsimd")
        Wbam = k.alloc("Wbam", 8 * 8, BF16)
        k.dma(Wbam, v3(Wbam.ap, 8), b_w_in_m, bwm[:, :, 1024:1032], eng="gpsimd")
        cwb = k.alloc("cwb", 32 * 4)
        cwb3 = v3(cwb.ap, 32)
        k.dma(cwb, cwb3, b_cw, b_cw.ap)
        cwm = k.alloc("cwm", 8 * 4)
        cwm3 = v3(cwm.ap, 8)
        k.dma(cwm, cwm3, b_cw_m, b_cw_m.ap)
        carry = k.alloc("carryB", 8 * 3)
        carry3 = v3(carry.ap, 8)
        k.memset("vector", carry, carry.ap, 0.0)
        carry_s = k.alloc("carryBs", 32 * SPC * 3)
        cs4 = carry_s.ap.rearrange("p (c s t) -> p c s t", c=32, s=SPC)
        for c in range(0, 32, 8):
            k.dma(carry_s, cs4[:, c:c + 8].rearrange("p c s t -> p c (s t)"), bconvT,
                  bconvT.ap[:, c:c + 8].rearrange("p c s t -> p c (s t)"))
        onesf = k.alloc("onesf", 128)
        k.memset("vector", onesf, onesf.ap, 1.0)
        one1 = k.alloc("one1", 1)
        k.memset("vector", one1, one1.ap, 1.0)

        def head_consts(nm, dt_in, al_in, n):
            dtb_ = k.alloc("dtb" + nm, n)
            negA_ = k.alloc("negA" + nm, n)
            k.dma(dtb_, dtb_.ap, dt_in, dt_in.ap.partition_broadcast(128))
            k.dma(negA_, negA_.ap, al_in, al_in.ap.partition_broadcast(128))
            k.act(negA_, negA_.ap, negA_, negA_.ap, AF.Exp)
            k.ts("vector", negA_, negA_.ap, negA_, negA_.ap, -1.0, ALU.mult)
            return dtb_, negA_

        dtb, negA = head_consts("", b_dt_bias, b_a_log, 16)
        dtbm, negAm = head_consts("m", b_dt_bias_m, b_a_log_m, 4)
        TBk = 512
        hb = [k.alloc(f"hblk{i}", 8 * TBk, BF16) for i in range(2)]
        NB1 = 4
        pre = [k.alloc(f"pre{i}", TBk + 3) for i in range(NB1)]
        tcv_ = [k.alloc(f"tcv{i}", TBk) for i in range(NB1)]
        av = [k.alloc(f"av{i}", TBk) for i in range(NB1)]
        sq = [k.alloc(f"sq{i}", TBk) for i in range(NB1)]
        sd = [k.alloc(f"sd{i}", TBk) for i in range(NB1)]
        xn = [k.alloc(f"xn{i}", TBk, BF16) for i in range(NB1)]
        tmb = [k.alloc(f"tmb{i}", TBk, BF16) for i in range(NB1)]
        braw = k.alloc("braw", 128)
        xa = k.alloc("xa", 64)
        ab = k.alloc("ab", 64)
        sp = k.alloc("sp", 64)
        bgt = [k.alloc(f"bgt{i}", 128) for i in range(2)]
        cnt = 0
        blocks = [("p", b) for b in range(SEQ // TBk)] + [("s", 0)]
        for bi, (kind, b) in enumerate(blocks):
            hbb = hb[bi % 2]
            if kind == "p":
                ntok = TBk
                h3 = v3(hbb.ap, 8)
                k.dma(hbb, h3, hTp, hTp.ap[:, :, b * TBk:(b + 1) * TBk])
                chunks = [(Wqm, Wqm3, cm * 128, [cwm3[:, cm, i_:i_ + 1] for i_ in range(4)], cwm,
                           "q" if cm < 2 else ("k" if cm < 4 else "v"), cm if cm < 2 else (cm - 2 if cm < 4 else cm - 4), cm)
                          for cm in range(8)]
                nvh = 4
            else:
                ntok = SPC * DEC_T
                h3 = v3(hbb.ap, 8)[:, :, 0:ntok]
                k.dma(hbb, h3, hTs, hTs.ap)
                chunks = [(Wq, Wq3, cc * 128, [cwb3[:, cc, i_:i_ + 1] for i_ in range(4)], cwb,
                           "q" if cc < 8 else ("k" if cc < 16 else "v"), cc if cc < 8 else (cc - 8 if cc < 16 else cc - 16), cc)
                          for cc in range(32)]
                nvh = 16
            for (wt, wt3, wcol, w, cwbuf, role, hidx, ci_) in chunks:
                j = cnt % NB1
                cnt += 1
                bank = nbank()
                for kc in range(8):
                    k.mm(bank, bank.ap[:, 0:ntok], wt, wt3[:, kc, wcol:wcol + 128], hbb, h3[:, kc, :],
                         start=(kc == 0), stop=(kc == 7))
                pj = pre[j]
                if kind == "p":
                    k.copy("gpsimd", pj, pj.ap[:, 0:3], carry, carry3[:, ci_, :])
                    k.copy("scalar", pj, pj.ap[:, 3:3 + ntok], bank, bank.ap[:, 0:ntok])
                    k.copy("gpsimd", carry, carry3[:, ci_, :], pj, pj.ap[:, ntok:ntok + 3])
                    xs_ = [pj.ap[:, i_:i_ + ntok] for i_ in range(4)]
                    tv = tcv_[j].ap[:, 0:ntok]
                else:
                    p3 = pj.ap[:, 0:SPC * 11].rearrange("p (s t) -> p s t", s=SPC)
                    k.copy("gpsimd", pj, p3[:, :, 0:3], carry_s, cs4[:, ci_])
                    k.copy("scalar", pj, p3[:, :, 3:11], bank, bank.ap[:, 0:ntok].rearrange("p (s t) -> p s t", s=SPC))
                    k.copy("gpsimd", carry_s, cs4[:, ci_], pj, p3[:, :, 8:11])
                    xs_ = [p3[:, :, i_:i_ + 8] for i_ in range(4)]
                    tv = tcv_[j].ap[:, 0:ntok].rearrange("p (s t) -> p s t", s=SPC)
                k.ts("vector", tcv_[j], tv, pj, xs_[0], w[0], ALU.mult, sbufs=[cwbuf])
                for i_ in range(1, 4):
                    k.stt(tcv_[j], tv, pj, xs_[i_], w[i_], ALU.mult, tcv_[j], tv, ALU.add, sbufs=[cwbuf])
                a_ = av[j]
                k.act(a_, a_.ap[:, 0:ntok], tcv_[j], tcv_[j].ap[:, 0:ntok], AF.Silu)
                nt_ = max(1, ntok // 128)
                if role in ("q", "k"):
                    k.tt("gpsimd", sq[j], sq[j].ap[:, 0:ntok], a_, a_.ap[:, 0:ntok], a_, a_.ap[:, 0:ntok], ALU.mult)
                    b2 = nbank()
                    k.mm(b2, b2.ap[:, 0:ntok], onesf, onesf.ap, sq[j], sq[j].ap[:, 0:ntok])
                    k.act(sd[j], sd[j].ap[:, 0:ntok], b2, b2.ap[:, 0:ntok], AF.Sqrt, bias=epsb.ap[:, 0:1], sbufs=[epsb])
                    recip("vector", sd[j], sd[j].ap[:, 0:ntok], sd[j], sd[j].ap[:, 0:ntok])
                    scl = (128.0 ** -0.5) if role == "q" else 1.0
                    k.stt(xn[j], xn[j].ap[:, 0:ntok], a_, a_.ap[:, 0:ntok], scl, ALU.mult, sd[j], sd[j].ap[:, 0:ntok], ALU.mult)
                    dstT = (qTb, qTbs) if role == "q" else (kTb, kTbs)
                    if kind == "p":
                        k.dma(dstT[0], dstT[0].ap[hidx][:, b * TBk:(b + 1) * TBk], xn[j], xn[j].ap[:, 0:ntok])
                    else:
                        k.dma(dstT[1], dstT[1].ap[hidx], xn[j], xn[j].ap[:, 0:ntok])
                    src_tm = xn[j] if role == "k" else None
                    dst_tm = (ktm, ktms, hidx)
                else:
                    k.copy("vector", xn[j], xn[j].ap[:, 0:ntok], a_, a_.ap[:, 0:ntok])
                    src_tm = xn[j]
                    dst_tm = (vtm, vtms, hidx)
                if src_tm is not None:
                    b3 = nbank()
                    b3v = b3.ap.bitcast(BF16)
                    if kind == "p":
                        for t in range(nt_):
                            k.tr(b3, b3v[:, t * 128:(t + 1) * 128], src_tm, src_tm.ap[:, t * 128:(t + 1) * 128], identb, identb.ap)
                        k.copy("scalar", tmb[j], tmb[j].ap[:, 0:ntok], b3, b3v[:, 0:ntok])
                        k.dma(dst_tm[0], dst_tm[0].ap[dst_tm[2]][b * TBk:(b + 1) * TBk, :].rearrange("(t p) d -> p t d", p=128),
                              tmb[j], v3(tmb[j].ap, nt_))
                    else:
                        for s in range(SPC):
                            k.tr(b3, b3v[0:DEC_T, s * 128:(s + 1) * 128], src_tm, src_tm.ap[:, s * DEC_T:(s + 1) * DEC_T],
                                 identb, identb.ap)
                        k.copy("scalar", tmb[j], tmb[j].ap[0:DEC_T, 0:SPC * 128], b3, b3v[0:DEC_T, 0:SPC * 128])
                        k.dma(dst_tm[1], dst_tm[1].ap[dst_tm[2]].rearrange("s t d -> t s d"),
                              tmb[j], v3(tmb[j].ap, SPC)[0:DEC_T])
            if kind == "p":
                tl = [(128, lambda kc, t=t: h3[:, kc, t * 128:(t + 1) * 128]) for t in range(4)]
                wba_, dtb_, negA_ = Wbam, dtbm, negAm
            else:
                tl = [(DEC_T, lambda kc, s=s: h3[:, kc, s * DEC_T:(s + 1) * DEC_T]) for s in range(SPC)]
                wba_, dtb_, negA_ = Wba, dtb, negA
            w2 = 2 * nvh
            bank = nbank()
            rows = tl[0][0]
            r = slice(0, rows)
            for ti, (_r, lf) in enumerate(tl):
                for kc in range(8):
                    k.mm(bank, bank.ap[r, ti * w2:(ti + 1) * w2], hbb, lf(kc), wba_, v3(wba_.ap, 8)[:, kc, :],
                         start=(kc == 0), stop=(kc == 7))
            k.copy("scalar", braw, braw.ap[r, 0:4 * w2], bank, bank.ap[r, 0:4 * w2])
            b3_ = v3(braw.ap[:, 0:4 * w2], 4)
            bgb = bgt[bi % 2]
            bg3 = v3(bgb.ap[:, 0:4 * w2], 4)
            k.act(bgb, bg3[r, :, 0:nvh], braw, b3_[r, :, 0:nvh], AF.Sigmoid)
            bcn = lambda t_: t_.ap[r, :].rearrange("p (o e) -> p o e", o=1).to_broadcast([rows, 4, nvh])
            xa3, ab3, sp3 = (v3(t_.ap[:, 0:4 * nvh], 4)[r] for t_ in (xa, ab, sp))
            k.tt("vector", xa, xa3, braw, b3_[r, :, nvh:w2], dtb_, bcn(dtb_), ALU.add)
            k.act(ab, ab3, xa, xa3, AF.Abs)
            k.act(ab, ab3, ab, ab3, AF.Exp, scale=-1.0)
            k.act(ab, ab3, ab, ab3, AF.Ln, bias=one1.ap[r, 0:1], sbufs=[one1])
            k.ts("vector", sp, sp3, xa, xa3, 0.0, ALU.max)
            k.tt("vector", sp, sp3, sp, sp3, ab, ab3, ALU.add)
            k.tt("vector", bgb, bg3[r, :, nvh:w2], sp, sp3, negA_, bcn(negA_), ALU.mult)
            if kind == "p":
                k.dma(bg_scr, bg_scr.ap[b * 4:(b + 1) * 4].rearrange("t p c -> p t c"), bgb, bg3)
            else:
                k.dma(bgs_scr, bgs_scr.ap.rearrange("s p c -> p s c"), bgb, bg3[r])
        k.dma(bconv_p_out, bconv_p_out.ap, carry, carry.ap)
        k.dma(bconv_s_out, bconv_s_out.ap, carry_s, carry_s.ap)

    def b2_run(prompt):
        k.phase()
        nqm, nvm, DS = (2, 4, 4) if prompt else (NQ, NV, 2)
        Um = k.alloc("Um", 128)
        Umb = k.alloc("Umb", 128, BF16)
        Ls = k.alloc("Ls", 128)
        sel = k.alloc("sel", NV * 128)
        k.dma(Um, Um.ap, Umat_in, Umat_in.ap)
        k.copy("vector", Umb, Umb.ap, Um, Um.ap)
        k.dma(Ls, Ls.ap, Lstrict_in, Lstrict_in.ap)
        k.dma(sel, sel.ap[0:NV, :], sel_in, sel_in.ap)
        sel3 = v3(sel.ap, NV)
        S = k.alloc("S", nvm * 128)
        Sb = k.alloc("Sb", nvm * 128, BF16)
        S3, Sb3 = v3(S.ap, nvm), v3(Sb.ap, nvm)
        qT = [k.alloc(f"qTc{i}", nqm * 128, BF16) for i in range(DS)]
        kT = [k.alloc(f"kTc{i}", nqm * 128, BF16) for i in range(DS)]
        kt = [k.alloc(f"ktc{i}", nqm * 128, BF16) for i in range(DS)]
        vt = [k.alloc(f"vtc{i}", nvm * 128, BF16) for i in range(DS)]
        bg = [k.alloc(f"bgc{i}", 32) for i in range(DS)]
        cg_ = [k.alloc(f"cg{i}", 16) for i in range(DS)]
        cgT_ = [k.alloc(f"cgT{i}", 128) for i in range(DS)]
        nbeta_ = [k.alloc(f"nbeta{i}", 16) for i in range(DS)]
        bec_ = [k.alloc(f"bec{i}", 16) for i in range(DS)]
        Gs_ = [k.alloc(f"Gs{i}", nqm * 128) for i in range(DS)]
        QKs_ = [k.alloc(f"QKs{i}", nqm * 128) for i in range(DS)]
        G4 = range(4)
        dec_ = [k.alloc(f"dec{g}", 512) for g in G4]
        decT_ = [k.alloc(f"decT{g}", 512) for g in G4]
        eR_ = [k.alloc(f"eR{g}", 512) for g in G4]
        tG_ = [k.alloc(f"tG{g}", 512) for g in G4]
        X_ = [[k.alloc(f"X{g}_{i}", 512) for i in range(2)] for g in G4]
        Xt_ = [[k.alloc(f"Xt{g}_{i}", 512) for i in range(2)] for g in G4]
        Tt_ = [[k.alloc(f"Tt{g}_{i}", 512) for i in range(2)] for g in G4]
        TtB_ = [k.alloc(f"TtB{g}", 512, BF16) for g in G4]
        PT_ = [k.alloc(f"PT{g}", 512, BF16) for g in G4]
        bv_ = [k.alloc(f"bv{g}", 512, BF16) for g in G4]
        bk_ = [k.alloc(f"bk{g}", 512, BF16) for g in G4]
        kdec_ = [k.alloc(f"kdec{g}", 512, BF16) for g in G4]
        qgT_ = [k.alloc(f"qgT{g}", 512, BF16) for g in G4]
        u0s_ = [k.alloc(f"u0s{g}", 512) for g in G4]
        wkT_ = [k.alloc(f"wkT{g}", 512, BF16) for g in G4]
        ub_ = [k.alloc(f"ub{g}", 512, BF16) for g in G4]
        otm_ = [k.alloc(f"otm{g}", 512) for g in G4]
        v4 = lambda t_, C: v3(t_.ap, 4)[0:C]

        done = [0]

        def chunk(C, levels, ci, loads, o_dst, nq, nv, ws_fn, order=None):
            r = slice(0, C)
            j = ci % DS
            cg, cgT, nbeta, bec, Gs, QKs = cg_[j], cgT_[j], nbeta_[j], bec_[j], Gs_[j], QKs_[j]
            q3, k3, kt3 = (v3(t_.ap[:, 0:nq * 128], nq) for t_ in (qT[j], kT[j], kt[j]))
            vt3 = v3(vt[j].ap[:, 0:nv * 128], nv)
            loads(qT[j], q3, kT[j], k3, kt[j], kt3, vt[j], vt3, bg[j])
            beta, g_ = bg[j].ap[r, 0:nv], bg[j].ap[r, nv:2 * nv]
            b0 = nbank()
            k.mm(b0, b0.ap[r, 0:nv], Um, Um.ap[r, r], bg[j], g_)
            k.copy("vector", cg, cg.ap[r, 0:nv], b0, b0.ap[r, 0:nv])
            b1 = nbank()
            k.tr(b1, b1.ap[0:nv, 0:C], cg, cg.ap[r, 0:nv], identf, identf.ap[r, r])
            k.copy("vector", cgT, cgT.ap[0:nv, 0:C], b1, b1.ap[0:nv, 0:C])
            k.ts("vector", nbeta, nbeta.ap[r, 0:nv], bg[j], beta, -1.0, ALU.mult)
            k.act(bec, bec.ap[r, 0:nv], cg, cg.ap[r, 0:nv], AF.Exp)
            k.tt("vector", bec, bec.ap[r, 0:nv], bec, bec.ap[r, 0:nv], bg[j], beta, ALU.mult)
            Gs3, QK3 = v3(Gs.ap, nqm), v3(QKs.ap, nqm)
            for h0 in range(0, nq, 4):
                hn = min(4, nq - h0)
                bG, bQ = nbank(), nbank()
                for hh in range(hn):
                    hq = h0 + hh
                    k.mm(bG, v3(bG.ap, 4)[r, hh, 0:C], kT[j], k3[:, hq, 0:C], kT[j], k3[:, hq, 0:C])
                    k.mm(bQ, v3(bQ.ap, 4)[r, hh, 0:C], kT[j], k3[:, hq, 0:C], qT[j], q3[:, hq, 0:C])
                k.copy("scalar", Gs, Gs3[r, h0:h0 + hn, 0:C], bG, v3(bG.ap, 4)[r, 0:hn, 0:C])
                k.copy("scalar", QKs, QK3[r, h0:h0 + hn, 0:C], bQ, v3(bQ.ap, 4)[r, 0:hn, 0:C])
            def group(gq, ws, order):
                dec, decT, eR, tG = dec_[ws], decT_[ws], eR_[ws], tG_[ws]
                X, Xt, Tt, TtB = X_[ws], Xt_[ws], Tt_[ws], TtB_[ws]
                PT, bv, bk, kdec, qgT = PT_[ws], bv_[ws], bk_[ws], kdec_[ws], qgT_[ws]
                u0s, wkT, ub = u0s_[ws], wkT_[ws], ub_[ws]
                hvs = [4 * gq + i_ for i_ in range(4)]
                bR = nbank()
                R4 = v3(bR.ap, 4)
                for i_, hv in enumerate(hvs):
                    k.mm(bR, R4[:, i_, 0:C], sel, sel3[0:nv, hv, :], cgT, cgT.ap[0:nv, 0:C])
                dec4, decT4, eR4, tG4 = v4(dec, C), v4(decT, C), v3(eR.ap, 4), v4(tG, C)
                for i_, hv in enumerate(hvs):
                    cgc = cg.ap[r, hv:hv + 1]
                    k.ts("vector", dec, dec4[:, i_, 0:C], bR, R4[r, i_, 0:C], cgc, ALU.subtract, 0.0, ALU.max, sbufs=[cg])
                    k.ts("gpsimd" if False else "vector", decT, decT4[:, i_, 0:C], bR, R4[r, i_, 0:C], cgc, ALU.subtract, 0.0,
                         ALU.min, sbufs=[cg])
                k.act(dec, dec4[:, :, 0:C], dec, dec4[:, :, 0:C], AF.Exp, scale=-1.0)
                k.act(decT, decT4[:, :, 0:C], decT, decT4[:, :, 0:C], AF.Exp)
                k.act(eR, eR4[:, :, 0:C], bR, R4[:, :, 0:C], AF.Exp)
                yield
                X4 = [v4(X[0], C), v4(X[1], C)]
                Xt4 = [v4(Xt[0], C), v4(Xt[1], C)]
                Tt4 = [v4(Tt[0], C), v4(Tt[1], C)]
                for i_, hv in enumerate(hvs):
                    hq = hv // 2
                    k.tt("gpsimd", tG, tG4[:, i_, 0:C], Gs, Gs3[r, hq, 0:C], dec, dec4[:, i_, 0:C], ALU.mult)
                    k.stt(X[0], X4[0][:, i_, 0:C], tG, tG4[:, i_, 0:C], nbeta.ap[r, hv:hv + 1], ALU.mult,
                          Ls, Ls.ap[r, r], ALU.mult, sbufs=[nbeta])
                bT = nbank()
                bTv = v3(bT.ap, 4)
                for i_ in range(4):
                    k.tr(bT, bTv[r, i_, 0:C], X[0], X4[0][:, i_, 0:C], identf, identf.ap[r, r])
                k.copy("scalar", Xt[0], Xt4[0][:, :, 0:C], bT, bTv[r, :, 0:C])
                yield
                idb = identf.ap[r, r].rearrange("p (o e) -> p o e", o=1).to_broadcast([C, 4, C])
                k.tt("vector", Tt[0], Tt4[0][:, :, 0:C], Xt[0], Xt4[0][:, :, 0:C], identf, idb, ALU.add)
                cur = 0
                for lv in range(1, levels + 1):
                    nxt = 1 - cur
                    bX = nbank()
                    for i_ in range(4):
                        k.mm(bX, v3(bX.ap, 4)[r, i_, 0:C], Xt[cur], Xt4[cur][:, i_, 0:C], X[cur], X4[cur][:, i_, 0:C])
                    k.copy("scalar", X[nxt], X4[nxt][:, :, 0:C], bX, v3(bX.ap, 4)[r, :, 0:C])
                    yield
                    if lv < levels:
                        bXt = nbank()
                        for i_ in range(4):
                            k.mm(bXt, v3(bXt.ap, 4)[r, i_, 0:C], X[cur], X4[cur][:, i_, 0:C], Xt[cur], Xt4[cur][:, i_, 0:C])
                        k.copy("gpsimd" if False else "vector", Xt[nxt], Xt4[nxt][:, :, 0:C], bXt, v3(bXt.ap, 4)[r, :, 0:C])
                    bD = nbank()
                    for i_ in range(4):
                        k.mm(bD, v3(bD.ap, 4)[r, i_, 0:C], X[nxt], X4[nxt][:, i_, 0:C], Tt[cur], Tt4[cur][:, i_, 0:C])
                    k.tt("vector", Tt[nxt], Tt4[nxt][:, :, 0:C], bD, v3(bD.ap, 4)[r, :, 0:C], Tt[cur], Tt4[cur][:, :, 0:C], ALU.add)
                    cur = nxt
                    yield
                TtF, TtF4 = TtB, v4(TtB, C)
                k.copy("vector", TtB, TtF4[:, :, 0:C], Tt[cur], Tt4[cur][:, :, 0:C])
                PT4, bv4, bk4, kd4, qg4 = v4(PT, C), v4(bv, C), v4(bk, C), v4(kdec, C), v3(qgT.ap, 4)
                for i_, hv in enumerate(hvs):
                    hq = hv // 2
                    k.tt("gpsimd", tG, tG4[:, i_, 0:C], QKs, QK3[r, hq, 0:C], decT, decT4[:, i_, 0:C], ALU.mult)
                    k.ts("vector", bv, bv4[:, i_, :], vt[j], vt3[r, hv, :], bg[j].ap[r, hv:hv + 1], ALU.mult, sbufs=[bg[j]])
                    k.ts("vector", bk, bk4[:, i_, :], kt[j], kt3[r, hq, :], bec.ap[r, hv:hv + 1], ALU.mult, sbufs=[bec])
                    k.ts("gpsimd", kdec, kd4[:, i_, :], kt[j], kt3[r, hq, :], decT4[:, i_, C - 1:C], ALU.mult, sbufs=[decT])
                    k.tt("gpsimd", qgT, qg4[:, i_, 0:C], qT[j], q3[:, hq, 0:C], eR, eR4[:, i_, 0:C], ALU.mult)
                umb = Umb.ap[r, r].rearrange("p (o e) -> p o e", o=1).to_broadcast([C, 4, C])
                k.tt("vector", PT, PT4[:, :, 0:C], tG, tG4[:, :, 0:C], Umb, umb, ALU.mult)
                yield
                bU = nbank()
                bW = nbank()
                for i_ in range(4):
                    k.mm(bU, v3(bU.ap, 4)[r, i_, :], TtF, TtF4[:, i_, 0:C], bv, bv4[:, i_, :])
                    k.mm(bW, v3(bW.ap, 4)[:, i_, 0:C], bk, bk4[:, i_, :], TtF, TtF4[:, i_, 0:C])
                k.copy("scalar", u0s, v4(u0s, C), bU, v3(bU.ap, 4)[r])
                wk4 = v3(wkT.ap, 4)
                k.copy("scalar", wkT, wk4[:, :, 0:C], bW, v3(bW.ap, 4)[:, :, 0:C])
                yield
                if order is not None:
                    while done[0] < order:
                        yield
                bS = nbank()
                for i_, hv in enumerate(hvs):
                    k.mm(bS, v3(bS.ap, 4)[r, i_, :], wkT, wk4[:, i_, 0:C], Sb, Sb3[:, hv, :])
                ub4 = v4(ub, C)
                k.tt("vector", ub, ub4, u0s, v4(u0s, C), bS, v3(bS.ap, 4)[r], ALU.subtract)
                yield
                bO = nbank()
                for i_, hv in enumerate(hvs):
                    k.mm(bO, v3(bO.ap, 4)[r, i_, :], qgT, qg4[:, i_, 0:C], Sb, Sb3[:, hv, :], start=True, stop=False)
                    k.mm(bO, v3(bO.ap, 4)[r, i_, :], PT, PT4[:, i_, 0:C], ub, ub4[:, i_, :], start=False, stop=True)
                oj = otm_[ws]
                k.copy("scalar", oj, v4(oj, C), bO, v3(bO.ap, 4)[r])
                ob, oap = o_dst(gq)
                k.dma(ob, oap, oj, v4(oj, C))
                yield
                bN = nbank()
                for i_, hv in enumerate(hvs):
                    k.mm(bN, v3(bN.ap, 4)[:, i_, :], kdec, kd4[:, i_, :], ub, ub4[:, i_, :])
                for i_, hv in enumerate(hvs):
                    k.stt(S, S3[:, hv, :], S, S3[:, hv, :], eR4[:, i_, C - 1:C], ALU.mult, bN, v3(bN.ap, 4)[:, i_, :], ALU.add,
                          sbufs=[eR])
                k.copy("scalar", Sb, Sb3[:, 4 * gq:4 * gq + 4, :], S, S3[:, 4 * gq:4 * gq + 4, :])
                if order is not None:
                    done[0] += 1

            return [group(gq, ws_fn(gq), order) for gq in range(nv // 4)]

        def step_all(active):
            for g_ in list(active):
                try:
                    next(g_)
                except StopIteration:
                    active.remove(g_)

        CH = 64
        if prompt:
            k.memset("vector", S, S.ap, 0.0)
            k.memset("vector", Sb, Sb.ap, 0.0)
            active = []
            for n in range(SEQ // CH):
                tok = slice(n * CH, (n + 1) * CH)

                def loads(qb, q3, kb, k3, ktb, kt3, vtb, vt3, bgb, n=n, tok=tok):
                    k.dma(qb, q3[:, :, 0:CH], qTb, qTb.ap[:, :, tok].rearrange("h p t -> p h t"))
                    k.dma(kb, k3[:, :, 0:CH], kTb, kTb.ap[:, :, tok].rearrange("h p t -> p h t"))
                    k.dma(ktb, kt3[0:CH], ktm, ktm.ap[:, tok, :].rearrange("h t d -> t h d"))
                    k.dma(vtb, vt3[0:CH], vtm, vtm.ap[:, tok, :].rearrange("h t d -> t h d"))
                    r0 = (n % 2) * CH
                    k.dma(bgb, bgb.ap[0:CH, 0:8], bg_scr, bg_scr.ap[n // 2][r0:r0 + CH, :])

                pc_, t0_ = (n * CH) // 512, (n * CH) % 512
                while len(active) >= 4:
                    step_all(active)
                active += chunk(CH, 5, n, loads,
                                lambda gq, pc_=pc_, t0_=t0_: (otm_m[pc_], otm_m[pc_].ap[t0_:t0_ + CH, :].rearrange("t (h d) -> t h d", h=4)),
                                2, 4, lambda gq, n=n: n % 4, order=n)
                step_all(active)
                if t0_ + CH == 512:
                    while active:
                        step_all(active)
                    k.allgather(otm_g[pc_], otm_g[pc_].ap, otm_m[pc_], otm_m[pc_].ap, CC_GROUPS)
            k.dma(ssm_p_out, ssm_p_out.ap.rearrange("h k v -> k h v"), S, S3[:, 0:4, :])
        else:
            for s in range(SPC):
                k.dma(S, S3, state_b_ssm, state_b_ssm.ap[s].rearrange("h k v -> k h v"))
                k.copy("scalar", Sb, Sb.ap, S, S.ap)

                def loads(qb, q3, kb, k3, ktb, kt3, vtb, vt3, bgb, s=s):
                    c8 = slice(s * DEC_T, (s + 1) * DEC_T)
                    k.dma(qb, q3[:, :, 0:DEC_T], qTbs, qTbs.ap[:, :, c8].rearrange("h p t -> p h t"))
                    k.dma(kb, k3[:, :, 0:DEC_T], kTbs, kTbs.ap[:, :, c8].rearrange("h p t -> p h t"))
                    k.dma(ktb, kt3[0:DEC_T], ktms, ktms.ap[:, s].rearrange("h t d -> t h d"))
                    k.dma(vtb, vt3[0:DEC_T], vtms, vtms.ap[:, s].rearrange("h t d -> t h d"))
                    k.dma(bgb, bgb.ap[0:DEC_T, :], bgs_scr, bgs_scr.ap[s])

                gens = chunk(DEC_T, 2, s, loads,
                             lambda gq, s=s: (otms_scr, otms_scr.ap[s][:, gq * 512:(gq + 1) * 512].rearrange("t (h d) -> t h d", h=4)),
                             NQ, NV, lambda gq: gq)
                while gens:
                    step_all(gens)
                k.dma(ssm_s_out, ssm_s_out.ap[s].rearrange("h k v -> k h v"), S, S3)

    def phase_b3(xin_p, xin_s, xout_p, xout_s):
        k.phase()
        Wz = k.alloc("Wz", 8 * 2048, BF16)
        Wz3 = v3(Wz.ap, 8)
        bw = b_w_in.ap.rearrange("(k p) n -> p k n", p=128)
        for c0 in range(0, 2048, 1024):
            k.dma(Wz, Wz3[:, :, c0:c0 + 1024], b_w_in, bw[:, :, B_CONV_DIM + c0:B_CONV_DIM + c0 + 1024], eng="gpsimd")
        Wo = k.alloc("Wo", NV * D, BF16)
        Wo3 = v3(Wo.ap, NV)
        k.dma(Wo, Wo3, b_w_out, b_w_out.ap.rearrange("(c p) n -> p c n", p=128), eng="gpsimd")
        gn = k.alloc("gn", 128)
        k.dma(gn, gn.ap, b_norm_g, b_norm_g.ap.partition_broadcast(128))
        G = k.alloc("G", D)
        k.dma(G, G.ap, mod_scr, mod_scr.ap[1][0:1, 2 * D:3 * D].partition_broadcast(128))
        Gs = [k.alloc(f"Gs{i}", D) for i in range(2)]
        ht = [k.alloc(f"ht{i}", 8 * 128, BF16) for i in range(2)]
        ot = [k.alloc(f"ot{i}", 2048) for i in range(2)]
        sqo = k.alloc("sqo", 2048)
        ssq = k.alloc("ssq", 16)
        sdv = k.alloc("sdv", 16)
        sz = k.alloc("sz", 2048)
        of = [k.alloc(f"of{i}", 2048, BF16) for i in range(2)]
        oT = [k.alloc(f"oT{i}", NV * 128, BF16) for i in range(2)]
        xt = [k.alloc(f"xt{i}", D) for i in range(2)]
        yt = [k.alloc(f"yt{i}", D) for i in range(2)]
        tiles = [(128, i, None) for i in range(NT)] + [(DEC_T, None, s) for s in range(SPC)]
        for it, (rows, ti, si) in enumerate(tiles):
            j = it % 2
            r = slice(0, rows)
            h3 = v3(ht[j].ap, 8)
            if si is None:
                tok = slice(ti * 128, (ti + 1) * 128)
                k.dma(ht[j], h3, hTp, hTp.ap[:, :, tok])
                og_ = otm_g[ti // 4]
                r0_ = (ti % 4) * 128
                k.dma(ot[j], v3(ot[j].ap, 4), og_, og_.ap.rearrange("(r t) c -> t r c", r=4)[r0_:r0_ + 128])
                k.dma(xt[j], xt[j].ap, xin_p, xin_p.ap[tok, :])
                Gb = G
            else:
                k.dma(ht[j], h3[:, :, 0:rows], hTs, hTs.ap[:, :, si * DEC_T:(si + 1) * DEC_T])
                k.dma(ot[j], ot[j].ap[r, :], otms_scr, otms_scr.ap[si])
                k.dma(xt[j], xt[j].ap[r, :], xin_s, xin_s.ap[si])
                Gb = Gs[si % 2]
                k.dma(Gb, Gb.ap[r, :], mod_scr, mod_scr.ap[1][1 + si:2 + si, 2 * D:3 * D].partition_broadcast(rows))
            for q in range(4):
                bank = nbank()
                for kc in range(8):
                    k.mm(bank, bank.ap[r, :], ht[j], h3[:, kc, 0:rows], Wz, Wz3[:, kc, q * 512:(q + 1) * 512],
                         start=(kc == 0), stop=(kc == 7))
                k.act(sz, sz.ap[r, q * 512:(q + 1) * 512], bank, bank.ap[r, :], AF.Silu)
            o3 = v3(ot[j].ap, NV)[r]
            k.tt("gpsimd", sqo, sqo.ap[r, :], ot[j], ot[j].ap[r, :], ot[j], ot[j].ap[r, :], ALU.mult)
            k.op("vector", (lambda o_, i_: (lambda e: e.tensor_reduce(out=o_, in_=i_, op=ALU.add, axis=AX.X)))(
                ssq.ap[r, :], v3(sqo.ap, NV)[r]), reads=[sqo], writes=[ssq])
            k.act(sdv, sdv.ap[r, :], ssq, ssq.ap[r, :], AF.Sqrt, scale=1.0 / 128, bias=epsb.ap[r, :], sbufs=[epsb])
            recip("vector", sdv, sdv.ap[r, :], sdv, sdv.ap[r, :])
            rb = sdv.ap[r, :].rearrange("p (h o) -> p h o", o=1).to_broadcast([rows, NV, 128])
            gb_ = gn.ap[r, :].rearrange("p (o e) -> p o e", o=1).to_broadcast([rows, NV, 128])
            k.tt("vector", sqo, v3(sqo.ap, NV)[r], ot[j], o3, sdv, rb, ALU.mult)
            k.tt("gpsimd", sqo, v3(sqo.ap, NV)[r], sqo, v3(sqo.ap, NV)[r], gn, gb_, ALU.mult)
            k.tt("vector", of[j], of[j].ap[r, :], sqo, sqo.ap[r, :], sz, sz.ap[r, :], ALU.mult)
            oT3 = v3(oT[j].ap, NV)
            for half in range(2):
                bank = nbank()
                bv_ = v3(bank.ap.bitcast(BF16), 8)
                for c in range(8):
                    cc = half * 8 + c
                    k.tr(bank, bv_[:, c, 0:rows], of[j], of[j].ap[r, cc * 128:(cc + 1) * 128], identb, identb.ap[r, r])
                k.copy("scalar", oT[j], oT3[:, half * 8:half * 8 + 8, 0:rows], bank, bv_[:, :, 0:rows])
            for half in range(2):
                bank = nbank()
                for c in range(NV):
                    k.mm(bank, bank.ap[r, :], oT[j], oT3[:, c, 0:rows], Wo, Wo3[:, c, half * 512:(half + 1) * 512],
                         start=(c == 0), stop=(c == NV - 1))
                hs = slice(half * 512, (half + 1) * 512)
                k.tt("vector", yt[j], yt[j].ap[r, hs], bank, bank.ap[r, :], Gb, Gb.ap[r, hs], ALU.mult)
                k.tt("gpsimd", yt[j], yt[j].ap[r, hs], yt[j], yt[j].ap[r, hs], xt[j], xt[j].ap[r, hs], ALU.add)
            if si is None:
                k.dma(xout_p, xout_p.ap[tok, :], yt[j], yt[j].ap)
            else:
                k.dma(xout_s, xout_s.ap[si], yt[j], yt[j].ap[r, :])

    def phase_final(xin_p, xin_s):
        k.phase()
        gf = k.alloc("gf", D)
        k.dma(gf, gf.ap, norm_final_g, norm_final_g.ap.partition_broadcast(128))
        xt = [k.alloc(f"xt{i}", D) for i in range(2)]
        sq = k.alloc("sq", D)
        ssq = [k.alloc(f"ssq{i}", 1) for i in range(2)]
        yt = [k.alloc(f"yt{i}", D) for i in range(2)]
        tiles = [(128, i, None) for i in range(NT)] + [(DEC_T, None, s) for s in range(SPC)]
        for it, (rows, ti, si) in enumerate(tiles):
            j = it % 2
            r = slice(0, rows)
            if si is None:
                k.dma(xt[j], xt[j].ap, xin_p, xin_p.ap[ti * 128:(ti + 1) * 128, :])
            else:
                k.dma(xt[j], xt[j].ap[r, :], xin_s, xin_s.ap[si])
            k.tt("gpsimd", sq, sq.ap[r, :], xt[j], xt[j].ap[r, :], xt[j], xt[j].ap[r, :], ALU.mult)
            k.op("vector", (lambda o_, i_: (lambda e: e.tensor_reduce(out=o_, in_=i_, op=ALU.add, axis=AX.X)))(
                ssq[j].ap[r, :], sq.ap[r, :]), reads=[sq], writes=[ssq[j]])
            k.act(ssq[j], ssq[j].ap[r, :], ssq[j], ssq[j].ap[r, :], AF.Sqrt, scale=1.0 / D, bias=epsb.ap[r, :], sbufs=[epsb])
            recip("vector", ssq[j], ssq[j].ap[r, :], ssq[j], ssq[j].ap[r, :])
            k.stt(yt[j], yt[j].ap[r, :], xt[j], xt[j].ap[r, :], ssq[j].ap[r, 0:1], ALU.mult, gf, gf.ap[r, :], ALU.mult,
                  sbufs=[ssq[j]])
            if si is None:
                k.dma(y_p_out, y_p_out.ap[ti * 128:(ti + 1) * 128, :], yt[j], yt[j].ap)
            else:
                k.dma(y_s_out, y_s_out.ap[si], yt[j], yt[j].ap[r, :])

    import os
    nph = int(os.environ.get("KSTAGE", "99"))
    for i_, ph_ in enumerate([phase_attn, phase_attn_sample, phase_combine, phase_out_a,
                                lambda: phase_norm(0, x1p, x1s, 4 * D, 3 * D),
                                lambda: phase_ffn(0, x1p, x1s, x2p, x2s),
                                lambda: phase_norm(1, x2p, x2s, 1 * D, 0),
                                phase_b1, lambda: b2_run(True), lambda: b2_run(False),
                                lambda: phase_b3(x2p, x2s, x3p, x3s),
                                lambda: phase_norm(1, x3p, x3s, 4 * D, 3 * D),
                                lambda: phase_ffn(1, x3p, x3s, x4p, x4s),
                                lambda: phase_final(x4p, x4s)]):
        if i_ < nph:
            ph_()

    with stack:
        k.emit()
    return nc


_NC = {}


def _rope_tables():
    half = DH // 2
    inv = THETA ** (-np.arange(half, dtype=np.float32) / half)
    tabs = np.zeros((3, NT, 128, 64), np.float32)
    for g, (_w, d) in enumerate(A_GROUPS):
        nb = group_blocks(d)
        for sb in range(4):
            for r_ in range(d):
                for b_ in range(nb):
                    n = sb * 16 + r_ * nb + b_
                    pos = (sb * 2048 + d * (b_ * 128 + np.arange(128)) + r_).astype(np.float32)
                    ang = pos[:, None] * inv[None, :]
                    tabs[g, n, :, :32] = np.cos(ang)
                    tabs[g, n, :, 32:] = np.sin(ang)
    pos = (PAST + np.arange(DEC_T)).astype(np.float32)
    ang = pos[:, None] * inv[None, :]
    ts = np.concatenate([np.cos(ang), np.sin(ang)], axis=1).astype(np.float32)
    return tabs, ts


def _masks():
    u = np.arange(128)[:, None]
    v = np.arange(128)[None, :]
    own = (u <= v).astype(np.float32)
    prev = (u >= v).astype(np.float32)
    mask4 = np.concatenate([own, prev, own, prev], axis=1)
    E = np.zeros((65, 64), np.float32)
    E[64, :] = 1.0
    ms = np.zeros((3, 8, 128, 8), np.float32)
    mn = np.zeros((3, 8, 8), np.float32)
    for g, (_w, d) in enumerate(A_GROUPS):
        for t0 in range(min(d, DEC_T)):
            for t in range(t0, DEC_T, d):
                a = (t - t0) // d
                ms[g, t0, a:, t] = 1.0
        for tk in range(DEC_T):
            for t in range(tk, DEC_T):
                if (t - tk) % d == 0:
                    mn[g, tk, t] = 1.0
    return dict(mask4=mask4, Emat=E, msk_s=ms, msk_n=mn)


def kernel(**inp):
    f32 = np.float32
    stage = 1
    if stage not in _NC:
        _NC[stage] = build_program(stage)
    nc = _NC[stage]
    g = lambda n: np.asarray(inp[n], dtype=f32)
    rope_p, rope_s = _rope_tables()
    shared = dict(
        w_mod=g("w_mod"), b_mod=g("b_mod"), norm_mix_g=g("norm_mix_g"), norm_ffn_g=g("norm_ffn_g"),
        a_w_in=g("a_w_in")[0], identf=np.eye(128, dtype=f32), rope_p=rope_p, rope_s=rope_s,
        a_w_out=g("a_w_out")[0], **_masks(),
        ffn_w_up=g("ffn_w_up"), ffn_w_down=g("ffn_w_down"),
    )
    NC_ = D_FF // 128
    cwb = np.concatenate([g("ffn_conv_w"), g("ffn_conv_b")[:, None, :]], axis=1)
    shared["ffn_cw"] = np.ascontiguousarray(cwb.reshape(2, 4, NC_, 128).transpose(0, 3, 2, 1))
    sfc = g("state_ffn_conv")
    u_ = np.arange(128)
    shared.update(
        b_w_in=g("b_w_in")[0], b_w_out=g("b_w_out")[0],
        b_cw=np.ascontiguousarray(g("b_conv_w")[0].reshape(4, 32, 128).transpose(2, 1, 0)),
        b_dt_bias=g("b_dt_bias").reshape(1, 16), b_a_log=g("b_a_log").reshape(1, 16),
        b_norm_g=g("b_norm_g").reshape(1, 128), norm_final_g=g("norm_final_g").reshape(1, D),
        Umat=(u_[:, None] <= u_[None, :]).astype(f32), Lstrict=(u_[None, :] < u_[:, None]).astype(f32),
        selm=np.ascontiguousarray(np.repeat(np.eye(16, dtype=f32)[:, :, None], 128, axis=2).reshape(16, 2048)),
    )
    sbc = g("state_b_conv")[0]
    ssm0 = g("state_b_ssm")[0]
    xprompt, xsample, cp, cs_ = g("x_prompt"), g("x_sample"), g("c_prompt"), g("c_sample")
    in_maps = []
    for c in range(NCORES):
        m = dict(shared)
        sl = slice(c * SPC, (c + 1) * SPC)
        seq_, rk = c // 4, c % 4
        m["xp"] = xprompt[seq_]
        m["xs"] = np.ascontiguousarray(xsample[sl])
        cpc = cp[seq_]
        awi = shared["a_w_in"].reshape(D, 3, 3, NH, DH)[:, :, :, rk * HM:(rk + 1) * HM, :]
        m["a_w_in_m"] = np.ascontiguousarray(awi.reshape(D, 9 * HM * DH))
        bwi = shared["b_w_in"]
        qh_, vh_ = slice(2 * rk * 128, (2 * rk + 2) * 128), slice(4 * rk * 128, (4 * rk + 4) * 128)
        m["b_w_in_m"] = np.ascontiguousarray(np.concatenate(
            [bwi[:, 0:1024][:, qh_], bwi[:, 1024:2048][:, qh_], bwi[:, 2048:4096][:, vh_],
             bwi[:, 6144 + 4 * rk:6144 + 4 * rk + 4], bwi[:, 6160 + 4 * rk:6160 + 4 * rk + 4]], axis=1))
        mych = [2 * rk, 2 * rk + 1, 8 + 2 * rk, 8 + 2 * rk + 1] + [16 + 4 * rk + i_ for i_ in range(4)]
        m["b_cw_m"] = np.ascontiguousarray(shared["b_cw"][:, mych, :])
        m["b_dt_bias_m"] = np.ascontiguousarray(shared["b_dt_bias"][:, 4 * rk:4 * rk + 4])
        m["b_a_log_m"] = np.ascontiguousarray(shared["b_a_log"][:, 4 * rk:4 * rk + 4])
        m["cT"] = np.ascontiguousarray(np.concatenate([cpc[None, :], cs_[sl]], axis=0).T)
        m["fconvT"] = np.ascontiguousarray(sfc[:, sl].reshape(2, SPC, 2, NC_, 128).transpose(0, 4, 3, 1, 2))
        m["bconvT"] = np.ascontiguousarray(sbc[sl].reshape(SPC, 3, 32, 128).transpose(3, 2, 0, 1))
        m["state_b_ssm"] = np.ascontiguousarray(ssm0[sl])
        for (w, _d) in A_GROUPS:
            for kv in ("k", "v"):
                a = g(f"cache_a_{kv}_w{w}")[0, sl]
                m[f"cache_{kv}_w{w}"] = np.ascontiguousarray(a.reshape(SPC, w, D))
        in_maps.append(m)
    res = run_bass_kernel_spmd(nc, in_maps, core_ids=list(range(NCORES))).results
    if DEBUG:
        global _DBG
        _DBG = res

    def cat_s(name, tail):
        return np.concatenate([res[c][name] for c in range(NCORES)], axis=0).reshape((1, DEC_B) + tail)

    PC = (0, 4)

    def cat_p(name, tail):
        per_seq = [np.concatenate([res[4 * b + r_][name].reshape(tail[0], HM, DH) for r_ in range(4)], axis=1) for b in range(2)]
        return np.stack(per_seq, axis=0).reshape((1, 2) + tail)

    o = {}
    for (w, _d) in A_GROUPS:
        for kv in ("k", "v"):
            o[f"{kv}{w}s"] = cat_s(f"new_{kv}_w{w}_s", (w, NH, DH))
            o[f"{kv}{w}p"] = cat_p(f"new_{kv}_w{w}_p", (w, NH, DH))
    z = lambda *s: np.zeros(s, f32)
    NC_ = D_FF // 128
    fcp = np.stack([res[c]["fconv_p"].reshape(2, 128, NC_, 2) for c in PC], axis=1)
    fcp = np.ascontiguousarray(fcp.transpose(0, 1, 4, 3, 2)).reshape(2, 2, 2, D_FF)
    fcs = np.stack([res[c]["fconv_s"].reshape(2, 128, NC_, SPC, 2) for c in range(NCORES)], axis=1)
    fcs = np.ascontiguousarray(fcs.transpose(0, 1, 4, 5, 3, 2)).reshape(2, DEC_B, 2, D_FF)
    y_p = np.stack([res[c]["y_p"] for c in PC], axis=0)
    y_s = np.concatenate([res[c]["y_s"] for c in range(NCORES)], axis=0)
    ssm_p = np.stack([np.concatenate([res[4 * b + r_]["ssm_p"] for r_ in range(4)], axis=0) for b in range(2)], axis=0)[None]
    ssm_s = np.concatenate([res[c]["ssm_s"] for c in range(NCORES)], axis=0)[None]
    bcp = np.zeros((2, 128, 32, 3), f32)
    for b in range(2):
        for r_ in range(4):
            mych = [2 * r_, 2 * r_ + 1, 8 + 2 * r_, 8 + 2 * r_ + 1] + [16 + 4 * r_ + i_ for i_ in range(4)]
            bcp[b][:, mych, :] = res[4 * b + r_]["bconv_p"].reshape(128, 8, 3)
    bcp = np.ascontiguousarray(bcp.transpose(0, 3, 2, 1)).reshape(1, 2, 3, B_CONV_DIM)
    bcs = np.stack([res[c]["bconv_s"].reshape(128, 32, SPC, 3) for c in range(NCORES)], axis=0)
    bcs = np.ascontiguousarray(bcs.transpose(0, 3, 4, 2, 1)).reshape(1, DEC_B, 3, B_CONV_DIM)
    return (
        y_p, y_s,
        o["k128p"], o["k128s"], o["v128p"], o["v128s"],
        o["k512p"], o["k512s"], o["v512p"], o["v512s"],
        o["k2048p"], o["k2048s"], o["v2048p"], o["v2048s"],
        ssm_p, ssm_s,
        bcp, bcs,
        fcp, fcs,
    )
```

```python
from contextlib import ExitStack

import numpy as np
import concourse.bass as bass
import concourse.mybir as mybir
from concourse.bass_utils import run_bass_kernel_spmd

F32 = mybir.dt.float32
BF16 = mybir.dt.bfloat16
ALU = mybir.AluOpType
AF = mybir.ActivationFunctionType
AX = mybir.AxisListType

NCORES = 8
D = 1024
SEQ = 8192
NT = SEQ // 128
DEC_B = 32
DEC_T = 8
SPC = DEC_B // NCORES
PAST = 16384
A_GROUPS = ((128, 1), (512, 4), (2048, 16))
NH = 16
DH = 64
D_FF = 2816
B_CONV_DIM = 4096
EPS = 1e-6
THETA = 10000.0

HM = 4
CC_GROUPS = [[0, 1, 2, 3], [4, 5, 6, 7]]
SEM_BLOCK = 8000
N_DMA_SEMS = 20
ARENA_F32 = 48 * 1024
DEBUG = False


class Buf:
    def __init__(self, ap, name=""):
        self.ap = ap
        self.name = name
        self.last_write = None
        self.reads = []
        self.sb = False


class K:
    ENGS = ("tensor", "vector", "scalar", "gpsimd", "sync")

    def __init__(self, nc, stack):
        self.nc = nc
        self.stack = stack
        self.ops = []
        self.arena = stack.enter_context(nc.sbuf_tensor("arena", [128, ARENA_F32], F32))
        self.psum_t = stack.enter_context(nc.psum_tensor("psum", [128, 8, 512], F32))
        self.banks = [Buf(self.psum_t[:, i, :], f"bank{i}") for i in range(8)]
        self.top = 0
        self.persist = 0

    def alloc(self, name, cols, dt=F32):
        words = cols if dt == F32 else (cols + 1) // 2
        a = self.top
        self.top += words
        assert self.top <= ARENA_F32, (name, self.top)
        ap = self.arena[:, a:a + words]
        if dt != F32:
            ap = ap.bitcast(dt)[:, 0:cols]
        b = Buf(ap, name)
        b.sb = True
        return b

    def keep(self):
        self.persist = self.top

    def phase(self):
        self.ops.append(dict(barrier=True))
        self.top = self.persist

    def dram(self, name, shape, dt=F32, kind="Internal", **kw):
        return Buf(self.nc.dram_tensor(name, list(shape), dt, kind=kind, **kw).ap(), name)

    def op(self, eng, fn, reads=(), writes=(), dma=False):
        self.ops.append(dict(eng=eng, fn=fn, reads=list(reads), writes=list(writes), dma=dma))

    def dma(self, out_b, out_ap, in_b, in_ap, eng="sync"):
        if eng == "sync" and out_b.sb and not in_b.sb:
            eng = "scalar"
        self.op(eng, lambda e: e.dma_start(out=out_ap, in_=in_ap), reads=[in_b], writes=[out_b], dma=True)

    def mm(self, out_b, out_ap, lb, lap, rb, rap, start=True, stop=True):
        self.op("tensor", lambda e: e.matmul(out_ap, lhsT=lap, rhs=rap, start=start, stop=stop),
                reads=[lb, rb], writes=[out_b])

    def tr(self, out_b, out_ap, in_b, in_ap, ident_b, ident_ap):
        self.op("tensor", lambda e: e.transpose(out=out_ap, in_=in_ap, identity=ident_ap),
                reads=[in_b, ident_b], writes=[out_b])

    def tt(self, eng, out_b, out_ap, a_b, a_ap, b_b, b_ap, op):
        self.op(eng, lambda e: e.tensor_tensor(out=out_ap, in0=a_ap, in1=b_ap, op=op),
                reads=[a_b, b_b], writes=[out_b])

    def ts(self, eng, out_b, out_ap, a_b, a_ap, s1, op0, s2=None, op1=None, sbufs=()):
        kw = dict(out=out_ap, in0=a_ap, scalar1=s1, scalar2=s2, op0=op0)
        if op1 is not None:
            kw["op1"] = op1
        self.op(eng, lambda e: e.tensor_scalar(**kw), reads=[a_b, *sbufs], writes=[out_b])

    def stt(self, out_b, out_ap, a_b, a_ap, scalar, op0, b_b, b_ap, op1, sbufs=()):
        self.op("vector", lambda e: e.scalar_tensor_tensor(out=out_ap, in0=a_ap, scalar=scalar, op0=op0,
                                                           in1=b_ap, op1=op1),
                reads=[a_b, b_b, *sbufs], writes=[out_b])

    def act(self, out_b, out_ap, in_b, in_ap, func, scale=1.0, bias=None, sbufs=()):
        kw = dict(out=out_ap, in_=in_ap, func=func, scale=scale)
        if bias is not None:
            kw["bias"] = bias
        self.op("scalar", lambda e: e.activation(**kw), reads=[in_b, *sbufs], writes=[out_b])

    def copy(self, eng, out_b, out_ap, in_b, in_ap):
        if eng == "scalar":
            self.act(out_b, out_ap, in_b, in_ap, AF.Copy)
        else:
            self.op(eng, lambda e: e.tensor_copy(out=out_ap, in_=in_ap), reads=[in_b], writes=[out_b])

    def allgather(self, out_b, out_ap, in_b, in_ap, groups):
        self.op("gpsimd", lambda e: e.collective_compute("AllGather", ALU.bypass, replica_groups=groups,
                                                         ins=[in_ap], outs=[out_ap]),
                reads=[in_b], writes=[out_b], dma=True)
        self.ops[-1]["cc"] = True

    def memset(self, eng, out_b, out_ap, val):
        self.op(eng, lambda e: e.memset(out_ap, val), writes=[out_b])

    def emit(self):
        nc = self.nc
        ops = []
        pending = {}
        last_eng = {}
        last_dma = {}
        dma_rr = {e: 0 for e in self.ENGS}
        for o in self.ops:
            if o.get("barrier"):
                b = set(last_eng.values()) | set(last_dma.values())
                for e in self.ENGS:
                    pending[e] = set(pending.get(e, set())) | b
                continue
            i = len(ops)
            ops.append(o)
            deps = set()
            for bf in o["reads"]:
                if bf.last_write is not None:
                    deps.add(bf.last_write)
            for bf in o["writes"]:
                if bf.last_write is not None:
                    deps.add(bf.last_write)
                deps.update(bf.reads)
            deps |= pending.pop(o["eng"], set())
            deps.discard(i)
            o["deps"] = deps
            for bf in o["reads"]:
                bf.reads.append(i)
            for bf in o["writes"]:
                bf.last_write = i
                bf.reads = []
            e = o["eng"]
            if o.get("cc"):
                o["slot"] = f"cc{i}"
                last_dma[(e, o["slot"])] = i
            elif o["dma"]:
                slot = dma_rr[e] % N_DMA_SEMS
                dma_rr[e] += 1
                o["slot"] = slot
                last_dma[(e, slot)] = i
            else:
                last_eng[e] = i
        n_eng = {e: 0 for e in self.ENGS}
        sems = {}
        dma_count = {}
        prev_on_sem = {}
        for o in ops:
            e = o["eng"]
            if o.get("cc"):
                o["sig"] = (f"d_{e}_{o['slot']}", 1, None)
                o["prev_same_sem"] = None
            elif o["dma"]:
                s = f"d_{e}_{o['slot']}"
                dma_count[s] = dma_count.get(s, 0) + 1
                o["sig"] = (s, 16 * dma_count[s], 16)
                o["prev_same_sem"] = prev_on_sem.get(s)
                prev_on_sem[s] = o
            else:
                kk = n_eng[e]
                n_eng[e] += 1
                o["sig"] = (f"c_{e}_{kk // SEM_BLOCK}", kk % SEM_BLOCK + 1, 1)
                o["prev_same_sem"] = None
            if o["sig"][0] not in sems:
                sems[o["sig"][0]] = self.stack.enter_context(nc.semaphore(o["sig"][0]))
        by_eng = {e: [o for o in ops if o["eng"] == e] for e in self.ENGS}
        final_waits = {}
        for o in ops:
            if o["dma"]:
                s, v, _ = o["sig"]
                final_waits[s] = max(final_waits.get(s, 0), v)
        print(f"[kernel] ops: " + ", ".join(f"{e}={len(by_eng[e])}" for e in self.ENGS) + f" sems={len(sems)}")

        def emit_engine(ename, eh):
            w = {}
            for o in by_eng[ename]:
                need = {}
                for d in o["deps"]:
                    od = ops[d]
                    if od["eng"] == ename and not od["dma"] and ename == "tensor":
                        continue
                    s, v, _ = od["sig"]
                    need[s] = max(need.get(s, 0), v)
                p = o["prev_same_sem"]
                if p is not None:
                    s, v, _ = p["sig"]
                    need[s] = max(need.get(s, 0), v)
                for s, v in need.items():
                    if w.get(s, 0) < v:
                        eh.wait_ge(sems[s], v)
                        w[s] = v
                ins = o["fn"](eh)
                s, v, inc = o["sig"]
                if inc is None:
                    ins.then_inc(sems[s])
                else:
                    ins.then_inc(sems[s], inc)
            if ename == "sync":
                for s, v in final_waits.items():
                    if w.get(s, 0) < v:
                        eh.wait_ge(sems[s], v)
                for en in self.ENGS:
                    if en == "sync" or not by_eng[en]:
                        continue
                    last = [o for o in by_eng[en] if not o["dma"]]
                    if last:
                        s, v, _ = last[-1]["sig"]
                        eh.wait_ge(sems[s], v)

        with nc.Block() as block:
            @block.tensor
            def _(e):
                emit_engine("tensor", e)

            @block.vector
            def _(e):
                emit_engine("vector", e)

            @block.scalar
            def _(e):
                emit_engine("scalar", e)

            @block.gpsimd
            def _(e):
                emit_engine("gpsimd", e)

            @block.sync
            def _(e):
                emit_engine("sync", e)


def v3(ap, a):
    return ap.rearrange("p (a b) -> p a b", a=a)


def group_blocks(d):
    return 16 // d


def build_program(stage):
    nc = bass.Bass("TRN2", target_bir_lowering=False)
    stack = ExitStack()
    k = K(nc, stack)
    B = k.banks

    def din(name, shape, dt=F32):
        return Buf(nc.dram_tensor(name, list(shape), dt, kind="ExternalInput").ap(), name)

    def dout(name, shape, dt=F32):
        return Buf(nc.dram_tensor(name, list(shape), dt, kind="ExternalOutput").ap(), name)

    xp = din("xp", [SEQ, D])
    xs = din("xs", [SPC, DEC_T, D])
    cT = din("cT", [D, 1 + SPC])
    w_mod = din("w_mod", [2, D, 6 * D])
    b_mod = din("b_mod", [2, 6 * D])
    norm_mix_g = din("norm_mix_g", [2, D])
    norm_ffn_g = din("norm_ffn_g", [2, D])
    a_w_in = din("a_w_in", [D, 9 * D])
    a_w_in_m = din("a_w_in_m", [D, 9 * HM * DH])
    identf_in = din("identf", [128, 128])
    rope_p = din("rope_p", [3, NT, 128, 64])
    rope_s = din("rope_s", [DEC_T, 64])
    cache = {}
    new_s = {}
    new_p = {}
    for (w, _d) in A_GROUPS:
        for kv in ("k", "v"):
            cache[(kv, w)] = din(f"cache_{kv}_w{w}", [SPC, w, D])
            new_s[(kv, w)] = dout(f"new_{kv}_w{w}_s", [SPC, w, D])
            new_p[(kv, w)] = dout(f"new_{kv}_w{w}_p", [w, HM * DH])

    mask4_in = din("mask4", [128, 512])
    E_in = din("Emat", [65, 64])
    msk_s_in = din("msk_s", [3, 8, 128, 8])
    msk_n_in = din("msk_n", [3, 8, 8])
    a_w_out = din("a_w_out", [D, D])
    b_w_in = din("b_w_in", [D, 6176])
    b_w_in_m = din("b_w_in_m", [D, 1032])
    b_cw_m = din("b_cw_m", [128, 8, 4])
    b_dt_bias_m = din("b_dt_bias_m", [1, 4])
    b_a_log_m = din("b_a_log_m", [1, 4])
    b_w_out = din("b_w_out", [2048, D])
    b_cw = din("b_cw", [128, 32, 4])
    bconvT = din("bconvT", [128, 32, SPC, 3])
    b_dt_bias = din("b_dt_bias", [1, 16])
    b_a_log = din("b_a_log", [1, 16])
    b_norm_g = din("b_norm_g", [1, 128])
    norm_final_g = din("norm_final_g", [1, D])
    Umat_in = din("Umat", [128, 128])
    Lstrict_in = din("Lstrict", [128, 128])
    sel_in = din("selm", [16, 16 * 128])
    state_b_ssm = din("state_b_ssm", [SPC, 16, 128, 128])
    bconv_p_out = dout("bconv_p", [128, 8 * 3])
    bconv_s_out = dout("bconv_s", [128, 32 * SPC * 3])
    ssm_p_out = dout("ssm_p", [4, 128, 128])
    ssm_s_out = dout("ssm_s", [SPC, 16, 128, 128])
    y_p_out = dout("y_p", [SEQ, D])
    y_s_out = dout("y_s", [SPC, DEC_T, D])
    ffn_w_up = din("ffn_w_up", [2, D, 2 * D_FF])
    ffn_w_down = din("ffn_w_down", [2, D_FF, D])
    ffn_cw = din("ffn_cw", [2, 128, D_FF // 128, 4])
    fconvT = din("fconvT", [2, 128, D_FF // 128, SPC, 2])
    fconv_p_out = dout("fconv_p", [2, 128, (D_FF // 128) * 2])
    fconv_s_out = dout("fconv_s", [2, 128, (D_FF // 128) * SPC * 2])

    dbg_kind = "ExternalOutput" if DEBUG else "Internal"
    o_scr = k.dram("o_scr", [3, HM, 65, SEQ])
    oT_mine = [k.dram(f"oT_mine{i}", [HM * 64, 2048], BF16) for i in range(4)]
    oT_scr = [k.dram(f"oT_scr{i}", [NH * 64, 2048], BF16, addr_space="Local") for i in range(4)]
    qTs_scr = k.dram("qTs_scr", [3, 64, NH, SPC * DEC_T], BF16)
    kTs_scr = k.dram("kTs_scr", [3, 64, NH, SPC * DEC_T], BF16)
    Vs_scr = k.dram("Vs_scr", [3, SPC, DEC_T, NH * 65], BF16)
    oTs_scr = k.dram("oTs_scr", [SPC, 64, NH * DEC_T], BF16)
    x1p = k.dram("x1p", [SEQ, D], kind=dbg_kind)
    x1s = k.dram("x1s", [SPC, DEC_T, D], kind=dbg_kind)
    x2p = k.dram("x2p", [SEQ, D], kind=dbg_kind)
    x2s = k.dram("x2s", [SPC, DEC_T, D], kind=dbg_kind)
    x3p = k.dram("x3p", [SEQ, D], kind=dbg_kind)
    x3s = k.dram("x3s", [SPC, DEC_T, D], kind=dbg_kind)
    x4p = k.dram("x4p", [SEQ, D], kind=dbg_kind)
    x4s = k.dram("x4s", [SPC, DEC_T, D], kind=dbg_kind)
    qTb = k.dram("qTb", [2, 128, SEQ], BF16)
    kTb = k.dram("kTb", [2, 128, SEQ], BF16)
    qTbs = k.dram("qTbs", [8, 128, SPC * DEC_T], BF16)
    kTbs = k.dram("kTbs", [8, 128, SPC * DEC_T], BF16)
    ktm = k.dram("ktm", [2, SEQ, 128], BF16)
    vtm = k.dram("vtm", [4, SEQ, 128], BF16)
    ktms = k.dram("ktms", [8, SPC, DEC_T, 128], BF16)
    vtms = k.dram("vtms", [16, SPC, DEC_T, 128], BF16)
    bg_scr = k.dram("bg_scr", [NT, 128, 8])
    bgs_scr = k.dram("bgs_scr", [SPC, DEC_T, 32])
    otm_m = [k.dram(f"otm_m{i}", [512, 512]) for i in range(SEQ // 512)]
    otm_g = [k.dram(f"otm_g{i}", [4 * 512, 512], addr_space="Local") for i in range(SEQ // 512)]
    otms_scr = k.dram("otms_scr", [SPC, DEC_T, 2048])
    mod_scr = k.dram("mod_scr", [2, 1 + SPC, 6 * D])
    hTp = k.dram("hTp", [128, 8, SEQ], BF16)
    hTs = k.dram("hTs", [128, 8, SPC * DEC_T], BF16)
    qT_scr = k.dram("qT_scr", [3, 64, HM, SEQ], BF16)
    kT_scr = k.dram("kT_scr", [3, 64, HM, SEQ], BF16)
    V_scr = k.dram("V_scr", [3, NT, 128, HM * 65], BF16)

    for (w, _d) in A_GROUPS:
        for kv in ("k", "v"):
            for s in range(SPC):
                k.dma(new_s[(kv, w)], new_s[(kv, w)].ap[s, 0:w - DEC_T, :],
                      cache[(kv, w)], cache[(kv, w)].ap[s, DEC_T:w, :], eng="sync")

    identf = k.alloc("identf", 128)
    identb = k.alloc("identb", 128, BF16)
    epsb = k.alloc("epsb", 1)
    k.dma(identf, identf.ap, identf_in, identf_in.ap)
    k.copy("vector", identb, identb.ap, identf, identf.ap)
    k.memset("vector", epsb, epsb.ap, EPS)
    csb = k.alloc("csb", 8 * 8, BF16)
    k.keep()

    def phase_mod():
        k.phase()
        cs = k.alloc("cs", 8 * 5)
        k.dma(cs, v3(cs.ap, 8), cT, cT.ap.rearrange("(k p) n -> p k n", p=128))
        k.act(cs, cs.ap, cs, cs.ap, AF.Silu)
        k.memset("vector", csb, csb.ap, 0.0)
        k.copy("vector", csb, v3(csb.ap, 8)[:, :, 0:5], cs, v3(cs.ap, 8))
        modt = k.alloc("modt", 6 * D)
        gb = k.alloc("gb", D)
        wm = [k.alloc(f"wm{i}", 8 * 512, BF16) for i in range(2)]
        n = 1 + SPC
        it = 0
        for layer in range(2):
            k.dma(modt, modt.ap[0:n, :], b_mod, b_mod.ap[layer:layer + 1, :].partition_broadcast(n))
            for cb in range(12):
                wb = wm[it % 2]
                bank = B[it % 2]
                it += 1
                k.dma(wb, v3(wb.ap, 8),
                      w_mod, w_mod.ap[layer].rearrange("(k p) n -> p k n", p=128)[:, :, cb * 512:(cb + 1) * 512],
                      eng="gpsimd")
                for kc in range(8):
                    k.mm(bank, bank.ap[0:n, :], csb, v3(csb.ap, 8)[:, kc, 0:n], wb, v3(wb.ap, 8)[:, kc, :],
                         start=(kc == 0), stop=(kc == 7))
                sl = slice(cb * 512, (cb + 1) * 512)
                k.tt("vector", modt, modt.ap[0:n, sl], bank, bank.ap[0:n, :], modt, modt.ap[0:n, sl], ALU.add)
            for (gsrc, off) in ((norm_mix_g, 1 * D), (norm_ffn_g, 4 * D)):
                k.dma(gb, gb.ap[0:n, :], gsrc, gsrc.ap[layer:layer + 1, :].partition_broadcast(n))
                k.stt(modt, modt.ap[0:n, off:off + D], modt, modt.ap[0:n, off:off + D], 1.0, ALU.add,
                      gb, gb.ap[0:n, :], ALU.mult)
            k.dma(mod_scr, mod_scr.ap[layer], modt, modt.ap[0:n, :])

    def phase_norm(layer, x_p, x_s, a_off, b_off):
        k.phase()
        A_p = k.alloc("A_p", D)
        B_p = k.alloc("B_p", D)
        A_s = [k.alloc(f"A_s{i}", D) for i in range(4)]
        B_s = [k.alloc(f"B_s{i}", D) for i in range(4)]
        xt = [k.alloc(f"xt{i}", D) for i in range(4)]
        sq = k.alloc("sq", D)
        ssq = [k.alloc(f"ssq{i}", 1) for i in range(4)]
        std = [k.alloc(f"std{i}", 1) for i in range(4)]
        rstd = [k.alloc(f"rstd{i}", 1) for i in range(4)]
        tmp = [k.alloc(f"tmp{i}", D) for i in range(4)]
        hb = [k.alloc(f"hb{i}", D, BF16) for i in range(4)]
        hTt = [k.alloc(f"hTt{i}", D, BF16) for i in range(4)]
        mrow = mod_scr.ap[layer]
        k.dma(A_p, A_p.ap, mod_scr, mrow[0:1, a_off:a_off + D].partition_broadcast(128))
        k.dma(B_p, B_p.ap, mod_scr, mrow[0:1, b_off:b_off + D].partition_broadcast(128))
        tiles = [(128, i, None) for i in range(NT)] + [(DEC_T, None, s) for s in range(SPC)]
        for it, (rows, ti, si) in enumerate(tiles):
            j = it % 4
            if si is None:
                src_b, src_ap = x_p, x_p.ap[ti * 128:(ti + 1) * 128, :]
                Ab, Bb = A_p, B_p
            else:
                src_b, src_ap = x_s, x_s.ap[si]
                Ab, Bb = A_s[si % 2], B_s[si % 2]
                k.dma(Ab, Ab.ap[0:rows, :], mod_scr, mrow[1 + si:2 + si, a_off:a_off + D].partition_broadcast(rows))
                k.dma(Bb, Bb.ap[0:rows, :], mod_scr, mrow[1 + si:2 + si, b_off:b_off + D].partition_broadcast(rows))
            r = slice(0, rows)
            k.dma(xt[j], xt[j].ap[r, :], src_b, src_ap)
            k.tt("gpsimd", sq, sq.ap[r, :], xt[j], xt[j].ap[r, :], xt[j], xt[j].ap[r, :], ALU.mult)
            k.op("vector", (lambda o_, i_: (lambda e: e.tensor_reduce(out=o_, in_=i_, op=ALU.add, axis=AX.X)))(
                ssq[j].ap[r, :], sq.ap[r, :]), reads=[sq], writes=[ssq[j]])
            k.act(std[j], std[j].ap[r, :], ssq[j], ssq[j].ap[r, :], AF.Sqrt, scale=1.0 / D, bias=epsb.ap[r, :],
                  sbufs=[epsb])
            k.op("vector", (lambda o_, i_: (lambda e: e.reciprocal(out=o_, in_=i_)))(rstd[j].ap[r, :], std[j].ap[r, :]),
                 reads=[std[j]], writes=[rstd[j]])
            k.stt(tmp[j], tmp[j].ap[r, :], xt[j], xt[j].ap[r, :], rstd[j].ap[r, 0:1], ALU.mult,
                  Ab, Ab.ap[r, :], ALU.mult, sbufs=[rstd[j]])
            k.tt("gpsimd", hb[j], hb[j].ap[r, :], tmp[j], tmp[j].ap[r, :], Bb, Bb.ap[r, :], ALU.add)
            bank = B[j]
            bview = bank.ap.bitcast(BF16)
            for kc in range(8):
                k.tr(bank, bview[:, kc * 128:kc * 128 + rows], hb[j], hb[j].ap[r, kc * 128:(kc + 1) * 128],
                     identb, identb.ap[r, r])
            if si is None:
                k.copy("scalar", hTt[j], hTt[j].ap, bank, bview)
                k.dma(hTp, hTp.ap[:, :, ti * 128:(ti + 1) * 128], hTt[j], v3(hTt[j].ap, 8))
            else:
                k.copy("scalar", hTt[j], v3(hTt[j].ap, 8)[:, :, 0:rows], bank, v3(bview, 8)[:, :, 0:rows])
                k.dma(hTs, hTs.ap[:, :, si * DEC_T:(si + 1) * DEC_T], hTt[j], v3(hTt[j].ap, 8)[:, :, 0:rows])

    def phase_proj_a():
        k.phase()
        W = [k.alloc(f"Wa{j}", 8 * D, BF16) for j in range(3)]
        Wm = [k.alloc(f"Wm{j}", 8 * HM * DH, BF16) for j in range(3)]
        hsb = k.alloc("hsb", 8 * 2048, BF16)
        hss = k.alloc("hss", 8 * SPC * DEC_T, BF16)
        tab = [k.alloc(f"tab{i}", 64) for i in range(4)]
        tabs = k.alloc("tabs", 64)
        raw = [k.alloc(f"raw{i}", D) for i in range(4)]
        rp = [k.alloc(f"rp{i}", D) for i in range(4)]
        t1 = k.alloc("t1", 512)
        t2 = k.alloc("t2", 512)
        t3 = k.alloc("t3", 512)
        t4 = k.alloc("t4", 512)
        vf = [k.alloc(f"vf{i}", D) for i in range(4)]
        vaug = [k.alloc(f"vaug{i}", NH * 65, BF16) for i in range(4)]
        qkT = [k.alloc(f"qkT{i}", NH * 128, BF16) for i in range(4)]
        for i in range(4):
            k.memset("vector", vaug[i], vaug[i].ap, 1.0)
        k.dma(hss, v3(hss.ap, 8), hTs, hTs.ap)
        k.dma(tabs, tabs.ap[0:DEC_T, :], rope_s, rope_s.ap)
        cnt = {"raw": 0, "v": 0, "T": 0}

        def rope(dst, src, tb, rows, nh):
            r = slice(0, rows)
            s4 = src.ap[:, 0:nh * DH].rearrange("p (h t e) -> p h t e", h=nh, t=2)
            d4 = dst.ap[:, 0:nh * DH].rearrange("p (h t e) -> p h t e", h=nh, t=2)
            x1, x2 = s4[r, :, 0, :], s4[r, :, 1, :]
            cosb = tb.ap[r, 0:32].rearrange("p (o e) -> p o e", o=1).to_broadcast([rows, nh, 32])
            sinb = tb.ap[r, 32:64].rearrange("p (o e) -> p o e", o=1).to_broadcast([rows, nh, 32])
            tv = lambda t: v3(t.ap[:, 0:nh * 32], nh)[r]
            k.tt("vector", t1, tv(t1), src, x1, tb, cosb, ALU.mult)
            k.tt("gpsimd", t2, tv(t2), src, x2, tb, sinb, ALU.mult)
            k.tt("vector", dst, d4[r, :, 0, :], t1, tv(t1), t2, tv(t2), ALU.subtract)
            k.tt("gpsimd", t3, tv(t3), src, x1, tb, sinb, ALU.mult)
            k.tt("vector", t4, tv(t4), src, x2, tb, cosb, ALU.mult)
            k.tt("gpsimd", dst, d4[r, :, 1, :], t3, tv(t3), t4, tv(t4), ALU.add)

        def project(lhs_b, lhs_fn, rows, wt, ncol):
            i = cnt["raw"] % 3
            cnt["raw"] += 1
            pieces = []
            for pi, c0 in enumerate(range(0, ncol, 512)):
                wdt = min(512, ncol - c0)
                bank = B[2 * i + pi]
                for kc in range(8):
                    k.mm(bank, bank.ap[0:rows, 0:wdt], lhs_b, lhs_fn(kc),
                         wt, v3(wt.ap, 8)[:, kc, c0:c0 + wdt], start=(kc == 0), stop=(kc == 7))
                pieces.append((bank, c0, wdt))
            return pieces

        def qk_path(pieces, rows, tb, out_dram_fn, kout, nh):
            r = slice(0, rows)
            j = cnt["T"] % 4
            cnt["T"] += 1
            for (bank, c0, wdt) in pieces:
                k.copy("scalar", raw[j], raw[j].ap[r, c0:c0 + wdt], bank, bank.ap[r, 0:wdt])
            rope(rp[j], raw[j], tb, rows, nh)
            for (ob, oap) in kout:
                k.dma(ob, oap, rp[j], rp[j].ap[r, 0:nh * DH])
            tbs = (B[6], B[7])
            qv = v3(qkT[j].ap, NH)
            for h0 in range(0, nh, 8):
                hn = min(8, nh - h0)
                for hh in range(hn):
                    h = h0 + hh
                    bank = tbs[hh // 4]
                    k.tr(bank, bank.ap[0:64, (hh % 4) * 128:(hh % 4) * 128 + rows], rp[j], rp[j].ap[r, h * 64:(h + 1) * 64],
                         identf, identf.ap[r, r])
                for bi_ in range((hn + 3) // 4):
                    k.copy("scalar", qkT[j], qv[0:64, h0 + 4 * bi_:h0 + 4 * bi_ + 4, 0:rows], tbs[bi_],
                           v3(tbs[bi_].ap, 4)[0:64, :, 0:rows])
            ob, oap = out_dram_fn
            k.dma(ob, oap, qkT[j], qv[0:64, 0:nh, 0:rows])

        def v_path(pieces, rows, vout, vs_out, nh):
            r = slice(0, rows)
            j = cnt["v"] % 4
            cnt["v"] += 1
            for (bank, c0, wdt) in pieces:
                k.copy("scalar", vf[j], vf[j].ap[r, c0:c0 + wdt], bank, bank.ap[r, 0:wdt])
            for (ob, oap) in vout:
                k.dma(ob, oap, vf[j], vf[j].ap[r, 0:nh * DH])
            k.copy("vector", vaug[j], v3(vaug[j].ap, NH)[r, 0:nh, 0:64], vf[j], v3(vf[j].ap, NH)[r, 0:nh])
            ob, oap = vs_out
            k.dma(ob, oap, vaug[j], vaug[j].ap[r, 0:nh * 65])

        for g, (win, d) in enumerate(A_GROUPS):
            nb = group_blocks(d)
            for j in range(3):
                k.dma(W[j], v3(W[j].ap, 8),
                      a_w_in, a_w_in.ap.rearrange("(k p) n -> p k n", p=128)[:, :, (g * 3 + j) * D:(g * 3 + j + 1) * D],
                      eng="gpsimd")
                wc = HM * DH
                k.dma(Wm[j], v3(Wm[j].ap, 8), a_w_in_m,
                      a_w_in_m.ap.rearrange("(k p) n -> p k n", p=128)[:, :, (g * 3 + j) * wc:(g * 3 + j + 1) * wc], eng="gpsimd")
            for sb in range(4):
                k.dma(hsb, v3(hsb.ap, 8), hTp, hTp.ap[:, :, sb * 2048:(sb + 1) * 2048])
                for r_ in range(d):
                    for b_ in range(nb):
                        n = sb * 16 + r_ * nb + b_
                        c0 = r_ + d * b_ * 128
                        lhs = lambda kc, c0=c0, d=d: v3(hsb.ap, 8)[:, kc, c0:c0 + 127 * d + 1:d]
                        tb = tab[n % 4]
                        k.dma(tb, tb.ap, rope_p, rope_p.ap[g, n])
                        last = (sb == 3 and b_ == nb - 1)
                        bq = project(hsb, lhs, 128, Wm[0], HM * DH)
                        bk = project(hsb, lhs, 128, Wm[1], HM * DH)
                        bv = project(hsb, lhs, 128, Wm[2], HM * DH)
                        qk_path(bq, 128, tb, (qT_scr, qT_scr.ap[g][:, :, n * 128:(n + 1) * 128]), [], HM)
                        kout = [(new_p[("k", win)], new_p[("k", win)].ap[r_::d, :])] if last else []
                        qk_path(bk, 128, tb, (kT_scr, kT_scr.ap[g][:, :, n * 128:(n + 1) * 128]), kout, HM)
                        vout = [(new_p[("v", win)], new_p[("v", win)].ap[r_::d, :])] if last else []
                        v_path(bv, 128, vout, (V_scr, V_scr.ap[g, n]), HM)
            for s in range(SPC):
                lhs = lambda kc, s=s: v3(hss.ap, 8)[:, kc, s * DEC_T:(s + 1) * DEC_T]
                cs8 = slice(s * DEC_T, (s + 1) * DEC_T)
                bq = project(hss, lhs, DEC_T, W[0], D)
                qk_path(bq, DEC_T, tabs, (qTs_scr, qTs_scr.ap[g][:, :, cs8]), [], NH)
                bk = project(hss, lhs, DEC_T, W[1], D)
                qk_path(bk, DEC_T, tabs, (kTs_scr, kTs_scr.ap[g][:, :, cs8]),
                        [(new_s[("k", win)], new_s[("k", win)].ap[s, win - DEC_T:win, :])], NH)
                bv = project(hss, lhs, DEC_T, W[2], D)
                v_path(bv, DEC_T, [(new_s[("v", win)], new_s[("v", win)].ap[s, win - DEC_T:win, :])],
                       (Vs_scr, Vs_scr.ap[g, s]), NH)

    def prev_block(n, d):
        nb = group_blocks(d)
        sb, rem = divmod(n, 16)
        r_, b_ = divmod(rem, nb)
        if b_ > 0:
            return n - 1
        if sb > 0:
            return (sb - 1) * 16 + r_ * nb + nb - 1
        return None

    def recip(eng, out_b, out_ap, in_b, in_ap):
        k.op(eng, lambda e: e.reciprocal(out=out_ap, in_=in_ap), reads=[in_b], writes=[out_b])

    def phase_attn():
        k.phase()
        mask4 = k.alloc("mask4", 512, BF16)
        k.dma(mask4, mask4.ap, mask4_in, mask4_in.ap, eng="gpsimd")
        qh = [k.alloc(f"qh{i}", 2 * SEQ, BF16) for i in range(1)] * 2
        kh = [k.alloc(f"kh{i}", 2 * SEQ, BF16) for i in range(1)] * 2
        Vh = [k.alloc(f"Vh{i}", NT * 130, BF16) for i in range(2)]
        P = [k.alloc(f"P{i}", 512, BF16) for i in range(3)]
        osb = [k.alloc(f"osb{i}", 512) for i in range(4)]
        it = 0
        for g, (win, d) in enumerate(A_GROUPS):
            for hp in range(HM // 2):
                j = it % 2
                it += 1
                q3_, k3_ = v3(qh[j].ap, 2), v3(kh[j].ap, 2)
                k.dma(qh[j], q3_[0:64], qT_scr, qT_scr.ap[g][:, 2 * hp:2 * hp + 2, :])
                k.dma(kh[j], k3_[0:64], kT_scr, kT_scr.ap[g][:, 2 * hp:2 * hp + 2, :])
                for n8 in range(0, NT, 8):
                    k.dma(Vh[j], v3(Vh[j].ap, NT)[:, n8:n8 + 8], V_scr,
                          V_scr.ap[g].rearrange("n p c -> p n c")[:, n8:n8 + 8, hp * 130:(hp + 1) * 130])
                Vv = v3(Vh[j].ap, NT)
                def scores(n):
                    pn = prev_block(n, d)
                    sbank = B[n % 4]
                    cn = slice(n * 128, (n + 1) * 128)
                    for h in range(2):
                        k.mm(sbank, sbank.ap[:, (2 * h) * 128:(2 * h + 1) * 128], kh[j], k3_[0:64, h, cn],
                             qh[j], q3_[0:64, h, cn])
                        if pn is not None:
                            k.mm(sbank, sbank.ap[:, (2 * h + 1) * 128:(2 * h + 2) * 128], kh[j],
                                 k3_[0:64, h, pn * 128:(pn + 1) * 128], qh[j], q3_[0:64, h, cn])

                scores(0)
                for n in range(NT):
                    pn = prev_block(n, d)
                    sbank = B[n % 4]
                    if n + 1 < NT:
                        scores(n + 1)
                    Pb = P[n % 3]
                    if pn is not None:
                        k.act(Pb, Pb.ap, sbank, sbank.ap, AF.Exp, scale=DH ** -0.5)
                        k.tt("gpsimd" if n % 2 else "vector", Pb, Pb.ap, Pb, Pb.ap, mask4, mask4.ap, ALU.mult)
                    else:
                        for h in range(2):
                            c_ = slice((2 * h) * 128, (2 * h + 1) * 128)
                            k.act(Pb, Pb.ap[:, c_], sbank, sbank.ap[:, c_], AF.Exp, scale=DH ** -0.5)
                            k.tt("vector", Pb, Pb.ap[:, c_], Pb, Pb.ap[:, c_], mask4, mask4.ap[:, c_], ALU.mult)
                    q4 = n % 4
                    par = (n // 4) % 2
                    for h in range(2):
                        acc = B[4 + 2 * par + h]
                        oap = acc.ap[0:65, q4 * 128:(q4 + 1) * 128]
                        k.mm(acc, oap, Vh[j], Vv[:, n, h * 65:(h + 1) * 65], Pb, Pb.ap[:, (2 * h) * 128:(2 * h + 1) * 128],
                             start=True, stop=(pn is None))
                        if pn is not None:
                            k.mm(acc, oap, Vh[j], Vv[:, pn, h * 65:(h + 1) * 65], Pb,
                                 Pb.ap[:, (2 * h + 1) * 128:(2 * h + 2) * 128], start=False, stop=True)
                    if q4 == 3:
                        for h in range(2):
                            acc = B[4 + 2 * par + h]
                            ob = osb[2 * par + h]
                            k.copy("scalar" if h else "vector", ob, ob.ap[0:65, :], acc, acc.ap[0:65, :])
                            k.dma(o_scr, o_scr.ap[g, hp * 2 + h][:, (n - 3) * 128:(n + 1) * 128], ob, ob.ap[0:65, :])

    def phase_attn_sample():
        k.phase()
        E = k.alloc("E", 64)
        k.dma(E, E.ap[0:65, :], E_in, E_in.ap)
        mks = k.alloc("mks", 3 * 8 * 8, BF16)
        mkn = k.alloc("mkn", 3 * 8, BF16)
        for g_ in range(3):
            k.dma(mks, mks.ap.rearrange("p (g c t) -> p g c t", g=3, c=8)[:, g_], msk_s_in,
                  msk_s_in.ap[g_].rearrange("c p t -> p c t"), eng="gpsimd")
        k.dma(mkn, v3(mkn.ap, 3)[0:8], msk_n_in, msk_n_in.ap.rearrange("g p t -> p g t"), eng="gpsimd")
        mks4 = mks.ap.rearrange("p (g c t) -> p g c t", g=3, c=8)
        mkn3 = v3(mkn.ap, 3)
        qs = [k.alloc(f"qs{i}", 3 * NH * 8, BF16) for i in range(2)]
        kn = [k.alloc(f"kn{i}", 3 * NH * 8, BF16) for i in range(2)]
        Vn = [k.alloc(f"Vn{i}", 3 * NH * 65, BF16) for i in range(2)]
        Kc = [k.alloc(f"Kc{i}", D) for i in range(2)]
        Vc = [k.alloc(f"Vc{i}", D) for i in range(2)]
        kTc = [k.alloc(f"kTc{i}", NH * 128, BF16) for i in range(2)]
        Vac = [k.alloc(f"Vac{i}", NH * 65, BF16) for i in range(2)]
        Pc = [k.alloc(f"Pc{i}", 128, BF16) for i in range(2)]
        Pn = [k.alloc(f"Pn{i}", 128, BF16) for i in range(2)]
        tot = [k.alloc(f"tots{i}", 128) for i in range(2)]
        rd = k.alloc("rds", 128)
        oTs = [k.alloc(f"oTs{i}", 128, BF16) for i in range(2)]
        for i in range(2):
            k.memset("vector", Vac[i], Vac[i].ap, 1.0)
        it = 0
        for s in range(SPC):
            js = s % 2
            cs8 = slice(s * DEC_T, (s + 1) * DEC_T)
            q4 = qs[js].ap.rearrange("p (g c t) -> p g c t", g=3, c=NH)[0:64]
            k4 = kn[js].ap.rearrange("p (g c t) -> p g c t", g=3, c=NH)[0:64]
            V3 = v3(Vn[js].ap, 3)
            for g_ in range(3):
                k.dma(qs[js], q4[:, g_], qTs_scr, qTs_scr.ap[g_][:, :, cs8])
                k.dma(kn[js], k4[:, g_], kTs_scr, kTs_scr.ap[g_][:, :, cs8])
            k.dma(Vn[js], V3[0:DEC_T], Vs_scr, Vs_scr.ap[:, s].rearrange("g t c -> t g c"))
            nacc = [0]

            def accumulate(pvb):
                if nacc[0] == 0:
                    k.copy("vector", tot[js], tot[js].ap[0:65, :], pvb, pvb.ap[0:65, 0:128])
                else:
                    k.tt("vector", tot[js], tot[js].ap[0:65, :], tot[js], tot[js].ap[0:65, :], pvb, pvb.ap[0:65, 0:128], ALU.add)
                nacc[0] += 1
            for g, (win, d) in enumerate(A_GROUPS):
                for t0 in range(min(d, DEC_T)):
                    j = it % 2
                    it += 1
                    rows_sl = slice(t0, t0 + 127 * d + 1, d)
                    k.dma(Kc[j], Kc[j].ap, cache[("k", win)], cache[("k", win)].ap[s, rows_sl, :])
                    k.dma(Vc[j], Vc[j].ap, cache[("v", win)], cache[("v", win)].ap[s, rows_sl, :])
                    tb0, tb1 = B[2 * j], B[2 * j + 1]
                    for rnd in range(2):
                        for hh in range(8):
                            h = rnd * 8 + hh
                            bank = tb0 if hh < 4 else tb1
                            k.tr(bank, bank.ap[0:64, (hh % 4) * 128:(hh % 4 + 1) * 128], Kc[j], Kc[j].ap[:, h * 64:(h + 1) * 64],
                                 identf, identf.ap)
                        c0_ = rnd * 1024
                        k.copy("scalar", kTc[j], kTc[j].ap[0:64, c0_:c0_ + 512], tb0, tb0.ap[0:64, :])
                        k.copy("scalar", kTc[j], kTc[j].ap[0:64, c0_ + 512:c0_ + 1024], tb1, tb1.ap[0:64, :])
                    k.copy("vector", Vac[j], v3(Vac[j].ap, NH)[:, :, 0:64], Vc[j], v3(Vc[j].ap, NH))
                    sbank = B[6 + j]
                    kT3 = v3(kTc[j].ap, NH)
                    for h in range(NH):
                        k.mm(sbank, sbank.ap[:, h * 8:(h + 1) * 8], kTc[j], kT3[0:64, h, :],
                             qs[js], q4[:, g, h, :])
                    k.act(Pc[j], Pc[j].ap, sbank, sbank.ap[:, 0:128], AF.Exp, scale=DH ** -0.5)
                    mb = mks4[:, g, t0, :].rearrange("p (o t) -> p o t", o=1).to_broadcast([128, NH, DEC_T])
                    k.tt("vector", Pc[j], v3(Pc[j].ap, NH), Pc[j], v3(Pc[j].ap, NH), mks, mb, ALU.mult)
                    Va3 = v3(Vac[j].ap, NH)
                    acc = B[4 + j]
                    for h in range(NH):
                        k.mm(acc, acc.ap[0:65, h * 8:(h + 1) * 8], Vac[j], Va3[:, h, :], Pc[j], Pc[j].ap[:, h * 8:(h + 1) * 8])
                    accumulate(acc)
                j = it % 2
                sbank = B[6 + j]
                for h in range(NH):
                    k.mm(sbank, sbank.ap[0:DEC_T, 128 + h * 8:128 + (h + 1) * 8], kn[js], k4[:, g, h, :],
                         qs[js], q4[:, g, h, :])
                k.act(Pn[j], Pn[j].ap[0:DEC_T, :], sbank, sbank.ap[0:DEC_T, 128:256], AF.Exp, scale=DH ** -0.5)
                mb = mkn3[0:DEC_T, g, :].rearrange("p (o t) -> p o t", o=1).to_broadcast([DEC_T, NH, DEC_T])
                k.tt("vector", Pn[j], v3(Pn[j].ap, NH)[0:DEC_T], Pn[j], v3(Pn[j].ap, NH)[0:DEC_T], mkn, mb, ALU.mult)
                Vg = V3[0:DEC_T, g, :].rearrange("p (h c) -> p h c", h=NH)
                acc = B[4 + j]
                for h in range(NH):
                    k.mm(acc, acc.ap[0:65, h * 8:(h + 1) * 8], Vn[js], Vg[:, h, :], Pn[j], Pn[j].ap[0:DEC_T, h * 8:(h + 1) * 8])
                accumulate(acc)
            dbank = B[js]
            k.mm(dbank, dbank.ap[0:64, 0:128], E, E.ap[0:65, :], tot[js], tot[js].ap[0:65, :])
            recip("vector", rd, rd.ap[0:64, :], dbank, dbank.ap[0:64, 0:128])
            k.tt("vector", oTs[js], oTs[js].ap[0:64, :], tot[js], tot[js].ap[0:64, :], rd, rd.ap[0:64, :], ALU.mult)
            k.dma(oTs_scr, oTs_scr.ap[s], oTs[js], oTs[js].ap[0:64, :])

    def phase_combine():
        k.phase()
        E = k.alloc("E", 64)
        k.dma(E, E.ap[0:65, :], E_in, E_in.ap)
        og = [[k.alloc(f"og{i}_{g}", 2048) for g in range(3)] for i in range(2)]
        rden = [k.alloc(f"rden{i}", 512) for i in range(2)]
        oTb = [k.alloc(f"oTb{i}", 2048, BF16) for i in range(2)]
        it = 0
        for h in range(HM):
            for sb in range(4):
                j = it % 2
                it += 1
                cs_ = slice(sb * 2048, (sb + 1) * 2048)
                for g in range(3):
                    k.dma(og[j][g], og[j][g].ap[0:65, :], o_scr, o_scr.ap[g, h][:, cs_])
                tot = og[j][0]
                for g, r in ((1, 4), (2, 16)):
                    tv = tot.ap[0:65, :].rearrange("p (i r) -> p i r", r=r)
                    gv = og[j][g].ap[0:65, :].rearrange("p (r i) -> p i r", r=r)
                    k.tt("vector" if g == 1 else "gpsimd", tot, tv, tot, tv, og[j][g], gv, ALU.add)
                for q in range(4):
                    bank = B[(it * 4 + q) % 8]
                    cq = slice(q * 512, (q + 1) * 512)
                    k.mm(bank, bank.ap[0:64, :], E, E.ap[0:65, :], tot, tot.ap[0:65, cq])
                    rd = rden[q % 2]
                    recip("vector", rd, rd.ap[0:64, :], bank, bank.ap[0:64, :])
                    k.tt("gpsimd", oTb[j], oTb[j].ap[0:64, cq], tot, tot.ap[0:64, cq], rd, rd.ap[0:64, :], ALU.mult)
                k.dma(oT_mine[sb], oT_mine[sb].ap[h * 64:(h + 1) * 64, :], oTb[j], oTb[j].ap[0:64, :])
        for sb in range(4):
            k.allgather(oT_scr[sb], oT_scr[sb].ap, oT_mine[sb], oT_mine[sb].ap, CC_GROUPS)

    def phase_out_a():
        k.phase()
        wo = k.alloc("wo", NH * D, BF16)
        wo3 = v3(wo.ap, NH)
        k.dma(wo, wo3[0:64], a_w_out, a_w_out.ap.rearrange("(h e) n -> e h n", e=64), eng="gpsimd")
        G = k.alloc("G", D)
        Gs = [k.alloc(f"Gs{i}", D) for i in range(2)]
        k.dma(G, G.ap, mod_scr, mod_scr.ap[0][0:1, 2 * D:3 * D].partition_broadcast(128))
        o4 = [k.alloc(f"o4{i}", NH * 512, BF16) for i in range(2)]
        os8 = [k.alloc(f"os8{i}", NH * DEC_T, BF16) for i in range(2)]
        xt = [k.alloc(f"xt{i}", D) for i in range(2)]
        yt = [k.alloc(f"yt{i}", D) for i in range(2)]
        nb_ = 0

        def one(rows, lhs_b, lhs_fn, x_src_b, x_src_ap, Gb, dst_b, dst_ap, j):
            nonlocal nb_
            r = slice(0, rows)
            k.dma(xt[j], xt[j].ap[r, :], x_src_b, x_src_ap)
            for half in range(2):
                bank = B[nb_ % 8]
                nb_ += 1
                for h in range(NH):
                    k.mm(bank, bank.ap[r, :], lhs_b, lhs_fn(h), wo, wo3[0:64, h, half * 512:(half + 1) * 512],
                         start=(h == 0), stop=(h == NH - 1))
                hs = slice(half * 512, (half + 1) * 512)
                k.tt("vector", yt[j], yt[j].ap[r, hs], bank, bank.ap[r, :], Gb, Gb.ap[r, hs], ALU.mult)
                k.tt("gpsimd", yt[j], yt[j].ap[r, hs], yt[j], yt[j].ap[r, hs], xt[j], xt[j].ap[r, hs], ALU.add)
            k.dma(dst_b, dst_ap, yt[j], yt[j].ap[r, :])

        for t in range(NT):
            jj = (t // 4) % 2
            if t % 4 == 0:
                osb_ = oT_scr[t // 16]
                k.dma(o4[jj], v3(o4[jj].ap, NH)[0:64], osb_,
                      osb_.ap[:, (t % 16) * 128:(t % 16 + 4) * 128].rearrange("(h e) c -> e h c", e=64))
            ov = v3(o4[jj].ap, NH)
            c_ = slice((t % 4) * 128, (t % 4 + 1) * 128)
            one(128, o4[jj], (lambda h, ov=ov, c_=c_: ov[0:64, h, c_]), xp, xp.ap[t * 128:(t + 1) * 128, :], G,
                x1p, x1p.ap[t * 128:(t + 1) * 128, :], t % 2)
        for s in range(SPC):
            j = s % 2
            k.dma(os8[j], os8[j].ap[0:64, :], oTs_scr, oTs_scr.ap[s])
            k.dma(Gs[j], Gs[j].ap[0:DEC_T, :], mod_scr, mod_scr.ap[0][1 + s:2 + s, 2 * D:3 * D].partition_broadcast(DEC_T))
            ov = v3(os8[j].ap, NH)
            one(DEC_T, os8[j], (lambda h, ov=ov: ov[0:64, h, :]), xs, xs.ap[s], Gs[j], x1s, x1s.ap[s], j)

    phase_mod()
    phase_norm(0, xp, xs, 1 * D, 0)
    phase_proj_a()
    def phase_ffn(layer, xin_p, xin_s, xout_p, xout_s):
        k.phase()
        TB = 1024
        NC_ = D_FF // 128
        wd = k.alloc("wd", NC_ * D, BF16)
        wd3 = v3(wd.ap, NC_)
        k.dma(wd, wd3, ffn_w_down, ffn_w_down.ap[layer].rearrange("(c p) n -> p c n", p=128), eng="gpsimd")
        cw = k.alloc("cw", NC_ * 4)
        cw3 = v3(cw.ap, NC_)
        k.dma(cw, cw3, ffn_cw, ffn_cw.ap[layer])
        carry = k.alloc("carry", NC_ * 2)
        carry3 = v3(carry.ap, NC_)
        k.memset("vector", carry, carry.ap, 0.0)
        carry_s = k.alloc("carry_s", NC_ * SPC * 2)
        cs4 = carry_s.ap.rearrange("p (c s t) -> p c s t", c=NC_, s=SPC)
        for c in range(0, NC_, 11):
            k.dma(carry_s, cs4[:, c:c + 11].rearrange("p c s t -> p c (s t)"), fconvT,
                  fconvT.ap[layer][:, c:c + 11].rearrange("p c s t -> p c (s t)"))
        G = k.alloc("G", D)
        k.dma(G, G.ap, mod_scr, mod_scr.ap[layer][0:1, 5 * D:6 * D].partition_broadcast(128))
        Gs = k.alloc("Gs", D)
        for s in range(SPC):
            k.dma(Gs, Gs.ap[s * DEC_T:(s + 1) * DEC_T, :], mod_scr,
                  mod_scr.ap[layer][1 + s:2 + s, 5 * D:6 * D].partition_broadcast(DEC_T))
        hb = [k.alloc(f"hblk{i}", 8 * TB, BF16) for i in range(2)]
        act = k.alloc("act", NC_ * TB, BF16)
        act3 = v3(act.ap, NC_)
        wg = [k.alloc(f"wg{i}", 8 * 128, BF16) for i in range(2)]
        wv = [k.alloc(f"wv{i}", 8 * 128, BF16) for i in range(2)]
        gp = [k.alloc(f"gp{i}", TB + 2) for i in range(2)]
        tc_ = [k.alloc(f"tc{i}", TB) for i in range(2)]
        sg = [k.alloc(f"sg{i}", TB) for i in range(2)]
        xt = [k.alloc(f"xt{i}", D) for i in range(2)]
        yt = [k.alloc(f"yt{i}", D) for i in range(2)]
        wup = ffn_w_up.ap[layer].rearrange("(k p) n -> p k n", p=128)
        cnt = 0
        blocks = [("p", b) for b in range(SEQ // TB)] + [("s", 0)]
        for bi, (kind, b) in enumerate(blocks):
            hbb = hb[bi % 2]
            if kind == "p":
                ntok = TB
                k.dma(hbb, v3(hbb.ap, 8), hTp, hTp.ap[:, :, b * TB:(b + 1) * TB])
                h3 = v3(hbb.ap, 8)
            else:
                ntok = SPC * DEC_T
                h3 = v3(hbb.ap, 8)[:, :, 0:ntok]
                k.dma(hbb, h3, hTs, hTs.ap)
            nq = max(1, ntok // 512)
            qw = min(512, ntok)
            for c in range(NC_):
                j = cnt % 2
                cnt += 1
                k.dma(wg[j], v3(wg[j].ap, 8), ffn_w_up, wup[:, :, c * 128:(c + 1) * 128], eng="gpsimd")
                k.dma(wv[j], v3(wv[j].ap, 8), ffn_w_up, wup[:, :, D_FF + c * 128:D_FF + (c + 1) * 128], eng="gpsimd")
                gpj = gp[j]
                if kind == "p":
                    k.copy("gpsimd", gpj, gpj.ap[:, 0:2], carry, carry3[:, c, :])
                    gnew = gpj.ap[:, 2:2 + ntok]
                else:
                    g3 = gpj.ap[:, 0:SPC * 10].rearrange("p (s t) -> p s t", s=SPC)
                    k.copy("gpsimd", gpj, g3[:, :, 0:2], carry_s, cs4[:, c])
                vbanks = []
                for q in range(nq):
                    gb_ = B[(4 * cnt + 2 * q) % 8]
                    vb_ = B[(4 * cnt + 2 * q + 1) % 8]
                    cq = slice(q * qw, (q + 1) * qw)
                    for kc in range(8):
                        k.mm(gb_, gb_.ap[:, 0:qw], wg[j], v3(wg[j].ap, 8)[:, kc, :], hbb, h3[:, kc, cq],
                             start=(kc == 0), stop=(kc == 7))
                    for kc in range(8):
                        k.mm(vb_, vb_.ap[:, 0:qw], wv[j], v3(wv[j].ap, 8)[:, kc, :], hbb, h3[:, kc, cq],
                             start=(kc == 0), stop=(kc == 7))
                    if kind == "p":
                        k.copy("scalar", gpj, gpj.ap[:, 2 + q * qw:2 + (q + 1) * qw], gb_, gb_.ap[:, 0:qw])
                    else:
                        k.copy("scalar", gpj, g3[:, :, 2:10], gb_, gb_.ap[:, 0:qw].rearrange("p (s t) -> p s t", s=SPC))
                    vbanks.append(vb_)
                w0, w1, w2, bb = (cw3[:, c, i_:i_ + 1] for i_ in range(4))
                if kind == "p":
                    x0, x1_, x2_ = gpj.ap[:, 0:ntok], gpj.ap[:, 1:1 + ntok], gpj.ap[:, 2:2 + ntok]
                    tcv, sgv = tc_[j].ap[:, 0:ntok], sg[j].ap[:, 0:ntok]
                else:
                    x0, x1_, x2_ = g3[:, :, 0:8], g3[:, :, 1:9], g3[:, :, 2:10]
                    tcv = tc_[j].ap[:, 0:ntok].rearrange("p (s t) -> p s t", s=SPC)
                    sgv = sg[j].ap[:, 0:ntok]
                k.ts("vector", tc_[j], tcv, gpj, x0, w0, ALU.mult, sbufs=[cw])
                k.stt(tc_[j], tcv, gpj, x1_, w1, ALU.mult, tc_[j], tcv, ALU.add, sbufs=[cw])
                k.stt(tc_[j], tcv, gpj, x2_, w2, ALU.mult, tc_[j], tcv, ALU.add, sbufs=[cw])
                k.act(sg[j], sgv, tc_[j], tc_[j].ap[:, 0:ntok], AF.Silu, bias=bb, sbufs=[cw])
                for q in range(nq):
                    cq = slice(q * qw, (q + 1) * qw)
                    k.tt("vector", act, act3[:, c, cq], vbanks[q], vbanks[q].ap[:, 0:qw], sg[j], sg[j].ap[:, cq], ALU.mult)
                if kind == "p":
                    k.copy("gpsimd", carry, carry3[:, c, :], gpj, gpj.ap[:, ntok:ntok + 2])
                else:
                    k.copy("gpsimd", carry_s, cs4[:, c], gpj, g3[:, :, 8:10])
            ntile = max(1, ntok // 128)
            rows = min(128, ntok)
            r = slice(0, rows)
            for t in range(ntile):
                j = t % 2
                if kind == "p":
                    tok0 = b * TB + t * 128
                    k.dma(xt[j], xt[j].ap, xin_p, xin_p.ap[tok0:tok0 + 128, :])
                    Gb = G
                else:
                    k.dma(xt[j], xt[j].ap[r, :], xin_s, xin_s.ap.rearrange("s t d -> (s t) d"))
                    Gb = Gs
                for half in range(2):
                    bank = B[(2 * t + half) % 8]
                    for c in range(NC_):
                        k.mm(bank, bank.ap[r, :], act, act3[:, c, t * 128:t * 128 + rows], wd, wd3[:, c, half * 512:(half + 1) * 512],
                             start=(c == 0), stop=(c == NC_ - 1))
                    hs = slice(half * 512, (half + 1) * 512)
                    k.tt("vector", yt[j], yt[j].ap[r, hs], bank, bank.ap[r, :], Gb, Gb.ap[r, hs], ALU.mult)
                    k.tt("gpsimd", yt[j], yt[j].ap[r, hs], yt[j], yt[j].ap[r, hs], xt[j], xt[j].ap[r, hs], ALU.add)
                if kind == "p":
                    k.dma(xout_p, xout_p.ap[tok0:tok0 + 128, :], yt[j], yt[j].ap)
                else:
                    k.dma(xout_s, xout_s.ap.rearrange("s t d -> (s t) d"), yt[j], yt[j].ap[r, :])
        k.dma(fconv_p_out, fconv_p_out.ap[layer], carry, carry3)
        k.dma(fconv_s_out, fconv_s_out.ap[layer], carry_s, carry_s.ap)

    NQ = 8
    NV = 16
    bankrr = [0]

    def nbank():
        b_ = B[bankrr[0] % 8]
        bankrr[0] += 1
        return b_

    def phase_b1():
        k.phase()
        Wq = k.alloc("Wqkv", 8 * B_CONV_DIM, BF16)
        Wq3 = v3(Wq.ap, 8)
        bw = b_w_in.ap.rearrange("(k p) n -> p k n", p=128)
        for c0 in range(0, B_CONV_DIM, 1024):
            k.dma(Wq, Wq3[:, :, c0:c0 + 1024], b_w_in, bw[:, :, c0:c0 + 1024], eng="gpsimd")
        Wba = k.alloc("Wba", 8 * 32, BF16)
        k.dma(Wba, v3(Wba.ap, 8), b_w_in, bw[:, :, 6144:6176], eng="gpsimd")
        bwm = b_w_in_m.ap.rearrange("(k p) n -> p k n", p=128)
        Wqm = k.alloc("Wqm", 8 * 1024, BF16)
        Wqm3 = v3(Wqm.ap, 8)
        k.dma(Wqm, Wqm3, b_w_in_m, bwm[:, :, 0:1024], eng="gpsimd")
        Wbam = k.alloc("Wbam", 8 * 8, BF16)
        k.dma(Wbam, v3(Wbam.ap, 8), b_w_in_m, bwm[:, :, 1024:1032], eng="gpsimd")
        cwb = k.alloc("cwb", 32 * 4)
        cwb3 = v3(cwb.ap, 32)
        k.dma(cwb, cwb3, b_cw, b_cw.ap)
        cwm = k.alloc("cwm", 8 * 4)
        cwm3 = v3(cwm.ap, 8)
        k.dma(cwm, cwm3, b_cw_m, b_cw_m.ap)
        carry = k.alloc("carryB", 8 * 3)
        carry3 = v3(carry.ap, 8)
        k.memset("vector", carry, carry.ap, 0.0)
        carry_s = k.alloc("carryBs", 32 * SPC * 3)
        cs4 = carry_s.ap.rearrange("p (c s t) -> p c s t", c=32, s=SPC)
        for c in range(0, 32, 8):
            k.dma(carry_s, cs4[:, c:c + 8].rearrange("p c s t -> p c (s t)"), bconvT,
                  bconvT.ap[:, c:c + 8].rearrange("p c s t -> p c (s t)"))
        onesf = k.alloc("onesf", 128)
        k.memset("vector", onesf, onesf.ap, 1.0)
        one1 = k.alloc("one1", 1)
        k.memset("vector", one1, one1.ap, 1.0)

        def head_consts(nm, dt_in, al_in, n):
            dtb_ = k.alloc("dtb" + nm, n)
            negA_ = k.alloc("negA" + nm, n)
            k.dma(dtb_, dtb_.ap, dt_in, dt_in.ap.partition_broadcast(128))
            k.dma(negA_, negA_.ap, al_in, al_in.ap.partition_broadcast(128))
            k.act(negA_, negA_.ap, negA_, negA_.ap, AF.Exp)
            k.ts("vector", negA_, negA_.ap, negA_, negA_.ap, -1.0, ALU.mult)
            return dtb_, negA_

        dtb, negA = head_consts("", b_dt_bias, b_a_log, 16)
        dtbm, negAm = head_consts("m", b_dt_bias_m, b_a_log_m, 4)
        TBk = 512
        hb = [k.alloc(f"hblk{i}", 8 * TBk, BF16) for i in range(2)]
        NB1 = 4
        pre = [k.alloc(f"pre{i}", TBk + 3) for i in range(NB1)]
        tcv_ = [k.alloc(f"tcv{i}", TBk) for i in range(NB1)]
        av = [k.alloc(f"av{i}", TBk) for i in range(NB1)]
        sq = [k.alloc(f"sq{i}", TBk) for i in range(NB1)]
        sd = [k.alloc(f"sd{i}", TBk) for i in range(NB1)]
        xn = [k.alloc(f"xn{i}", TBk, BF16) for i in range(NB1)]
        tmb = [k.alloc(f"tmb{i}", TBk, BF16) for i in range(NB1)]
        braw = k.alloc("braw", 128)
        xa = k.alloc("xa", 64)
        ab = k.alloc("ab", 64)
        sp = k.alloc("sp", 64)
        bgt = [k.alloc(f"bgt{i}", 128) for i in range(2)]
        cnt = 0
        blocks = [("p", b) for b in range(SEQ // TBk)] + [("s", 0)]
        for bi, (kind, b) in enumerate(blocks):
            hbb = hb[bi % 2]
            if kind == "p":
                ntok = TBk
                h3 = v3(hbb.ap, 8)
                k.dma(hbb, h3, hTp, hTp.ap[:, :, b * TBk:(b + 1) * TBk])
                chunks = [(Wqm, Wqm3, cm * 128, [cwm3[:, cm, i_:i_ + 1] for i_ in range(4)], cwm,
                           "q" if cm < 2 else ("k" if cm < 4 else "v"), cm if cm < 2 else (cm - 2 if cm < 4 else cm - 4), cm)
                          for cm in range(8)]
                nvh = 4
            else:
                ntok = SPC * DEC_T
                h3 = v3(hbb.ap, 8)[:, :, 0:ntok]
                k.dma(hbb, h3, hTs, hTs.ap)
                chunks = [(Wq, Wq3, cc * 128, [cwb3[:, cc, i_:i_ + 1] for i_ in range(4)], cwb,
                           "q" if cc < 8 else ("k" if cc < 16 else "v"), cc if cc < 8 else (cc - 8 if cc < 16 else cc - 16), cc)
                          for cc in range(32)]
                nvh = 16
            for (wt, wt3, wcol, w, cwbuf, role, hidx, ci_) in chunks:
                j = cnt % NB1
                cnt += 1
                bank = nbank()
                for kc in range(8):
                    k.mm(bank, bank.ap[:, 0:ntok], wt, wt3[:, kc, wcol:wcol + 128], hbb, h3[:, kc, :],
                         start=(kc == 0), stop=(kc == 7))
                pj = pre[j]
                if kind == "p":
                    k.copy("gpsimd", pj, pj.ap[:, 0:3], carry, carry3[:, ci_, :])
                    k.copy("scalar", pj, pj.ap[:, 3:3 + ntok], bank, bank.ap[:, 0:ntok])
                    k.copy("gpsimd", carry, carry3[:, ci_, :], pj, pj.ap[:, ntok:ntok + 3])
                    xs_ = [pj.ap[:, i_:i_ + ntok] for i_ in range(4)]
                    tv = tcv_[j].ap[:, 0:ntok]
                else:
                    p3 = pj.ap[:, 0:SPC * 11].rearrange("p (s t) -> p s t", s=SPC)
                    k.copy("gpsimd", pj, p3[:, :, 0:3], carry_s, cs4[:, ci_])
                    k.copy("scalar", pj, p3[:, :, 3:11], bank, bank.ap[:, 0:ntok].rearrange("p (s t) -> p s t", s=SPC))
                    k.copy("gpsimd", carry_s, cs4[:, ci_], pj, p3[:, :, 8:11])
                    xs_ = [p3[:, :, i_:i_ + 8] for i_ in range(4)]
                    tv = tcv_[j].ap[:, 0:ntok].rearrange("p (s t) -> p s t", s=SPC)
                k.ts("vector", tcv_[j], tv, pj, xs_[0], w[0], ALU.mult, sbufs=[cwbuf])
                for i_ in range(1, 4):
                    k.stt(tcv_[j], tv, pj, xs_[i_], w[i_], ALU.mult, tcv_[j], tv, ALU.add, sbufs=[cwbuf])
                a_ = av[j]
                k.act(a_, a_.ap[:, 0:ntok], tcv_[j], tcv_[j].ap[:, 0:ntok], AF.Silu)
                nt_ = max(1, ntok // 128)
                if role in ("q", "k"):
                    k.tt("gpsimd", sq[j], sq[j].ap[:, 0:ntok], a_, a_.ap[:, 0:ntok], a_, a_.ap[:, 0:ntok], ALU.mult)
                    b2 = nbank()
                    k.mm(b2, b2.ap[:, 0:ntok], onesf, onesf.ap, sq[j], sq[j].ap[:, 0:ntok])
                    k.act(sd[j], sd[j].ap[:, 0:ntok], b2, b2.ap[:, 0:ntok], AF.Sqrt, bias=epsb.ap[:, 0:1], sbufs=[epsb])
                    recip("vector", sd[j], sd[j].ap[:, 0:ntok], sd[j], sd[j].ap[:, 0:ntok])
                    scl = (128.0 ** -0.5) if role == "q" else 1.0
                    k.stt(xn[j], xn[j].ap[:, 0:ntok], a_, a_.ap[:, 0:ntok], scl, ALU.mult, sd[j], sd[j].ap[:, 0:ntok], ALU.mult)
                    dstT = (qTb, qTbs) if role == "q" else (kTb, kTbs)
                    if kind == "p":
                        k.dma(dstT[0], dstT[0].ap[hidx][:, b * TBk:(b + 1) * TBk], xn[j], xn[j].ap[:, 0:ntok])
                    else:
                        k.dma(dstT[1], dstT[1].ap[hidx], xn[j], xn[j].ap[:, 0:ntok])
                    src_tm = xn[j] if role == "k" else None
                    dst_tm = (ktm, ktms, hidx)
                else:
                    k.copy("vector", xn[j], xn[j].ap[:, 0:ntok], a_, a_.ap[:, 0:ntok])
                    src_tm = xn[j]
                    dst_tm = (vtm, vtms, hidx)
                if src_tm is not None:
                    b3 = nbank()
                    b3v = b3.ap.bitcast(BF16)
                    if kind == "p":
                        for t in range(nt_):
                            k.tr(b3, b3v[:, t * 128:(t + 1) * 128], src_tm, src_tm.ap[:, t * 128:(t + 1) * 128], identb, identb.ap)
                        k.copy("scalar", tmb[j], tmb[j].ap[:, 0:ntok], b3, b3v[:, 0:ntok])
                        k.dma(dst_tm[0], dst_tm[0].ap[dst_tm[2]][b * TBk:(b + 1) * TBk, :].rearrange("(t p) d -> p t d", p=128),
                              tmb[j], v3(tmb[j].ap, nt_))
                    else:
                        for s in range(SPC):
                            k.tr(b3, b3v[0:DEC_T, s * 128:(s + 1) * 128], src_tm, src_tm.ap[:, s * DEC_T:(s + 1) * DEC_T],
                                 identb, identb.ap)
                        k.copy("scalar", tmb[j], tmb[j].ap[0:DEC_T, 0:SPC * 128], b3, b3v[0:DEC_T, 0:SPC * 128])
                        k.dma(dst_tm[1], dst_tm[1].ap[dst_tm[2]].rearrange("s t d -> t s d"),
                              tmb[j], v3(tmb[j].ap, SPC)[0:DEC_T])
            if kind == "p":
                tl = [(128, lambda kc, t=t: h3[:, kc, t * 128:(t + 1) * 128]) for t in range(4)]
                wba_, dtb_, negA_ = Wbam, dtbm, negAm
            else:
                tl = [(DEC_T, lambda kc, s=s: h3[:, kc, s * DEC_T:(s + 1) * DEC_T]) for s in range(SPC)]
                wba_, dtb_, negA_ = Wba, dtb, negA
            w2 = 2 * nvh
            bank = nbank()
            rows = tl[0][0]
            r = slice(0, rows)
            for ti, (_r, lf) in enumerate(tl):
                for kc in range(8):
                    k.mm(bank, bank.ap[r, ti * w2:(ti + 1) * w2], hbb, lf(kc), wba_, v3(wba_.ap, 8)[:, kc, :],
                         start=(kc == 0), stop=(kc == 7))
            k.copy("scalar", braw, braw.ap[r, 0:4 * w2], bank, bank.ap[r, 0:4 * w2])
            b3_ = v3(braw.ap[:, 0:4 * w2], 4)
            bgb = bgt[bi % 2]
            bg3 = v3(bgb.ap[:, 0:4 * w2], 4)
            k.act(bgb, bg3[r, :, 0:nvh], braw, b3_[r, :, 0:nvh], AF.Sigmoid)
            bcn = lambda t_: t_.ap[r, :].rearrange("p (o e) -> p o e", o=1).to_broadcast([rows, 4, nvh])
            xa3, ab3, sp3 = (v3(t_.ap[:, 0:4 * nvh], 4)[r] for t_ in (xa, ab, sp))
            k.tt("vector", xa, xa3, braw, b3_[r, :, nvh:w2], dtb_, bcn(dtb_), ALU.add)
            k.act(ab, ab3, xa, xa3, AF.Abs)
            k.act(ab, ab3, ab, ab3, AF.Exp, scale=-1.0)
            k.act(ab, ab3, ab, ab3, AF.Ln, bias=one1.ap[r, 0:1], sbufs=[one1])
            k.ts("vector", sp, sp3, xa, xa3, 0.0, ALU.max)
            k.tt("vector", sp, sp3, sp, sp3, ab, ab3, ALU.add)
            k.tt("vector", bgb, bg3[r, :, nvh:w2], sp, sp3, negA_, bcn(negA_), ALU.mult)
            if kind == "p":
                k.dma(bg_scr, bg_scr.ap[b * 4:(b + 1) * 4].rearrange("t p c -> p t c"), bgb, bg3)
            else:
                k.dma(bgs_scr, bgs_scr.ap.rearrange("s p c -> p s c"), bgb, bg3[r])
        k.dma(bconv_p_out, bconv_p_out.ap, carry, carry.ap)
        k.dma(bconv_s_out, bconv_s_out.ap, carry_s, carry_s.ap)

    def b2_run(prompt):
        k.phase()
        nqm, nvm, DS = (2, 4, 4) if prompt else (NQ, NV, 2)
        Um = k.alloc("Um", 128)
        Umb = k.alloc("Umb", 128, BF16)
        Ls = k.alloc("Ls", 128)
        sel = k.alloc("sel", NV * 128)
        k.dma(Um, Um.ap, Umat_in, Umat_in.ap)
        k.copy("vector", Umb, Umb.ap, Um, Um.ap)
        k.dma(Ls, Ls.ap, Lstrict_in, Lstrict_in.ap)
        k.dma(sel, sel.ap[0:NV, :], sel_in, sel_in.ap)
        sel3 = v3(sel.ap, NV)
        S = k.alloc("S", nvm * 128)
        Sb = k.alloc("Sb", nvm * 128, BF16)
        S3, Sb3 = v3(S.ap, nvm), v3(Sb.ap, nvm)
        qT = [k.alloc(f"qTc{i}", nqm * 128, BF16) for i in range(DS)]
        kT = [k.alloc(f"kTc{i}", nqm * 128, BF16) for i in range(DS)]
        kt = [k.alloc(f"ktc{i}", nqm * 128, BF16) for i in range(DS)]
        vt = [k.alloc(f"vtc{i}", nvm * 128, BF16) for i in range(DS)]
        bg = [k.alloc(f"bgc{i}", 32) for i in range(DS)]
        cg_ = [k.alloc(f"cg{i}", 16) for i in range(DS)]
        cgT_ = [k.alloc(f"cgT{i}", 128) for i in range(DS)]
        nbeta_ = [k.alloc(f"nbeta{i}", 16) for i in range(DS)]
        bec_ = [k.alloc(f"bec{i}", 16) for i in range(DS)]
        Gs_ = [k.alloc(f"Gs{i}", nqm * 128) for i in range(DS)]
        QKs_ = [k.alloc(f"QKs{i}", nqm * 128) for i in range(DS)]
        G4 = range(4)
        dec_ = [k.alloc(f"dec{g}", 512) for g in G4]
        decT_ = [k.alloc(f"decT{g}", 512) for g in G4]
        eR_ = [k.alloc(f"eR{g}", 512) for g in G4]
        tG_ = [k.alloc(f"tG{g}", 512) for g in G4]
        X_ = [[k.alloc(f"X{g}_{i}", 512) for i in range(2)] for g in G4]
        Xt_ = [[k.alloc(f"Xt{g}_{i}", 512) for i in range(2)] for g in G4]
        Tt_ = [[k.alloc(f"Tt{g}_{i}", 512) for i in range(2)] for g in G4]
        TtB_ = [k.alloc(f"TtB{g}", 512, BF16) for g in G4]
        PT_ = [k.alloc(f"PT{g}", 512, BF16) for g in G4]
        bv_ = [k.alloc(f"bv{g}", 512, BF16) for g in G4]
        bk_ = [k.alloc(f"bk{g}", 512, BF16) for g in G4]
        kdec_ = [k.alloc(f"kdec{g}", 512, BF16) for g in G4]
        qgT_ = [k.alloc(f"qgT{g}", 512, BF16) for g in G4]
        u0s_ = [k.alloc(f"u0s{g}", 512) for g in G4]
        wkT_ = [k.alloc(f"wkT{g}", 512, BF16) for g in G4]
        ub_ = [k.alloc(f"ub{g}", 512, BF16) for g in G4]
        otm_ = [k.alloc(f"otm{g}", 512) for g in G4]
        v4 = lambda t_, C: v3(t_.ap, 4)[0:C]

        done = [0]

        def chunk(C, levels, ci, loads, o_dst, nq, nv, ws_fn, order=None):
            r = slice(0, C)
            j = ci % DS
            cg, cgT, nbeta, bec, Gs, QKs = cg_[j], cgT_[j], nbeta_[j], bec_[j], Gs_[j], QKs_[j]
            q3, k3, kt3 = (v3(t_.ap[:, 0:nq * 128], nq) for t_ in (qT[j], kT[j], kt[j]))
            vt3 = v3(vt[j].ap[:, 0:nv * 128], nv)
            loads(qT[j], q3, kT[j], k3, kt[j], kt3, vt[j], vt3, bg[j])
            beta, g_ = bg[j].ap[r, 0:nv], bg[j].ap[r, nv:2 * nv]
            b0 = nbank()
            k.mm(b0, b0.ap[r, 0:nv], Um, Um.ap[r, r], bg[j], g_)
            k.copy("vector", cg, cg.ap[r, 0:nv], b0, b0.ap[r, 0:nv])
            b1 = nbank()
            k.tr(b1, b1.ap[0:nv, 0:C], cg, cg.ap[r, 0:nv], identf, identf.ap[r, r])
            k.copy("vector", cgT, cgT.ap[0:nv, 0:C], b1, b1.ap[0:nv, 0:C])
            k.ts("vector", nbeta, nbeta.ap[r, 0:nv], bg[j], beta, -1.0, ALU.mult)
            k.act(bec, bec.ap[r, 0:nv], cg, cg.ap[r, 0:nv], AF.Exp)
            k.tt("vector", bec, bec.ap[r, 0:nv], bec, bec.ap[r, 0:nv], bg[j], beta, ALU.mult)
            Gs3, QK3 = v3(Gs.ap, nqm), v3(QKs.ap, nqm)
            for h0 in range(0, nq, 4):
                hn = min(4, nq - h0)
                bG, bQ = nbank(), nbank()
                for hh in range(hn):
                    hq = h0 + hh
                    k.mm(bG, v3(bG.ap, 4)[r, hh, 0:C], kT[j], k3[:, hq, 0:C], kT[j], k3[:, hq, 0:C])
                    k.mm(bQ, v3(bQ.ap, 4)[r, hh, 0:C], kT[j], k3[:, hq, 0:C], qT[j], q3[:, hq, 0:C])
                k.copy("scalar", Gs, Gs3[r, h0:h0 + hn, 0:C], bG, v3(bG.ap, 4)[r, 0:hn, 0:C])
                k.copy("scalar", QKs, QK3[r, h0:h0 + hn, 0:C], bQ, v3(bQ.ap, 4)[r, 0:hn, 0:C])
            def group(gq, ws, order):
                dec, decT, eR, tG = dec_[ws], decT_[ws], eR_[ws], tG_[ws]
                X, Xt, Tt, TtB = X_[ws], Xt_[ws], Tt_[ws], TtB_[ws]
                PT, bv, bk, kdec, qgT = PT_[ws], bv_[ws], bk_[ws], kdec_[ws], qgT_[ws]
                u0s, wkT, ub = u0s_[ws], wkT_[ws], ub_[ws]
                hvs = [4 * gq + i_ for i_ in range(4)]
                bR = nbank()
                R4 = v3(bR.ap, 4)
                for i_, hv in enumerate(hvs):
                    k.mm(bR, R4[:, i_, 0:C], sel, sel3[0:nv, hv, :], cgT, cgT.ap[0:nv, 0:C])
                dec4, decT4, eR4, tG4 = v4(dec, C), v4(decT, C), v3(eR.ap, 4), v4(tG, C)
                for i_, hv in enumerate(hvs):
                    cgc = cg.ap[r, hv:hv + 1]
                    k.ts("vector", dec, dec4[:, i_, 0:C], bR, R4[r, i_, 0:C], cgc, ALU.subtract, 0.0, ALU.max, sbufs=[cg])
                    k.ts("gpsimd" if False else "vector", decT, decT4[:, i_, 0:C], bR, R4[r, i_, 0:C], cgc, ALU.subtract, 0.0,
                         ALU.min, sbufs=[cg])
                k.act(dec, dec4[:, :, 0:C], dec, dec4[:, :, 0:C], AF.Exp, scale=-1.0)
                k.act(decT, decT4[:, :, 0:C], decT, decT4[:, :, 0:C], AF.Exp)
                k.act(eR, eR4[:, :, 0:C], bR, R4[:, :, 0:C], AF.Exp)
                yield
                X4 = [v4(X[0], C), v4(X[1], C)]
                Xt4 = [v4(Xt[0], C), v4(Xt[1], C)]
                Tt4 = [v4(Tt[0], C), v4(Tt[1], C)]
                for i_, hv in enumerate(hvs):
                    hq = hv // 2
                    k.tt("gpsimd", tG, tG4[:, i_, 0:C], Gs, Gs3[r, hq, 0:C], dec, dec4[:, i_, 0:C], ALU.mult)
                    k.stt(X[0], X4[0][:, i_, 0:C], tG, tG4[:, i_, 0:C], nbeta.ap[r, hv:hv + 1], ALU.mult,
                          Ls, Ls.ap[r, r], ALU.mult, sbufs=[nbeta])
                bT = nbank()
                bTv = v3(bT.ap, 4)
                for i_ in range(4):
                    k.tr(bT, bTv[r, i_, 0:C], X[0], X4[0][:, i_, 0:C], identf, identf.ap[r, r])
                k.copy("scalar", Xt[0], Xt4[0][:, :, 0:C], bT, bTv[r, :, 0:C])
                yield
                idb = identf.ap[r, r].rearrange("p (o e) -> p o e", o=1).to_broadcast([C, 4, C])
                k.tt("vector", Tt[0], Tt4[0][:, :, 0:C], Xt[0], Xt4[0][:, :, 0:C], identf, idb, ALU.add)
                cur = 0
                for lv in range(1, levels + 1):
                    nxt = 1 - cur
                    bX = nbank()
                    for i_ in range(4):
                        k.mm(bX, v3(bX.ap, 4)[r, i_, 0:C], Xt[cur], Xt4[cur][:, i_, 0:C], X[cur], X4[cur][:, i_, 0:C])
                    k.copy("scalar", X[nxt], X4[nxt][:, :, 0:C], bX, v3(bX.ap, 4)[r, :, 0:C])
                    yield
                    if lv < levels:
                        bXt = nbank()
                        for i_ in range(4):
                            k.mm(bXt, v3(bXt.ap, 4)[r, i_, 0:C], X[cur], X4[cur][:, i_, 0:C], Xt[cur], Xt4[cur][:, i_, 0:C])
                        k.copy("gpsimd" if False else "vector", Xt[nxt], Xt4[nxt][:, :, 0:C], bXt, v3(bXt.ap, 4)[r, :, 0:C])
                    bD = nbank()
                    for i_ in range(4):
                        k.mm(bD, v3(bD.ap, 4)[r, i_, 0:C], X[nxt], X4[nxt][:, i_, 0:C], Tt[cur], Tt4[cur][:, i_, 0:C])
                    k.tt("vector", Tt[nxt], Tt4[nxt][:, :, 0:C], bD, v3(bD.ap, 4)[r, :, 0:C], Tt[cur], Tt4[cur][:, :, 0:C], ALU.add)
                    cur = nxt
                    yield
                TtF, TtF4 = TtB, v4(TtB, C)
                k.copy("vector", TtB, TtF4[:, :, 0:C], Tt[cur], Tt4[cur][:, :, 0:C])
                PT4, bv4, bk4, kd4, qg4 = v4(PT, C), v4(bv, C), v4(bk, C), v4(kdec, C), v3(qgT.ap, 4)
                for i_, hv in enumerate(hvs):
                    hq = hv // 2
                    k.tt("gpsimd", tG, tG4[:, i_, 0:C], QKs, QK3[r, hq, 0:C], decT, decT4[:, i_, 0:C], ALU.mult)
                    k.ts("vector", bv, bv4[:, i_, :], vt[j], vt3[r, hv, :], bg[j].ap[r, hv:hv + 1], ALU.mult, sbufs=[bg[j]])
                    k.ts("vector", bk, bk4[:, i_, :], kt[j], kt3[r, hq, :], bec.ap[r, hv:hv + 1], ALU.mult, sbufs=[bec])
                    k.ts("gpsimd", kdec, kd4[:, i_, :], kt[j], kt3[r, hq, :], decT4[:, i_, C - 1:C], ALU.mult, sbufs=[decT])
                    k.tt("gpsimd", qgT, qg4[:, i_, 0:C], qT[j], q3[:, hq, 0:C], eR, eR4[:, i_, 0:C], ALU.mult)
                umb = Umb.ap[r, r].rearrange("p (o e) -> p o e", o=1).to_broadcast([C, 4, C])
                k.tt("vector", PT, PT4[:, :, 0:C], tG, tG4[:, :, 0:C], Umb, umb, ALU.mult)
                yield
                bU = nbank()
                bW = nbank()
                for i_ in range(4):
                    k.mm(bU, v3(bU.ap, 4)[r, i_, :], TtF, TtF4[:, i_, 0:C], bv, bv4[:, i_, :])
                    k.mm(bW, v3(bW.ap, 4)[:, i_, 0:C], bk, bk4[:, i_, :], TtF, TtF4[:, i_, 0:C])
                k.copy("scalar", u0s, v4(u0s, C), bU, v3(bU.ap, 4)[r])
                wk4 = v3(wkT.ap, 4)
                k.copy("scalar", wkT, wk4[:, :, 0:C], bW, v3(bW.ap, 4)[:, :, 0:C])
                yield
                if order is not None:
                    while done[0] < order:
                        yield
                bS = nbank()
                for i_, hv in enumerate(hvs):
                    k.mm(bS, v3(bS.ap, 4)[r, i_, :], wkT, wk4[:, i_, 0:C], Sb, Sb3[:, hv, :])
                ub4 = v4(ub, C)
                k.tt("vector", ub, ub4, u0s, v4(u0s, C), bS, v3(bS.ap, 4)[r], ALU.subtract)
                yield
                bO = nbank()
                for i_, hv in enumerate(hvs):
                    k.mm(bO, v3(bO.ap, 4)[r, i_, :], qgT, qg4[:, i_, 0:C], Sb, Sb3[:, hv, :], start=True, stop=False)
                    k.mm(bO, v3(bO.ap, 4)[r, i_, :], PT, PT4[:, i_, 0:C], ub, ub4[:, i_, :], start=False, stop=True)
                oj = otm_[ws]
                k.copy("scalar", oj, v4(oj, C), bO, v3(bO.ap, 4)[r])
                ob, oap = o_dst(gq)
                k.dma(ob, oap, oj, v4(oj, C))
                yield
                bN = nbank()
                for i_, hv in enumerate(hvs):
                    k.mm(bN, v3(bN.ap, 4)[:, i_, :], kdec, kd4[:, i_, :], ub, ub4[:, i_, :])
                for i_, hv in enumerate(hvs):
                    k.stt(S, S3[:, hv, :], S, S3[:, hv, :], eR4[:, i_, C - 1:C], ALU.mult, bN, v3(bN.ap, 4)[:, i_, :], ALU.add,
                          sbufs=[eR])
                k.copy("scalar", Sb, Sb3[:, 4 * gq:4 * gq + 4, :], S, S3[:, 4 * gq:4 * gq + 4, :])
                if order is not None:
                    done[0] += 1

            return [group(gq, ws_fn(gq), order) for gq in range(nv // 4)]

        def step_all(active):
            for g_ in list(active):
                try:
                    next(g_)
                except StopIteration:
                    active.remove(g_)

        CH = 64
        if prompt:
            k.memset("vector", S, S.ap, 0.0)
            k.memset("vector", Sb, Sb.ap, 0.0)
            active = []
            for n in range(SEQ // CH):
                tok = slice(n * CH, (n + 1) * CH)

                def loads(qb, q3, kb, k3, ktb, kt3, vtb, vt3, bgb, n=n, tok=tok):
                    k.dma(qb, q3[:, :, 0:CH], qTb, qTb.ap[:, :, tok].rearrange("h p t -> p h t"))
                    k.dma(kb, k3[:, :, 0:CH], kTb, kTb.ap[:, :, tok].rearrange("h p t -> p h t"))
                    k.dma(ktb, kt3[0:CH], ktm, ktm.ap[:, tok, :].rearrange("h t d -> t h d"))
                    k.dma(vtb, vt3[0:CH], vtm, vtm.ap[:, tok, :].rearrange("h t d -> t h d"))
                    r0 = (n % 2) * CH
                    k.dma(bgb, bgb.ap[0:CH, 0:8], bg_scr, bg_scr.ap[n // 2][r0:r0 + CH, :])

                pc_, t0_ = (n * CH) // 512, (n * CH) % 512
                while len(active) >= 4:
                    step_all(active)
                active += chunk(CH, 5, n, loads,
                                lambda gq, pc_=pc_, t0_=t0_: (otm_m[pc_], otm_m[pc_].ap[t0_:t0_ + CH, :].rearrange("t (h d) -> t h d", h=4)),
                                2, 4, lambda gq, n=n: n % 4, order=n)
                step_all(active)
                if t0_ + CH == 512:
                    while active:
                        step_all(active)
                    k.allgather(otm_g[pc_], otm_g[pc_].ap, otm_m[pc_], otm_m[pc_].ap, CC_GROUPS)
            k.dma(ssm_p_out, ssm_p_out.ap.rearrange("h k v -> k h v"), S, S3[:, 0:4, :])
        else:
            for s in range(SPC):
                k.dma(S, S3, state_b_ssm, state_b_ssm.ap[s].rearrange("h k v -> k h v"))
                k.copy("scalar", Sb, Sb.ap, S, S.ap)

                def loads(qb, q3, kb, k3, ktb, kt3, vtb, vt3, bgb, s=s):
                    c8 = slice(s * DEC_T, (s + 1) * DEC_T)
                    k.dma(qb, q3[:, :, 0:DEC_T], qTbs, qTbs.ap[:, :, c8].rearrange("h p t -> p h t"))
                    k.dma(kb, k3[:, :, 0:DEC_T], kTbs, kTbs.ap[:, :, c8].rearrange("h p t -> p h t"))
                    k.dma(ktb, kt3[0:DEC_T], ktms, ktms.ap[:, s].rearrange("h t d -> t h d"))
                    k.dma(vtb, vt3[0:DEC_T], vtms, vtms.ap[:, s].rearrange("h t d -> t h d"))
                    k.dma(bgb, bgb.ap[0:DEC_T, :], bgs_scr, bgs_scr.ap[s])

                gens = chunk(DEC_T, 2, s, loads,
                             lambda gq, s=s: (otms_scr, otms_scr.ap[s][:, gq * 512:(gq + 1) * 512].rearrange("t (h d) -> t h d", h=4)),
                             NQ, NV, lambda gq: gq)
                while gens:
                    step_all(gens)
                k.dma(ssm_s_out, ssm_s_out.ap[s].rearrange("h k v -> k h v"), S, S3)

    def phase_b3(xin_p, xin_s, xout_p, xout_s):
        k.phase()
        Wz = k.alloc("Wz", 8 * 2048, BF16)
        Wz3 = v3(Wz.ap, 8)
        bw = b_w_in.ap.rearrange("(k p) n -> p k n", p=128)
        for c0 in range(0, 2048, 1024):
            k.dma(Wz, Wz3[:, :, c0:c0 + 1024], b_w_in, bw[:, :, B_CONV_DIM + c0:B_CONV_DIM + c0 + 1024], eng="gpsimd")
        Wo = k.alloc("Wo", NV * D, BF16)
        Wo3 = v3(Wo.ap, NV)
        k.dma(Wo, Wo3, b_w_out, b_w_out.ap.rearrange("(c p) n -> p c n", p=128), eng="gpsimd")
        gn = k.alloc("gn", 128)
        k.dma(gn, gn.ap, b_norm_g, b_norm_g.ap.partition_broadcast(128))
        G = k.alloc("G", D)
        k.dma(G, G.ap, mod_scr, mod_scr.ap[1][0:1, 2 * D:3 * D].partition_broadcast(128))
        Gs = [k.alloc(f"Gs{i}", D) for i in range(2)]
        ht = [k.alloc(f"ht{i}", 8 * 128, BF16) for i in range(2)]
        ot = [k.alloc(f"ot{i}", 2048) for i in range(2)]
        sqo = k.alloc("sqo", 2048)
        ssq = k.alloc("ssq", 16)
        sdv = k.alloc("sdv", 16)
        sz = k.alloc("sz", 2048)
        of = [k.alloc(f"of{i}", 2048, BF16) for i in range(2)]
        oT = [k.alloc(f"oT{i}", NV * 128, BF16) for i in range(2)]
        xt = [k.alloc(f"xt{i}", D) for i in range(2)]
        yt = [k.alloc(f"yt{i}", D) for i in range(2)]
        tiles = [(128, i, None) for i in range(NT)] + [(DEC_T, None, s) for s in range(SPC)]
        for it, (rows, ti, si) in enumerate(tiles):
            j = it % 2
            r = slice(0, rows)
            h3 = v3(ht[j].ap, 8)
            if si is None:
                tok = slice(ti * 128, (ti + 1) * 128)
                k.dma(ht[j], h3, hTp, hTp.ap[:, :, tok])
                og_ = otm_g[ti // 4]
                r0_ = (ti % 4) * 128
                k.dma(ot[j], v3(ot[j].ap, 4), og_, og_.ap.rearrange("(r t) c -> t r c", r=4)[r0_:r0_ + 128])
                k.dma(xt[j], xt[j].ap, xin_p, xin_p.ap[tok, :])
                Gb = G
            else:
                k.dma(ht[j], h3[:, :, 0:rows], hTs, hTs.ap[:, :, si * DEC_T:(si + 1) * DEC_T])
                k.dma(ot[j], ot[j].ap[r, :], otms_scr, otms_scr.ap[si])
                k.dma(xt[j], xt[j].ap[r, :], xin_s, xin_s.ap[si])
                Gb = Gs[si % 2]
                k.dma(Gb, Gb.ap[r, :], mod_scr, mod_scr.ap[1][1 + si:2 + si, 2 * D:3 * D].partition_broadcast(rows))
            for q in range(4):
                bank = nbank()
                for kc in range(8):
                    k.mm(bank, bank.ap[r, :], ht[j], h3[:, kc, 0:rows], Wz, Wz3[:, kc, q * 512:(q + 1) * 512],
                         start=(kc == 0), stop=(kc == 7))
                k.act(sz, sz.ap[r, q * 512:(q + 1) * 512], bank, bank.ap[r, :], AF.Silu)
            o3 = v3(ot[j].ap, NV)[r]
            k.tt("gpsimd", sqo, sqo.ap[r, :], ot[j], ot[j].ap[r, :], ot[j], ot[j].ap[r, :], ALU.mult)
            k.op("vector", (lambda o_, i_: (lambda e: e.tensor_reduce(out=o_, in_=i_, op=ALU.add, axis=AX.X)))(
                ssq.ap[r, :], v3(sqo.ap, NV)[r]), reads=[sqo], writes=[ssq])
            k.act(sdv, sdv.ap[r, :], ssq, ssq.ap[r, :], AF.Sqrt, scale=1.0 / 128, bias=epsb.ap[r, :], sbufs=[epsb])
            recip("vector", sdv, sdv.ap[r, :], sdv, sdv.ap[r, :])
            rb = sdv.ap[r, :].rearrange("p (h o) -> p h o", o=1).to_broadcast([rows, NV, 128])
            gb_ = gn.ap[r, :].rearrange("p (o e) -> p o e", o=1).to_broadcast([rows, NV, 128])
            k.tt("vector", sqo, v3(sqo.ap, NV)[r], ot[j], o3, sdv, rb, ALU.mult)
            k.tt("gpsimd", sqo, v3(sqo.ap, NV)[r], sqo, v3(sqo.ap, NV)[r], gn, gb_, ALU.mult)
            k.tt("vector", of[j], of[j].ap[r, :], sqo, sqo.ap[r, :], sz, sz.ap[r, :], ALU.mult)
            oT3 = v3(oT[j].ap, NV)
            for half in range(2):
                bank = nbank()
                bv_ = v3(bank.ap.bitcast(BF16), 8)
                for c in range(8):
                    cc = half * 8 + c
                    k.tr(bank, bv_[:, c, 0:rows], of[j], of[j].ap[r, cc * 128:(cc + 1) * 128], identb, identb.ap[r, r])
                k.copy("scalar", oT[j], oT3[:, half * 8:half * 8 + 8, 0:rows], bank, bv_[:, :, 0:rows])
            for half in range(2):
                bank = nbank()
                for c in range(NV):
                    k.mm(bank, bank.ap[r, :], oT[j], oT3[:, c, 0:rows], Wo, Wo3[:, c, half * 512:(half + 1) * 512],
                         start=(c == 0), stop=(c == NV - 1))
                hs = slice(half * 512, (half + 1) * 512)
                k.tt("vector", yt[j], yt[j].ap[r, hs], bank, bank.ap[r, :], Gb, Gb.ap[r, hs], ALU.mult)
                k.tt("gpsimd", yt[j], yt[j].ap[r, hs], yt[j], yt[j].ap[r, hs], xt[j], xt[j].ap[r, hs], ALU.add)
            if si is None:
                k.dma(xout_p, xout_p.ap[tok, :], yt[j], yt[j].ap)
            else:
                k.dma(xout_s, xout_s.ap[si], yt[j], yt[j].ap[r, :])

    def phase_final(xin_p, xin_s):
        k.phase()
        gf = k.alloc("gf", D)
        k.dma(gf, gf.ap, norm_final_g, norm_final_g.ap.partition_broadcast(128))
        xt = [k.alloc(f"xt{i}", D) for i in range(4)]
        sq = k.alloc("sq", D)
        ssq = [k.alloc(f"ssq{i}", 1) for i in range(4)]
        yt = [k.alloc(f"yt{i}", D) for i in range(4)]
        tiles = [(128, i, None) for i in range(NT)] + [(DEC_T, None, s) for s in range(SPC)]
        for it, (rows, ti, si) in enumerate(tiles):
            j = it % 4
            r = slice(0, rows)
            if si is None:
                k.dma(xt[j], xt[j].ap, xin_p, xin_p.ap[ti * 128:(ti + 1) * 128, :])
            else:
                k.dma(xt[j], xt[j].ap[r, :], xin_s, xin_s.ap[si])
            k.tt("gpsimd", sq, sq.ap[r, :], xt[j], xt[j].ap[r, :], xt[j], xt[j].ap[r, :], ALU.mult)
            k.op("vector", (lambda o_, i_: (lambda e: e.tensor_reduce(out=o_, in_=i_, op=ALU.add, axis=AX.X)))(
                ssq[j].ap[r, :], sq.ap[r, :]), reads=[sq], writes=[ssq[j]])
            k.act(ssq[j], ssq[j].ap[r, :], ssq[j], ssq[j].ap[r, :], AF.Sqrt, scale=1.0 / D, bias=epsb.ap[r, :], sbufs=[epsb])
            recip("vector", ssq[j], ssq[j].ap[r, :], ssq[j], ssq[j].ap[r, :])
            k.stt(yt[j], yt[j].ap[r, :], xt[j], xt[j].ap[r, :], ssq[j].ap[r, 0:1], ALU.mult, gf, gf.ap[r, :], ALU.mult,
                  sbufs=[ssq[j]])
            if si is None:
                k.dma(y_p_out, y_p_out.ap[ti * 128:(ti + 1) * 128, :], yt[j], yt[j].ap)
            else:
                k.dma(y_s_out, y_s_out.ap[si], yt[j], yt[j].ap[r, :])

    import os
    nph = int(os.environ.get("KSTAGE", "99"))
    for i_, ph_ in enumerate([phase_attn, phase_attn_sample, phase_combine, phase_out_a,
                                lambda: phase_norm(0, x1p, x1s, 4 * D, 3 * D),
                                lambda: phase_ffn(0, x1p, x1s, x2p, x2s),
                                lambda: phase_norm(1, x2p, x2s, 1 * D, 0),
                                phase_b1, lambda: b2_run(True), lambda: b2_run(False),
                                lambda: phase_b3(x2p, x2s, x3p, x3s),
                                lambda: phase_norm(1, x3p, x3s, 4 * D, 3 * D),
                                lambda: phase_ffn(1, x3p, x3s, x4p, x4s),
                                lambda: phase_final(x4p, x4s)]):
        if i_ < nph:
            ph_()

    with stack:
        k.emit()
    return nc


_NC = {}


def _rope_tables():
    half = DH // 2
    inv = THETA ** (-np.arange(half, dtype=np.float32) / half)
    tabs = np.zeros((3, NT, 128, 64), np.float32)
    for g, (_w, d) in enumerate(A_GROUPS):
        nb = group_blocks(d)
        for sb in range(4):
            for r_ in range(d):
                for b_ in range(nb):
                    n = sb * 16 + r_ * nb + b_
                    pos = (sb * 2048 + d * (b_ * 128 + np.arange(128)) + r_).astype(np.float32)
                    ang = pos[:, None] * inv[None, :]
                    tabs[g, n, :, :32] = np.cos(ang)
                    tabs[g, n, :, 32:] = np.sin(ang)
    pos = (PAST + np.arange(DEC_T)).astype(np.float32)
    ang = pos[:, None] * inv[None, :]
    ts = np.concatenate([np.cos(ang), np.sin(ang)], axis=1).astype(np.float32)
    return tabs, ts


def _masks():
    u = np.arange(128)[:, None]
    v = np.arange(128)[None, :]
    own = (u <= v).astype(np.float32)
    prev = (u >= v).astype(np.float32)
    mask4 = np.concatenate([own, prev, own, prev], axis=1)
    E = np.zeros((65, 64), np.float32)
    E[64, :] = 1.0
    ms = np.zeros((3, 8, 128, 8), np.float32)
    mn = np.zeros((3, 8, 8), np.float32)
    for g, (_w, d) in enumerate(A_GROUPS):
        for t0 in range(min(d, DEC_T)):
            for t in range(t0, DEC_T, d):
                a = (t - t0) // d
                ms[g, t0, a:, t] = 1.0
        for tk in range(DEC_T):
            for t in range(tk, DEC_T):
                if (t - tk) % d == 0:
                    mn[g, tk, t] = 1.0
    return dict(mask4=mask4, Emat=E, msk_s=ms, msk_n=mn)


def kernel(**inp):
    f32 = np.float32
    stage = 1
    if stage not in _NC:
        _NC[stage] = build_program(stage)
    nc = _NC[stage]
    g = lambda n: np.asarray(inp[n], dtype=f32)
    rope_p, rope_s = _rope_tables()
    shared = dict(
        w_mod=g("w_mod"), b_mod=g("b_mod"), norm_mix_g=g("norm_mix_g"), norm_ffn_g=g("norm_ffn_g"),
        a_w_in=g("a_w_in")[0], identf=np.eye(128, dtype=f32), rope_p=rope_p, rope_s=rope_s,
        a_w_out=g("a_w_out")[0], **_masks(),
        ffn_w_up=g("ffn_w_up"), ffn_w_down=g("ffn_w_down"),
    )
    NC_ = D_FF // 128
    cwb = np.concatenate([g("ffn_conv_w"), g("ffn_conv_b")[:, None, :]], axis=1)
    shared["ffn_cw"] = np.ascontiguousarray(cwb.reshape(2, 4, NC_, 128).transpose(0, 3, 2, 1))
    sfc = g("state_ffn_conv")
    u_ = np.arange(128)
    shared.update(
        b_w_in=g("b_w_in")[0], b_w_out=g("b_w_out")[0],
        b_cw=np.ascontiguousarray(g("b_conv_w")[0].reshape(4, 32, 128).transpose(2, 1, 0)),
        b_dt_bias=g("b_dt_bias").reshape(1, 16), b_a_log=g("b_a_log").reshape(1, 16),
        b_norm_g=g("b_norm_g").reshape(1, 128), norm_final_g=g("norm_final_g").reshape(1, D),
        Umat=(u_[:, None] <= u_[None, :]).astype(f32), Lstrict=(u_[None, :] < u_[:, None]).astype(f32),
        selm=np.ascontiguousarray(np.repeat(np.eye(16, dtype=f32)[:, :, None], 128, axis=2).reshape(16, 2048)),
    )
    sbc = g("state_b_conv")[0]
    ssm0 = g("state_b_ssm")[0]
    xprompt, xsample, cp, cs_ = g("x_prompt"), g("x_sample"), g("c_prompt"), g("c_sample")
    in_maps = []
    for c in range(NCORES):
        m = dict(shared)
        sl = slice(c * SPC, (c + 1) * SPC)
        seq_, rk = c // 4, c % 4
        m["xp"] = xprompt[seq_]
        m["xs"] = np.ascontiguousarray(xsample[sl])
        cpc = cp[seq_]
        awi = shared["a_w_in"].reshape(D, 3, 3, NH, DH)[:, :, :, rk * HM:(rk + 1) * HM, :]
        m["a_w_in_m"] = np.ascontiguousarray(awi.reshape(D, 9 * HM * DH))
        bwi = shared["b_w_in"]
        qh_, vh_ = slice(2 * rk * 128, (2 * rk + 2) * 128), slice(4 * rk * 128, (4 * rk + 4) * 128)
        m["b_w_in_m"] = np.ascontiguousarray(np.concatenate(
            [bwi[:, 0:1024][:, qh_], bwi[:, 1024:2048][:, qh_], bwi[:, 2048:4096][:, vh_],
             bwi[:, 6144 + 4 * rk:6144 + 4 * rk + 4], bwi[:, 6160 + 4 * rk:6160 + 4 * rk + 4]], axis=1))
        mych = [2 * rk, 2 * rk + 1, 8 + 2 * rk, 8 + 2 * rk + 1] + [16 + 4 * rk + i_ for i_ in range(4)]
        m["b_cw_m"] = np.ascontiguousarray(shared["b_cw"][:, mych, :])
        m["b_dt_bias_m"] = np.ascontiguousarray(shared["b_dt_bias"][:, 4 * rk:4 * rk + 4])
        m["b_a_log_m"] = np.ascontiguousarray(shared["b_a_log"][:, 4 * rk:4 * rk + 4])
        m["cT"] = np.ascontiguousarray(np.concatenate([cpc[None, :], cs_[sl]], axis=0).T)
        m["fconvT"] = np.ascontiguousarray(sfc[:, sl].reshape(2, SPC, 2, NC_, 128).transpose(0, 4, 3, 1, 2))
        m["bconvT"] = np.ascontiguousarray(sbc[sl].reshape(SPC, 3, 32, 128).transpose(3, 2, 0, 1))
        m["state_b_ssm"] = np.ascontiguousarray(ssm0[sl])
        for (w, _d) in A_GROUPS:
            for kv in ("k", "v"):
                a = g(f"cache_a_{kv}_w{w}")[0, sl]
                m[f"cache_{kv}_w{w}"] = np.ascontiguousarray(a.reshape(SPC, w, D))
        in_maps.append(m)
    res = run_bass_kernel_spmd(nc, in_maps, core_ids=list(range(NCORES))).results
    if DEBUG:
        global _DBG
        _DBG = res

    def cat_s(name, tail):
        return np.concatenate([res[c][name] for c in range(NCORES)], axis=0).reshape((1, DEC_B) + tail)

    PC = (0, 4)

    def cat_p(name, tail):
        per_seq = [np.concatenate([res[4 * b + r_][name].reshape(tail[0], HM, DH) for r_ in range(4)], axis=1) for b in range(2)]
        return np.stack(per_seq, axis=0).reshape((1, 2) + tail)

    o = {}
    for (w, _d) in A_GROUPS:
        for kv in ("k", "v"):
            o[f"{kv}{w}s"] = cat_s(f"new_{kv}_w{w}_s", (w, NH, DH))
            o[f"{kv}{w}p"] = cat_p(f"new_{kv}_w{w}_p", (w, NH, DH))
    z = lambda *s: np.zeros(s, f32)
    NC_ = D_FF // 128
    fcp = np.stack([res[c]["fconv_p"].reshape(2, 128, NC_, 2) for c in PC], axis=1)
    fcp = np.ascontiguousarray(fcp.transpose(0, 1, 4, 3, 2)).reshape(2, 2, 2, D_FF)
    fcs = np.stack([res[c]["fconv_s"].reshape(2, 128, NC_, SPC, 2) for c in range(NCORES)], axis=1)
    fcs = np.ascontiguousarray(fcs.transpose(0, 1, 4, 5, 3, 2)).reshape(2, DEC_B, 2, D_FF)
    y_p = np.stack([res[c]["y_p"] for c in PC], axis=0)
    y_s = np.concatenate([res[c]["y_s"] for c in range(NCORES)], axis=0)
    ssm_p = np.stack([np.concatenate([res[4 * b + r_]["ssm_p"] for r_ in range(4)], axis=0) for b in range(2)], axis=0)[None]
    ssm_s = np.concatenate([res[c]["ssm_s"] for c in range(NCORES)], axis=0)[None]
    bcp = np.zeros((2, 128, 32, 3), f32)
    for b in range(2):
        for r_ in range(4):
            mych = [2 * r_, 2 * r_ + 1, 8 + 2 * r_, 8 + 2 * r_ + 1] + [16 + 4 * r_ + i_ for i_ in range(4)]
            bcp[b][:, mych, :] = res[4 * b + r_]["bconv_p"].reshape(128, 8, 3)
    bcp = np.ascontiguousarray(bcp.transpose(0, 3, 2, 1)).reshape(1, 2, 3, B_CONV_DIM)
    bcs = np.stack([res[c]["bconv_s"].reshape(128, 32, SPC, 3) for c in range(NCORES)], axis=0)
    bcs = np.ascontiguousarray(bcs.transpose(0, 3, 4, 2, 1)).reshape(1, DEC_B, 3, B_CONV_DIM)
    return (
        y_p, y_s,
        o["k128p"], o["k128s"], o["v128p"], o["v128s"],
        o["k512p"], o["k512s"], o["v512p"], o["v512s"],
        o["k2048p"], o["k2048s"], o["v2048p"], o["v2048s"],
        ssm_p, ssm_s,
        bcp, bcs,
        fcp, fcs,
    )
```
